# Optimizing a Trainium2 kernel written in Bass

```python
import math
import numpy as np
import jax
import jax.numpy as jnp
from jax import lax

D_MODEL = 1024
BATCH = 2
SEQ = 16384
DEPTH = 1
DEC_BATCH = 32
DEC_SEQ = 32
PAST_LEN = 2048

CHUNK = 64
N_MEM = 256
EPS = 1e-6
SSD_WIDTH = D_MODEL // 2
SSD_HEAD_DIM = 64
SSD_HEADS = SSD_WIDTH // SSD_HEAD_DIM
SSD_GROUPS = 2
SSD_STATE = 128
CONV_WIDTH = 4
CONV_DIM = SSD_WIDTH + 2 * SSD_GROUPS * SSD_STATE
DIFF_WIDTH = D_MODEL - SSD_WIDTH
DIFF_HEADS = 4
DIFF_V_DIM = DIFF_WIDTH // DIFF_HEADS
DIFF_QK_DIM = DIFF_V_DIM // 2
ROT_DIM = DIFF_QK_DIM // 4
ROPE_THETA = 500000.0
Q_BLOCK = 128
MEM_HEADS = 4
MEM_HEAD_DIM = D_MODEL // MEM_HEADS
N_KEYS = 128
N_EXPERTS = N_KEYS * N_KEYS
PEER_HEADS = 8
PEER_QUERY_DIM = 256
PEER_HALF = PEER_QUERY_DIM // 2
PEER_TOPK = 16
PEER_TOKEN_BLOCK = 128
IN_SIZES = (SSD_WIDTH, CONV_DIM, SSD_HEADS, DIFF_HEADS * 2 * DIFF_QK_DIM, DIFF_HEADS * 2 * DIFF_QK_DIM, DIFF_HEADS * DIFF_V_DIM)
IN_DIM = sum(IN_SIZES)
IN_SPLITS = tuple(int(s) for s in np.cumsum(IN_SIZES)[:-1])

kernel_name = 'hybrid_ssd_diffattn_peer_streaming_step'


def rmsnorm(x, g):
    xf = x.astype(jnp.float32)
    r = xf * lax.rsqrt(jnp.mean(xf * xf, axis=-1, keepdims=True) + EPS)
    return (r * g.astype(jnp.float32)).astype(x.dtype)


def rope_partial(t, pos):
    half = ROT_DIM // 2
    inv = 1.0 / (ROPE_THETA ** (jnp.arange(half, dtype=jnp.float32) / half))
    ang = pos[:, None] * inv[None, :]
    cos = jnp.cos(ang)[None, :, None, None, :].astype(t.dtype)
    sin = jnp.sin(ang)[None, :, None, None, :].astype(t.dtype)
    t1 = t[..., :half]
    t2 = t[..., half:ROT_DIM]
    return jnp.concatenate([t1 * cos - t2 * sin, t2 * cos + t1 * sin, t[..., ROT_DIM:]], axis=-1)


def causal_conv(xbc, prev, w, b):
    L = xbc.shape[1]
    xpad = jnp.concatenate([prev.astype(xbc.dtype), xbc], axis=1)
    out = b + sum(xpad[:, j:j + L] * w[j] for j in range(CONV_WIDTH))
    return jax.nn.silu(out), xpad[:, -(CONV_WIDTH - 1):]


def ssd_scan(x, dt, a, b_in, c_in, d_skip, h0):
    f32 = jnp.float32
    bsz, L = x.shape[:2]
    q = min(CHUNK, L)
    nc = L // q
    rep = SSD_HEADS // SSD_GROUPS
    xf = x.astype(f32)
    bh = jnp.repeat(b_in.astype(f32), rep, axis=2)
    ch = jnp.repeat(c_in.astype(f32), rep, axis=2)
    xdt = xf * dt[..., None]
    da = dt * a
    blk = lambda t: t.reshape((bsz, nc, q) + t.shape[2:])
    xdt, bh, ch, da = blk(xdt), blk(bh), blk(ch), blk(da)
    a_cum = jnp.cumsum(da, axis=2)
    seg = a_cum[:, :, :, None, :] - a_cum[:, :, None, :, :]
    causal = jnp.tril(jnp.ones((q, q), dtype=bool))[None, None, :, :, None]
    decay = jnp.exp(jnp.where(causal, seg, -jnp.inf))
    cb = jnp.einsum('bcihn,bcjhn->bcijh', ch, bh)
    y_diag = jnp.einsum('bcijh,bcjhp->bcihp', cb * decay, xdt)
    to_end = jnp.exp(a_cum[:, :, -1:, :] - a_cum)
    states = jnp.einsum('bcjhn,bcjh,bcjhp->bchpn', bh, to_end, xdt)
    chunk_decay = jnp.exp(a_cum[:, :, -1, :])

    def step(h, inp):
        s, dec = inp
        return h * dec[:, :, None, None] + s, h

    h_last, h_in = lax.scan(step, h0.astype(f32), (jnp.moveaxis(states, 1, 0), jnp.moveaxis(chunk_decay, 1, 0)))
    h_in = jnp.moveaxis(h_in, 0, 1)
    y_off = jnp.einsum('bcihn,bchpn,bcih->bcihp', ch, h_in, jnp.exp(a_cum))
    y = (y_diag + y_off).reshape(bsz, L, SSD_HEADS, SSD_HEAD_DIM) + d_skip.astype(f32)[:, None] * xf
    return y.astype(x.dtype), h_last


def diff_weights(s, lam):
    p = jax.nn.softmax(s, axis=-1)
    return p[:, :, 0] - lam * p[:, :, 1]


def diff_attn_blockwise(q, k, v, lam):
    bsz, S = q.shape[:2]
    nblk = S // Q_BLOCK
    scale = 1.0 / math.sqrt(DIFF_QK_DIM)
    qb = jnp.swapaxes(q.reshape((bsz, nblk, Q_BLOCK) + q.shape[2:]), 0, 1)
    key_chunk = jnp.arange(S) // CHUNK

    def one(args):
        qi, bi = args
        q_chunk = (bi * Q_BLOCK + jnp.arange(Q_BLOCK)) // CHUNK
        mask = key_chunk[None, :] <= q_chunk[:, None]
        s = jnp.einsum('bqhcd,bkhcd->bhcqk', qi, k).astype(jnp.float32) * scale
        s = jnp.where(mask, s, -jnp.inf)
        a = diff_weights(s, lam)
        return jnp.einsum('bhqk,bkhd->bqhd', a.astype(v.dtype), v)

    out = lax.map(one, (qb, jnp.arange(nblk)))
    return jnp.swapaxes(out, 0, 1).reshape(bsz, S, DIFF_HEADS, DIFF_V_DIM)


def diff_attn_dense(q, k, v, lam):
    s = jnp.einsum('bqhcd,bkhcd->bhcqk', q, k).astype(jnp.float32) / math.sqrt(DIFF_QK_DIM)
    a = diff_weights(s, lam)
    return jnp.einsum('bhqk,bkhd->bqhd', a.astype(v.dtype), v)


def memory_kv(mem, g, w_k, w_v):
    m = rmsnorm(mem, g)
    shp = mem.shape[:2] + (MEM_HEADS, MEM_HEAD_DIM)
    return (m @ w_k).reshape(shp), (m @ w_v).reshape(shp)


def peer_ffn(h, w_pq, sub_keys, u_tab, v_tab):
    bsz, L, d = h.shape
    n = bsz * L
    nblk = -(-n // PEER_TOKEN_BLOCK)
    t = jnp.pad(h.reshape(n, d), ((0, nblk * PEER_TOKEN_BLOCK - n), (0, 0)))

    def one(tb):
        qy = (tb @ w_pq).reshape(tb.shape[0], PEER_HEADS, 2, PEER_HALF)
        s = jnp.einsum('thcd,hcnd->thcn', qy, sub_keys).astype(jnp.float32)
        s1, i1 = lax.top_k(s[:, :, 0], PEER_TOPK)
        s2, i2 = lax.top_k(s[:, :, 1], PEER_TOPK)
        comb = (s1[..., :, None] + s2[..., None, :]).reshape(tb.shape[0], PEER_HEADS, PEER_TOPK * PEER_TOPK)
        sv, si = lax.top_k(comb, PEER_TOPK)
        e = jnp.take_along_axis(i1, si // PEER_TOPK, axis=-1) * N_KEYS + jnp.take_along_axis(i2, si % PEER_TOPK, axis=-1)
        g = jax.nn.softmax(sv, axis=-1)
        act = jax.nn.gelu(jnp.einsum('thkd,td->thk', u_tab[e], tb).astype(jnp.float32)) * g
        return jnp.einsum('thk,thkd->td', act.astype(tb.dtype), v_tab[e])

    out = lax.map(one, t.reshape(nblk, PEER_TOKEN_BLOCK, d))
    return out.reshape(-1, d)[:n].reshape(bsz, L, d)


def trunk_layer(x, pos_offset, k_past, v_past, h0, conv_prev, mem_k, mem_v, p, lambda_init):
    f32 = jnp.float32
    bsz, L, _ = x.shape
    h = rmsnorm(x, p['g_mix'])
    z, xbc, dt_raw, q, k, v = jnp.split(h @ p['w_in'], IN_SPLITS, axis=-1)
    xbc, conv_new = causal_conv(xbc, conv_prev, p['conv_w'], p['conv_b'])
    xs, b_in, c_in = jnp.split(xbc, (SSD_WIDTH, SSD_WIDTH + SSD_GROUPS * SSD_STATE), axis=-1)
    dt = jax.nn.softplus(dt_raw.astype(f32) + p['dt_bias'].astype(f32))
    a = -jnp.exp(p['a_log'].astype(f32))
    y_ssd, h_last = ssd_scan(xs.reshape(bsz, L, SSD_HEADS, SSD_HEAD_DIM), dt, a,
                             b_in.reshape(bsz, L, SSD_GROUPS, SSD_STATE),
                             c_in.reshape(bsz, L, SSD_GROUPS, SSD_STATE), p['d_skip'], h0)
    y_ssd = rmsnorm(y_ssd.reshape(bsz, L, SSD_WIDTH) * jax.nn.silu(z), p['g_ssd'])
    pos = jnp.arange(L, dtype=f32) + pos_offset
    q = rope_partial(q.reshape(bsz, L, DIFF_HEADS, 2, DIFF_QK_DIM), pos)
    k = rope_partial(k.reshape(bsz, L, DIFF_HEADS, 2, DIFF_QK_DIM), pos)
    v = v.reshape(bsz, L, DIFF_HEADS, DIFF_V_DIM)
    lam = (jnp.exp(jnp.sum(p['lam_q1'].astype(f32) * p['lam_k1'].astype(f32)))
           - jnp.exp(jnp.sum(p['lam_q2'].astype(f32) * p['lam_k2'].astype(f32))) + lambda_init)
    if k_past is None:
        o = diff_attn_blockwise(q, k, v, lam)
    else:
        n_past = k_past.shape[1]
        k_all = jnp.concatenate([k_past.reshape(bsz, n_past, DIFF_HEADS, 2, DIFF_QK_DIM).astype(k.dtype), k], axis=1)
        v_all = jnp.concatenate([v_past.astype(v.dtype), v], axis=1)
        o = diff_attn_dense(q, k_all, v_all, lam)
    o = rmsnorm(o, p['g_subln']) * (1.0 - lambda_init)
    mixed = jnp.concatenate([y_ssd, o.reshape(bsz, L, DIFF_WIDTH).astype(y_ssd.dtype)], axis=-1)
    x = x + mixed @ p['w_out']
    qm = (rmsnorm(x, p['g_mem_q']) @ p['w_mq']).reshape(bsz, L, MEM_HEADS, MEM_HEAD_DIM)
    sm = jnp.einsum('blhd,bmhd->bhlm', qm, mem_k.astype(qm.dtype)).astype(f32) / math.sqrt(MEM_HEAD_DIM)
    pm = jax.nn.softmax(sm, axis=-1).astype(x.dtype)
    om = jnp.einsum('bhlm,bmhd->blhd', pm, mem_v.astype(x.dtype)).reshape(bsz, L, D_MODEL)
    x = x + om @ p['w_mo']
    x = x + peer_ffn(rmsnorm(x, p['g_ffn']), p['w_pq'], p['peer_keys'], p['peer_u'], p['peer_v'])
    k_rows = k.reshape(bsz, L, DIFF_HEADS, 2 * DIFF_QK_DIM)
    return x, k_rows, v, h_last.astype(x.dtype), conv_new


def setup_inputs(seed: int = 0) -> dict:
    key = jax.random.key(seed)
    ks = iter(jax.random.split(key, 48))
    f32 = jnp.float32
    nrm = lambda shape, scale: jax.random.normal(next(ks), shape, f32) * scale
    gain = lambda shape: 1.0 + 0.02 * jax.random.normal(next(ks), shape, f32)
    dsc = D_MODEL ** -0.5
    x_prompt = nrm((BATCH, SEQ, D_MODEL), 1.0)
    x_sample = nrm((DEC_BATCH, DEC_SEQ, D_MODEL), 1.0)
    cache_attn_k = nrm((DEPTH, DEC_BATCH, PAST_LEN, DIFF_HEADS, 2 * DIFF_QK_DIM), 1.0)
    cache_attn_v = nrm((DEPTH, DEC_BATCH, PAST_LEN, DIFF_HEADS, DIFF_V_DIM), 1.0)
    cache_mem_k = nrm((DEPTH, DEC_BATCH, N_MEM, MEM_HEADS, MEM_HEAD_DIM), 1.0)
    cache_mem_v = nrm((DEPTH, DEC_BATCH, N_MEM, MEM_HEADS, MEM_HEAD_DIM), 1.0)
    state_ssm = nrm((DEPTH, DEC_BATCH, SSD_HEADS, SSD_HEAD_DIM, SSD_STATE), 0.1)
    state_conv = nrm((DEPTH, DEC_BATCH, CONV_WIDTH - 1, CONV_DIM), 1.0)
    mem_prompt = nrm((BATCH, N_MEM, D_MODEL), 1.0)
    g_mix = gain((DEPTH, D_MODEL))
    w_in = nrm((DEPTH, D_MODEL, IN_DIM), dsc)
    conv_w = nrm((DEPTH, CONV_WIDTH, CONV_DIM), CONV_WIDTH ** -0.5)
    conv_b = nrm((DEPTH, CONV_DIM), 0.01)
    dt0 = jnp.exp(jax.random.uniform(next(ks), (DEPTH, SSD_HEADS), f32, math.log(1e-3), math.log(1e-1)))
    dt_bias = dt0 + jnp.log(-jnp.expm1(-dt0))
    a_log = jnp.log(jax.random.uniform(next(ks), (DEPTH, SSD_HEADS), f32, 1.0, 16.0))
    d_skip = gain((DEPTH, SSD_HEADS))
    g_ssd = gain((DEPTH, SSD_WIDTH))
    lam_q1 = nrm((DEPTH, DIFF_QK_DIM), 0.1)
    lam_k1 = nrm((DEPTH, DIFF_QK_DIM), 0.1)
    lam_q2 = nrm((DEPTH, DIFF_QK_DIM), 0.1)
    lam_k2 = nrm((DEPTH, DIFF_QK_DIM), 0.1)
    g_subln = gain((DEPTH, DIFF_V_DIM))
    w_out = nrm((DEPTH, D_MODEL, D_MODEL), dsc)
    g_mem_q = gain((DEPTH, D_MODEL))
    g_mem_kv = gain((DEPTH, D_MODEL))
    w_mq = nrm((DEPTH, D_MODEL, D_MODEL), dsc)
    w_mk = nrm((DEPTH, D_MODEL, D_MODEL), dsc)
    w_mv = nrm((DEPTH, D_MODEL, D_MODEL), dsc)
    w_mo = nrm((DEPTH, D_MODEL, D_MODEL), dsc)
    g_ffn = gain((DEPTH, D_MODEL))
    w_pq = nrm((DEPTH, D_MODEL, PEER_HEADS * PEER_QUERY_DIM), dsc)
    peer_keys = nrm((DEPTH, PEER_HEADS, 2, N_KEYS, PEER_HALF), PEER_HALF ** -0.5)
    peer_u = nrm((DEPTH, N_EXPERTS, D_MODEL), dsc)
    peer_v = nrm((DEPTH, N_EXPERTS, D_MODEL), (PEER_HEADS * PEER_TOPK) ** -0.5)
    g_final = gain((D_MODEL,))
    return {'x_prompt': x_prompt, 'x_sample': x_sample, 'cache_attn_k': cache_attn_k, 'cache_attn_v': cache_attn_v,
            'cache_mem_k': cache_mem_k, 'cache_mem_v': cache_mem_v, 'state_ssm': state_ssm, 'state_conv': state_conv,
            'mem_prompt': mem_prompt, 'g_mix': g_mix, 'w_in': w_in, 'conv_w': conv_w, 'conv_b': conv_b,
            'dt_bias': dt_bias, 'a_log': a_log, 'd_skip': d_skip, 'g_ssd': g_ssd, 'lam_q1': lam_q1, 'lam_k1': lam_k1,
            'lam_q2': lam_q2, 'lam_k2': lam_k2, 'g_subln': g_subln, 'w_out': w_out, 'g_mem_q': g_mem_q,
            'g_mem_kv': g_mem_kv, 'w_mq': w_mq, 'w_mk': w_mk, 'w_mv': w_mv, 'w_mo': w_mo, 'g_ffn': g_ffn,
            'w_pq': w_pq, 'peer_keys': peer_keys, 'peer_u': peer_u, 'peer_v': peer_v, 'g_final': g_final}


def reference(x_prompt, x_sample, cache_attn_k, cache_attn_v, cache_mem_k, cache_mem_v, state_ssm, state_conv,
              mem_prompt, g_mix, w_in, conv_w, conv_b, dt_bias, a_log, d_skip, g_ssd, lam_q1, lam_k1, lam_q2, lam_k2,
              g_subln, w_out, g_mem_q, g_mem_kv, w_mq, w_mk, w_mv, w_mo, g_ffn, w_pq, peer_keys, peer_u, peer_v,
              g_final):
    xp = x_prompt
    xs = x_sample
    bp = xp.shape[0]
    past_len = cache_attn_k.shape[2]
    kp_l, vp_l, hp_l, cp_l, mkp_l, mvp_l = [], [], [], [], [], []
    ks_l, vs_l, hs_l, cs_l = [], [], [], []
    for l in range(DEPTH):
        p = {'g_mix': g_mix[l], 'w_in': w_in[l], 'conv_w': conv_w[l], 'conv_b': conv_b[l], 'dt_bias': dt_bias[l],
             'a_log': a_log[l], 'd_skip': d_skip[l], 'g_ssd': g_ssd[l], 'lam_q1': lam_q1[l], 'lam_k1': lam_k1[l],
             'lam_q2': lam_q2[l], 'lam_k2': lam_k2[l], 'g_subln': g_subln[l], 'w_out': w_out[l],
             'g_mem_q': g_mem_q[l], 'w_mq': w_mq[l], 'w_mo': w_mo[l], 'g_ffn': g_ffn[l], 'w_pq': w_pq[l],
             'peer_keys': peer_keys[l], 'peer_u': peer_u[l], 'peer_v': peer_v[l]}
        lambda_init = 0.8 - 0.6 * math.exp(-0.3 * l)
        mk_p, mv_p = memory_kv(mem_prompt, g_mem_kv[l], w_mk[l], w_mv[l])
        h0_p = jnp.zeros((bp, SSD_HEADS, SSD_HEAD_DIM, SSD_STATE), jnp.float32)
        conv0_p = jnp.zeros((bp, CONV_WIDTH - 1, CONV_DIM), xp.dtype)
        xp, kp, vp, hp, cp = trunk_layer(xp, 0, None, None, h0_p, conv0_p, mk_p, mv_p, p, lambda_init)
        xs, ksn, vsn, hsn, csn = trunk_layer(xs, past_len, cache_attn_k[l], cache_attn_v[l], state_ssm[l],
                                             state_conv[l], cache_mem_k[l], cache_mem_v[l], p, lambda_init)
        kp_l.append(kp); vp_l.append(vp); hp_l.append(hp); cp_l.append(cp); mkp_l.append(mk_p); mvp_l.append(mv_p)
        ks_l.append(ksn); vs_l.append(vsn); hs_l.append(hsn); cs_l.append(csn)
    y_prompt = rmsnorm(xp, g_final)
    y_sample = rmsnorm(xs, g_final)
    return (y_prompt, y_sample, jnp.stack(kp_l), jnp.stack(vp_l), jnp.stack(hp_l), jnp.stack(cp_l),
            jnp.stack(mkp_l), jnp.stack(mvp_l), jnp.stack(ks_l), jnp.stack(vs_l), jnp.stack(hs_l), jnp.stack(cs_l))
```

```python
import math
from contextlib import ExitStack

import numpy as np
import concourse.bass as bass
import concourse.mybir as mybir
from concourse.bass_utils import run_bass_kernel_spmd

F32 = mybir.dt.float32
BF16 = mybir.dt.bfloat16
U32 = mybir.dt.uint32
AF = mybir.ActivationFunctionType
ALU = mybir.AluOpType

D = 1024
IN_DIM = 3080
EPS = 1e-6
NCORES = 8
C_Z, C_XBC, C_DT, C_Q, C_K, C_V = 0, 512, 1536, 1544, 2056, 2568
LAMBDA_INIT = 0.8 - 0.6 * math.exp(0.0)


class Trk:
    __slots__ = ("w", "r", "name")

    def __init__(self, name=""):
        self.w = None
        self.r = []
        self.name = name


class FW:
    SEM_CAP = 6000
    N_DMA_SEMS = 16

    def __init__(self, nc, stack):
        self.nc = nc
        self.stack = stack
        self.eng = {"pe": nc.tensor, "dve": nc.vector, "act": nc.scalar, "pool": nc.gpsimd, "sp": nc.sync}
        self._keep = []
        self.csem = {}
        self.ccnt = {}
        self.nsem = 0
        for e in ("pe", "dve", "act", "pool"):
            self._new_csem(e)
        self.dsem = {}
        for q in ("sp", "pool"):
            self.dsem[q] = [[self._sem(f"d{q}{i}"), 0] for i in range(self.N_DMA_SEMS)]
        self.dnext = {"sp": 0, "pool": 0}
        self.waited = {e: {} for e in self.eng}
        self.out_tokens = []
        self.n_inst = 0
        self.cur_stack = None

    def _sem(self, name):
        self.nsem += 1
        h = self.stack.enter_context(self.nc.semaphore(f"{name}_{self.nsem}"))
        self._keep.append(h)
        return h

    def _new_csem(self, e):
        self.csem[e] = self._sem(f"c{e}")
        self.ccnt[e] = 0

    def _wait(self, e, toks, defer=False):
        need = {}
        for t in toks:
            if t is None:
                continue
            sem, val, src = t
            if src == "pe" and e == "pe":
                continue
            k = id(sem)
            if k not in need or need[k][1] < val:
                need[k] = (sem, val)
        todo = [(k, sem, val) for k, (sem, val) in need.items() if self.waited[e].get(k, 0) < val]
        held = None
        if defer and todo:
            held = todo.pop()
        for k, sem, val in todo:
            self.eng[e].wait_ge(sem, val)
            self.waited[e][k] = val
            self.n_inst += 1
        if held is not None:
            k, sem, val = held
            self.waited[e][k] = val
            return (sem, val)
        return None

    @staticmethod
    def _deps(reads, writes):
        toks = []
        for b in reads:
            toks.append(b.w)
        for b in writes:
            toks.append(b.w)
            toks.extend(b.r)
        return toks

    @staticmethod
    def _commit(tok, reads, writes):
        for b in reads:
            if b not in writes:
                b.r.append(tok)
                if len(b.r) > 64:
                    b.r = b.r[-64:]
        for b in writes:
            b.w = tok
            b.r = []

    def op(self, e, fn, reads=(), writes=()):
        reads = [b for b in reads if b is not None]
        writes = [b for b in writes if b is not None]
        held = self._wait(e, self._deps(reads, writes), defer=True)
        if self.ccnt[e] >= self.SEM_CAP:
            self._new_csem(e)
        n0 = self.nc.n_instructions()
        ins = fn(self.eng[e])
        assert self.nc.n_instructions() - n0 == 1, "multi-instruction op: cannot attach wait"
        if held is not None:
            ins._wait_ge(held[0], held[1])
        self.ccnt[e] += 1
        ins.then_inc(self.csem[e], 1)
        tok = (self.csem[e], self.ccnt[e], e)
        self._commit(tok, reads, writes)
        self.n_inst += 1
        return tok

    def dma(self, q, out, in_, reads=(), writes=(), is_output=False, **kw):
        reads = [b for b in reads if b is not None]
        writes = [b for b in writes if b is not None]
        slot = self.dsem[q][self.dnext[q]]
        self.dnext[q] = (self.dnext[q] + 1) % self.N_DMA_SEMS
        if slot[1] >= self.SEM_CAP:
            slot[0] = self._sem(f"d{q}")
            slot[1] = 0
        toks = self._deps(reads, writes)
        if slot[1] > 0:
            toks.append((slot[0], slot[1], "dma"))
        held = self._wait(q, toks, defer=True)
        n0 = self.nc.n_instructions()
        ins = self.eng[q].dma_start(out=out, in_=in_, **kw)
        if held is not None:
            if self.nc.n_instructions() - n0 == 1:
                ins._wait_ge(held[0], held[1])
            else:
                raise AssertionError("multi-instruction dma")
        slot[1] += 16
        ins.then_inc(slot[0], 16)
        tok = (slot[0], slot[1], "dma")
        self._commit(tok, reads, writes)
        if is_output:
            self.out_tokens.append(tok)
        self.n_inst += 1
        return tok

    def finish(self):
        toks = list(self.out_tokens)
        for q in self.dsem:
            for slot in self.dsem[q]:
                if slot[1] > 0:
                    toks.append((slot[0], slot[1], "dma"))
        self._wait("sp", toks)

    def sb(self, name, shape, dtype=F32):
        st = self.cur_stack if self.cur_stack is not None else self.stack
        return st.enter_context(self.nc.sbuf_tensor(name, list(shape), dtype))

    def barrier(self):
        toks = []
        for e in ("pe", "dve", "act", "pool"):
            if self.ccnt[e] > 0:
                toks.append((self.csem[e], self.ccnt[e], "x"))
        for q in self.dsem:
            for slot in self.dsem[q]:
                if slot[1] > 0:
                    toks.append((slot[0], slot[1], "dma"))
        for e in ("pe", "dve", "act", "pool", "sp"):
            self._wait(e, toks)

    def ps(self, name, shape, dtype=F32):
        return self.stack.enter_context(self.nc.psum_tensor(name, list(shape), dtype))


class Prog:
    def __init__(self, SEQ, PAST, stages=3, debug=False):
        self.debug = debug
        self.SEQ, self.PAST = SEQ, PAST
        self.NT = SEQ // 128
        self.NOWN = self.NT // 4
        self.NKP = PAST // 128
        self.stages = stages
        self.nc = bass.Bass("TRN2", target_bir_lowering=False)
        self.in_shapes = {}
        self.out_shapes = {}

    def din(self, name, shape, dt=F32):
        self.in_shapes[name] = (tuple(shape), dt)
        return self.nc.dram_tensor(name, list(shape), dt, kind="ExternalInput").ap()

    def dout(self, name, shape, dt=F32):
        self.out_shapes[name] = (tuple(shape), dt)
        return self.nc.dram_tensor(name, list(shape), dt, kind="ExternalOutput").ap()

    def dscr(self, name, shape, dt=F32):
        if self.debug and dt == F32:
            return self.dout("dbg_" + name, shape, dt)
        return self.nc.dram_tensor(name, list(shape), dt, kind="Internal").ap()

    def build(self):
        with ExitStack() as st:
            self.fw = FW(self.nc, st)
            self._declare()
            self._setup()
            for ph, fn in ((1, self._phase1), (2, self._phase2), (3, self._phase3)):
                if self.stages >= ph:
                    with ExitStack() as pst:
                        self.fw.cur_stack = pst
                        fn()
                        self.fw.barrier()
                    self.fw.cur_stack = None
            self.fw.finish()
        return self.nc

    def _declare(self):
        SEQ, PAST, NT, NOWN = self.SEQ, self.PAST, self.NT, self.NOWN
        di, do, ds = self.din, self.dout, self.dscr
        I = self.I = {}
        O = self.O = {}
        S = self.S = {}
        I["x_all"] = di("x_all", [SEQ, D])
        I["x_own"] = di("x_own", [NOWN * 128, D])
        I["x_smp"] = di("x_smp", [128, D])
        I["mem_p"] = di("mem_p", [256, D])
        I["w_in"] = di("w_in", [D, IN_DIM])
        for n in ("w_out", "w_mq", "w_mk", "w_mv", "w_mo"):
            I[n] = di(n, [D, D])
        I["w_pq"] = di("w_pq", [D, 2048])
        if self.stages >= 3:
            I["keysT"] = di("keysT", [128, 16, 128])
            I["peer_uT"] = di("peer_uT", [128, 128, 8, 128])
            I["peer_v"] = di("peer_v", [16384, D])
        for n in ("g_mix", "g_memq", "g_memkv", "g_ffn", "g_final"):
            I[n] = di(n, [128, D])
        I["g_ssd"] = di("g_ssd", [128, 512])
        I["g_subln"] = di("g_subln", [128, 128])
        I["dt_bias"] = di("dt_bias", [128, 8])
        I["a_log"] = di("a_log", [128, 8])
        I["d_skip"] = di("d_skip", [128, 8])
        I["lam"] = di("lam", [128, 4, 64])
        I["conv_wT"] = di("conv_wT", [128, 8, 4])
        I["conv_bT"] = di("conv_bT", [128, 8])
        I["ident"] = di("ident", [128, 128])
        for L, nch in ((64, 2), (32, 4)):
            I[f"tri{L}"] = di(f"tri{L}", [128, 128])
            I[f"u{L}"] = di(f"u{L}", [128, 128])
            I[f"onb{L}"] = di(f"onb{L}", [128, 128])
            I[f"sel{L}"] = di(f"sel{L}", [128, nch, 128])
            I[f"rm{L}"] = di(f"rm{L}", [128, nch])
            I[f"cm{L}"] = di(f"cm{L}", [128, nch, 128])
        I["cos_p"] = di("cos_p", [128, NT, 8])
        I["sin_p"] = di("sin_p", [128, NT, 8])
        I["cos_s"] = di("cos_s", [128, 8])
        I["sin_s"] = di("sin_s", [128, 8])
        I["amask_p"] = di("amask_p", [128, 4, 128])
        I["amask_s"] = di("amask_s", [128, 5, 128])
        I["selj"] = di("selj", [128, 4])
        I["iota"] = di("iota", [128, 128])
        I["ck_T"] = di("ck_T", [4, 4, 128, PAST])
        I["cv"] = di("cv", [4, PAST, 512])
        I["cmk_T"] = di("cmk_T", [4, 4, 2, 128, 256])
        I["cmv"] = di("cmv", [4, 256, D])
        I["ssm_T"] = di("ssm_T", [4, 128, 512])
        I["conv_T"] = di("conv_T", [128, 8, 4, 3])

        O["y_own"] = do("y_own", [NOWN * 128, D])
        O["y_smp"] = do("y_smp", [128, D])
        O["newk"] = do("newk", [SEQ, 512])
        O["newv"] = do("newv", [SEQ, 512])
        O["ssm_p"] = do("ssm_p", [512, 128])
        O["conv_p"] = do("conv_p", [128, 8, 1, 3])
        O["memk_p"] = do("memk_p", [256, D])
        O["memv_p"] = do("memv_p", [256, D])
        O["newk_s"] = do("newk_s", [128, 512])
        O["newv_s"] = do("newv_s", [128, 512])
        O["ssm_s"] = do("ssm_s", [4, 512, 128])
        O["conv_s"] = do("conv_s", [128, 8, 4, 3])

        S["KT"] = ds("KT", [4, 128, SEQ], BF16)
        S["VA"] = ds("VA", [4, NT, 128, 130], BF16)
        S["q_own"] = ds("q_own", [NOWN, 128, 512])
        S["yn_own"] = ds("yn_own", [NOWN, 128, 512])
        S["o_own"] = ds("o_own", [NOWN, 128, 512])
        S["q_smp"] = ds("q_smp", [128, 512])
        S["yn_smp"] = ds("yn_smp", [128, 512])
        S["o_smp"] = ds("o_smp", [128, 512])
        if self.debug:
            for nm, shp in (("x1_own", [NOWN, 128, D]), ("x2_own", [NOWN, 128, D]), ("x1_smp", [128, D]), ("x2_smp", [128, D]), ("x3_smp", [128, D])):
                S[nm] = ds(nm, shp)
        S["KT_s"] = ds("KT_s", [4, 128, 128], BF16)
        S["VA_s"] = ds("VA_s", [4, 128, 130], BF16)
        self.tS = {k: Trk("scr_" + k) for k in S}

    def _load_const(self, name, shape, dt=F32, q="sp", src=None):
        t = self.fw.sb("c_" + name, shape, dt)
        trk = Trk("c_" + name)
        src = self.I[name] if src is None else src
        self.fw.dma(q, t[:], src, writes=[trk])
        return t, trk

    def _setup(self):
        fw, I = self.fw, self.I
        C = self.C = {}
        CT = self.CT = {}

        def ld(name, shape, dt=F32, q="sp"):
            C[name], CT[name] = self._load_const(name, shape, dt, q)

        ld("ident", [128, 128])
        C["identb"], CT["identb"] = self._load_const("identb", [128, 128], BF16, "pool", src=I["ident"])
        cst = C["cst"] = fw.sb("cst", [128, 8])
        CT["cst"] = Trk("cst")
        vals = [1.0, D * EPS, 512 * EPS, 128 * EPS, 0.0, EPS, 0.0, 0.0]
        for i, v in enumerate(vals):
            fw.op("dve", lambda e, i=i, v=v: e.memset(cst[:, i:i + 1], v), writes=[CT["cst"]])
        self.PS = [fw.ps(f"ps{i}", [128, 512]) for i in range(8)]
        self.PT = [Trk(f"ps{i}") for i in range(8)]

    def _phase1(self):
        fw, I, O, S, C, CT = self.fw, self.I, self.O, self.S, self.C, self.CT
        PS, PT = self.PS, self.PT
        sb = fw.sb
        def ld(name, shape, dt=F32, q="sp"):
            C[name], CT[name] = self._load_const(name, shape, dt, q)

        for L, nch in ((64, 2), (32, 4)):
            ld(f"tri{L}", [128, 128]); ld(f"u{L}", [128, 128]); ld(f"onb{L}", [128, 128])
            ld(f"sel{L}", [128, nch, 128]); ld(f"rm{L}", [128, nch]); ld(f"cm{L}", [128, nch, 128])
        ld("g_mix", [128, D]); ld("g_ssd", [128, 512])
        ld("dt_bias", [128, 8]); ld("a_log", [128, 8]); ld("d_skip", [128, 8])
        ld("conv_wT", [128, 8, 4]); ld("conv_bT", [128, 8])
        ld("cos_p", [128, self.NT, 8]); ld("sin_p", [128, self.NT, 8]); ld("cos_s", [128, 8]); ld("sin_s", [128, 8])
        ld("selj", [128, 4])
        a_t = C["a"] = fw.sb("a_neg", [128, 8]); CT["a"] = Trk("a")
        fw.op("act", lambda e: e.activation(a_t[:], C["a_log"][:], AF.Exp), reads=[CT["a_log"]], writes=[CT["a"]])
        fw.op("dve", lambda e: e.tensor_scalar(a_t[:], a_t[:], -1.0, None, ALU.mult), reads=[CT["a"]], writes=[CT["a"]])
        fw.op("dve", lambda e: e.tensor_scalar(C["g_mix"][:], C["g_mix"][:], math.sqrt(D), None, ALU.mult),
              reads=[CT["g_mix"]], writes=[CT["g_mix"]])
        fw.op("dve", lambda e: e.tensor_scalar(C["g_ssd"][:], C["g_ssd"][:], math.sqrt(512.0), None, ALU.mult),
              reads=[CT["g_ssd"]], writes=[CT["g_ssd"]])
        dsk = C["dskb"] = fw.sb("dskb", [128, 8, 64]); CT["dskb"] = Trk("dskb")
        fw.op("dve", lambda e: e.tensor_copy(dsk[:], C["d_skip"][:].unsqueeze(2).to_broadcast([128, 8, 64])),
              reads=[CT["d_skip"]], writes=[CT["dskb"]])
        w = C["w_in"] = fw.sb("w_in_sb", [128, 8, IN_DIM], BF16); CT["w_in"] = Trk("w_in")
        for k in range(8):
            fw.dma("pool", w[:, k, :], I["w_in"][k * 128:(k + 1) * 128, :], writes=[CT["w_in"]])
        W = self.W1 = {}
        T = self.T1 = {}

        def mk(name, shape, dt=F32, n=1):
            if n == 1:
                W[name] = sb("p1_" + name, shape, dt); T[name] = Trk(name)
            else:
                W[name] = [sb(f"p1_{name}{i}", shape, dt) for i in range(n)]
                T[name] = [Trk(f"{name}{i}") for i in range(n)]

        mk("xt", [128, D], F32, 2)
        mk("junk", [128, D], F32)
        mk("ss", [128, 4], F32)
        mk("hb", [128, D], BF16)
        mk("hT", [128, 8, 128], BF16)
        mk("cbuf", [128, 8, 140], F32)
        mk("cacc", [128, 8, 128], F32)
        mk("xc", [128, 8, 128], F32)
        mk("bct", [128, 4, 128], BF16, 2)
        mk("ctm", [128, 4, 2, 128], BF16, 2)
        mk("xs", [128, 512], F32, 2)
        mk("btm", [128, 256], BF16, 2)
        mk("dt", [128, 8], F32, 2)
        mk("da", [128, 8], F32)
        mk("ex", [128, 48], F32)
        mk("wch", [128, 4, 8], F32)
        mk("xdt", [128, 8, 64], BF16)
        mk("xdte", [128, 4, 512], BF16)
        mk("cbm", [128, 2, 128], F32)
        mk("dau", [128, 8, 128], F32)
        mk("dec", [128, 8, 128], F32)
        mk("mt", [128, 8, 128], BF16)
        mk("sst", [128, 8, 64], F32, 2)
        mk("sbf", [128, 4, 512], BF16)
        mk("stmp", [128, 8, 64], F32)
        mk("ytmp", [128, 8, 64], F32)
        mk("y", [128, 512], F32)
        mk("zs", [128, 512], F32, 2)
        mk("yn", [128, 512], F32)
        mk("q", [128, 512], F32, 2)
        mk("k", [128, 512], F32)
        mk("v", [128, 512], F32)
        mk("rt", [128, 4, 8, 8], F32)
        mk("ktt", [128, 4, 128], BF16)
        mk("va", [128, 4, 130], BF16, 2)
        mk("ownq", [128, 512], F32)
        mk("ownyn", [128, 512], F32)
        mk("h0", [128, 4, 512], F32)
        mk("sout", [128, 128], F32)
        mk("ptmp", [128, 512], F32)
        self.caccT = [Trk(f"cacc{i}") for i in range(8)]
        mk("rt2", [128, 4, 8, 8], F32)
        self.rtT = {nm: [Trk(f"rt{nm}{i}") for i in range(4)] for nm in ("q", "k")}
        self.xdteT = [Trk(f"xdte{i}") for i in range(4)]
        self.decT = [Trk(f"dec{i}") for i in range(2)]
        self.mtT = [Trk(f"mt{i}") for i in range(2)]
        mk("ss2", [128, 4], F32)
        mk("junk2", [128, 512], F32)
        for i in range(2):
            fw.op("pool", lambda e, i=i: e.memset(W["va"][i][:, :, 128:130], 1.0), writes=[T["va"][i]])
        fw.op("dve", lambda e: e.memset(W["sst"][0][:], 0.0), writes=[T["sst"][0]])
        fw.op("dve", lambda e: e.memset(W["cbuf"][:], 0.0), writes=[T["cbuf"]])

        self.s_cur = 0
        self._mixer_A(0, "p")
        for t in range(self.NT):
            if t + 1 < self.NT:
                self._mixer_A(t + 1, "p")
            self._mixer_B(t, "p")
        self._state_out(W["sst"][self.s_cur], T["sst"][self.s_cur], O["ssm_p"])
        cb_v = W["cbuf"][:, :, 0:131].rearrange("p k (s l) -> p k s l", s=1)
        fw.dma("sp", O["conv_p"], cb_v[:, :, :, 0:3], reads=[T["cbuf"]], is_output=True)
        self._mixer_tile(0, "s")

    def _state_out(self, s_ap, s_trk, out_ap):
        fw, C, CT, PS, PT, W, T = self.fw, self.C, self.CT, self.PS, self.PT, self.W1, self.T1
        sv = s_ap.rearrange("p h d -> p (h d)")
        for c4 in range(4):
            fw.op("pe", lambda e, c4=c4: e.transpose(PS[7][:, c4 * 128:(c4 + 1) * 128], sv[:, c4 * 128:(c4 + 1) * 128], C["ident"][:]),
                  reads=[s_trk, CT["ident"]], writes=[PT[7]])
        for c4 in range(4):
            fw.op("act", lambda e, c4=c4: e.copy(W["sout"][:], PS[7][:, c4 * 128:(c4 + 1) * 128]), reads=[PT[7]], writes=[T["sout"]])
            fw.dma("sp", out_ap[c4 * 128:(c4 + 1) * 128, :], W["sout"][:], reads=[T["sout"]], is_output=True)

    _DB = ("dt", "xs", "btm", "bct", "ctm", "zs", "q")

    def _views(self, t, mode, part):
        par = (t if mode == "p" else self.NT) % 2
        W, T = dict(self.W1), dict(self.T1)
        for nm in self._DB:
            W[nm], T[nm] = self.W1[nm][par], self.T1[nm][par]
        if part == "B":
            W["ss"], T["ss"] = self.W1["ss2"], self.T1["ss2"]
            W["junk"], T["junk"] = self.W1["junk2"], self.T1["junk2"]
        return W, T

    def _mixer_tile(self, t, mode):
        self._mixer_A(t, mode)
        self._mixer_B(t, mode)

    def _mixer_A(self, t, mode):
        fw, I, O, S, C, CT, tS = self.fw, self.I, self.O, self.S, self.C, self.CT, self.tS
        PS, PT = self.PS, self.PT
        W, T = self._views(t, mode, "A")
        P = mode == "p"
        L = 64 if P else 32
        NCH = 2 if P else 4
        NSEG, SL = (1, 128) if P else (4, 32)
        sfx = str(L)
        xt, xtT = W["xt"][t % 2], T["xt"][t % 2]
        src = I["x_all"][t * 128:(t + 1) * 128, :] if P else I["x_smp"]
        fw.dma("sp", xt[:], src, writes=[xtT])
        fw.op("dve", lambda e: e.memset(W["ss"][:, 0:1], 0.0), writes=[T["ss"]])
        fw.op("act", lambda e: e.activation(W["junk"][:], xt[:], AF.Square, accum_out=W["ss"][:, 0:1]),
              reads=[xtT], writes=[T["junk"], T["ss"]])
        fw.op("act", lambda e: e.activation(W["ss"][:, 1:2], W["ss"][:, 0:1], AF.Sqrt, bias=C["cst"][:, 1:2], scale=1.0),
              reads=[T["ss"], CT["cst"]], writes=[T["ss"]])
        fw.op("dve", lambda e: e.reciprocal(W["ss"][:, 2:3], W["ss"][:, 1:2]), reads=[T["ss"]], writes=[T["ss"]])
        fw.op("dve", lambda e: e.scalar_tensor_tensor(W["hb"][:], xt[:], W["ss"][:, 2:3], C["g_mix"][:], ALU.mult, ALU.mult),
              reads=[xtT, T["ss"], CT["g_mix"]], writes=[T["hb"]])
        psb = PS[0][:].bitcast(BF16)
        for k in range(8):
            fw.op("pe", lambda e, k=k: e.transpose(psb[:, k * 128:(k + 1) * 128], W["hb"][:, k * 128:(k + 1) * 128], C["identb"][:]),
                  reads=[T["hb"], CT["identb"]], writes=[PT[0]])
        fw.op("act", lambda e: e.copy(W["hT"][:].rearrange("p k n -> p (k n)"), psb[:, :]), reads=[PT[0]], writes=[T["hT"]])
        wi = C["w_in"]

        def proj_tm(bank, c0, n):
            for k in range(8):
                fw.op("pe", lambda e, k=k: e.matmul(PS[bank][:, 0:n], W["hT"][:, k, :], wi[:, k, c0:c0 + n], start=(k == 0), stop=(k == 7)),
                      reads=[T["hT"], CT["w_in"]], writes=[PT[bank]])

        for ck in range(8):
            bank = 1 + ck // 4
            for k in range(8):
                fw.op("pe", lambda e, k=k, ck=ck, bank=bank: e.matmul(
                    PS[bank][:, (ck % 4) * 128:(ck % 4 + 1) * 128], wi[:, k, C_XBC + ck * 128:C_XBC + (ck + 1) * 128], W["hT"][:, k, :],
                    start=(k == 0), stop=(k == 7)), reads=[T["hT"], CT["w_in"]], writes=[PT[bank]])
        proj_tm(3, C_Z, 512)
        proj_tm(4, C_Q, 512)
        proj_tm(5, C_K, 512)
        proj_tm(6, C_V, 512)
        for k in range(8):
            fw.op("pe", lambda e, k=k: e.matmul(PS[7][:, 0:8], W["hT"][:, k, :], wi[:, k, C_DT:C_DT + 8], start=(k == 0), stop=(k == 7)),
                  reads=[T["hT"], CT["w_in"]], writes=[PT[7]])
        cb_v = W["cbuf"][:, :, 0:NSEG * (SL + 3)].rearrange("p k (s l) -> p k s l", s=NSEG)
        if not P:
            fw.dma("sp", cb_v[:, :, :, 0:3], I["conv_T"], writes=[T["cbuf"]])
        for hb2 in range(2):
            fw.op("act", lambda e, hb2=hb2: e.copy(
                cb_v[:, 4 * hb2:4 * hb2 + 4, :, 3:3 + SL],
                PS[1 + hb2][:].rearrange("p (k s l) -> p k s l", k=4, s=NSEG)), reads=[PT[1 + hb2]], writes=[T["cbuf"]])
        acc_v = W["cacc"][:].rearrange("p k (s l) -> p k s l", s=NSEG)
        cT = self.caccT
        for ck in range(8):
            fw.op("dve", lambda e, ck=ck: e.tensor_scalar(acc_v[:, ck], cb_v[:, ck, :, 0:SL], C["conv_wT"][:, ck, 0:1], C["conv_bT"][:, ck:ck + 1],
                                                          ALU.mult, ALU.add),
                  reads=[T["cbuf"], CT["conv_wT"], CT["conv_bT"]], writes=[cT[ck]])
        for j in range(1, 4):
            for ck in range(8):
                fw.op("dve", lambda e, ck=ck, j=j: e.scalar_tensor_tensor(acc_v[:, ck], cb_v[:, ck, :, j:j + SL], C["conv_wT"][:, ck, j:j + 1],
                                                                         acc_v[:, ck], ALU.mult, ALU.add),
                      reads=[T["cbuf"], CT["conv_wT"], cT[ck]], writes=[cT[ck]])
        fw.op("act", lambda e: e.activation(W["xc"][:], W["cacc"][:], AF.Silu), reads=list(cT), writes=[T["xc"]])
        if P:
            fw.op("dve", lambda e: e.tensor_copy(cb_v[:, :, :, 0:3], cb_v[:, :, :, SL:SL + 3]), reads=[T["cbuf"]], writes=[T["cbuf"]])
        else:
            fw.dma("sp", O["conv_s"], cb_v[:, :, :, SL:SL + 3], reads=[T["cbuf"]], is_output=True)
        fw.op("act", lambda e: e.copy(W["bct"][:], W["xc"][:, 4:8, :]), reads=[T["xc"]], writes=[T["bct"]])
        fw.op("dve", lambda e: e.tensor_tensor(W["ctm"][:, 0:NCH], W["xc"][:, 6:8, :].unsqueeze(1).to_broadcast([128, NCH, 2, 128]),
                                                C["cm" + sfx][:].unsqueeze(2).to_broadcast([128, NCH, 2, 128]), ALU.mult),
              reads=[T["xc"], CT["cm" + sfx]], writes=[T["ctm"]])
        for ck in range(6):
            bank = 1 + ck // 4
            fw.op("pe", lambda e, ck=ck, bank=bank: e.transpose(PS[bank][:, (ck % 4) * 128:(ck % 4 + 1) * 128], W["xc"][:, ck, :], C["ident"][:]),
                  reads=[T["xc"], CT["ident"]], writes=[PT[bank]])
        fw.op("act", lambda e: e.copy(W["xs"][:], PS[1][:]), reads=[PT[1]], writes=[T["xs"]])
        fw.op("act", lambda e: e.copy(W["btm"][:], PS[2][:, 0:256]), reads=[PT[2]], writes=[T["btm"]])
        fw.op("dve", lambda e: e.tensor_tensor(W["dt"][:], PS[7][:, 0:8], C["dt_bias"][:], ALU.add), reads=[PT[7], CT["dt_bias"]], writes=[T["dt"]])
        fw.op("act", lambda e: e.activation(W["dt"][:], W["dt"][:], AF.Exp), reads=[T["dt"]], writes=[T["dt"]])
        fw.op("act", lambda e: e.activation(W["dt"][:], W["dt"][:], AF.Ln, bias=C["cst"][:, 0:1], scale=1.0), reads=[T["dt"], CT["cst"]], writes=[T["dt"]])
        fw.op("act", lambda e: e.activation(W["zs"][:], PS[3][:], AF.Silu), reads=[PT[3]], writes=[T["zs"]])
        fw.op("act", lambda e: e.copy(W["q"][:], PS[4][:]), reads=[PT[4]], writes=[T["q"]])
        fw.op("dve", lambda e: e.tensor_copy(W["k"][:], PS[5][:]), reads=[PT[5]], writes=[T["k"]])
        fw.op("act", lambda e: e.copy(W["v"][:], PS[6][:]), reads=[PT[6]], writes=[T["v"]])
        cos = C["cos_p"][:, t, :] if P else C["cos_s"][:]
        sin = C["sin_p"][:, t, :] if P else C["sin_s"][:]
        cosb = cos.unsqueeze(1).to_broadcast([128, 8, 8])
        sinb = sin.unsqueeze(1).to_broadcast([128, 8, 8])
        tcs = CT["cos_p"] if P else CT["cos_s"]
        tsn = CT["sin_p"] if P else CT["sin_s"]
        rp = {}
        for nm, rtn in (("q", "rt"), ("k", "rt2")):
            v3 = W[nm][:].rearrange("p (g d) -> p g d", g=8)
            rp[nm] = (v3[:, :, 0:8], v3[:, :, 8:16], W[rtn], self.rtT[nm])
        for step in range(6):
            for nm in ("q", "k"):
                t1, t2, rt, rT = rp[nm]
                if step == 0:
                    fw.op("dve", lambda e, t1=t1, rt=rt: e.tensor_tensor(rt[:, 0], t1, cosb, ALU.mult), reads=[T[nm], tcs], writes=[rT[0]])
                elif step == 1:
                    fw.op("dve", lambda e, t2=t2, rt=rt: e.tensor_tensor(rt[:, 1], t2, sinb, ALU.mult), reads=[T[nm], tsn], writes=[rT[1]])
                elif step == 2:
                    fw.op("dve", lambda e, t2=t2, rt=rt: e.tensor_tensor(rt[:, 2], t2, cosb, ALU.mult), reads=[T[nm], tcs], writes=[rT[2]])
                elif step == 3:
                    fw.op("dve", lambda e, t1=t1, rt=rt: e.tensor_tensor(rt[:, 3], t1, sinb, ALU.mult), reads=[T[nm], tsn], writes=[rT[3]])
                elif step == 4:
                    fw.op("dve", lambda e, t1=t1, rt=rt: e.tensor_tensor(t1, rt[:, 0], rt[:, 1], ALU.subtract), reads=[rT[0], rT[1], rT[3]], writes=[T[nm]])
                else:
                    fw.op("dve", lambda e, t2=t2, rt=rt: e.tensor_tensor(t2, rt[:, 2], rt[:, 3], ALU.add), reads=[rT[2], rT[3], rT[1]], writes=[T[nm]])
        ko = O["newk"][t * 128:(t + 1) * 128, :] if P else O["newk_s"]
        vo = O["newv"][t * 128:(t + 1) * 128, :] if P else O["newv_s"]
        fw.dma("sp", ko, W["k"][:], reads=[T["k"]], is_output=True)
        fw.dma("sp", vo, W["v"][:], reads=[T["v"]], is_output=True)
        for h in range(4):
            fw.op("pe", lambda e, h=h: e.transpose(PS[5][:, h * 128:(h + 1) * 128], W["k"][:, h * 128:(h + 1) * 128], C["ident"][:]),
                  reads=[T["k"], CT["ident"]], writes=[PT[5]])
        fw.op("act", lambda e: e.copy(W["ktt"][:].rearrange("p h n -> p (h n)"), PS[5][:]), reads=[PT[5]], writes=[T["ktt"]])
        va, vaT = W["va"][t % 2], T["va"][t % 2]
        fw.op("act", lambda e: e.copy(va[:, :, 0:128], W["v"][:].rearrange("p (h d) -> p h d", h=4)), reads=[T["v"]], writes=[vaT])
        if P:
            fw.dma("sp", S["KT"][:, :, t * 128:(t + 1) * 128].rearrange("h p n -> p h n"), W["ktt"][:], reads=[T["ktt"]], writes=[tS["KT"]])
            fw.dma("sp", S["VA"][:, t, :, :].rearrange("h p n -> p h n"), va[:], reads=[vaT], writes=[tS["VA"]])
        else:
            fw.dma("sp", S["KT_s"].rearrange("h p n -> p h n"), W["ktt"][:], reads=[T["ktt"]], writes=[tS["KT_s"]])
            fw.dma("sp", S["VA_s"].rearrange("h p n -> p h n"), va[:], reads=[vaT], writes=[tS["VA_s"]])
    def _mixer_B(self, t, mode):
        fw, I, O, S, C, CT, tS = self.fw, self.I, self.O, self.S, self.C, self.CT, self.tS
        PS, PT = self.PS, self.PT
        W, T = self._views(t, mode, "B")
        P = mode == "p"
        L = 64 if P else 32
        NCH = 2 if P else 4
        sfx = str(L)
        xs3 = W["xs"][:].rearrange("p (h d) -> p h d", h=8)
        tri, uu, onb, sel, rm = C["tri" + sfx], C["u" + sfx], C["onb" + sfx], C["sel" + sfx], C["rm" + sfx]
        ctri, cu, conb, csel, crm = CT["tri" + sfx], CT["u" + sfx], CT["onb" + sfx], CT["sel" + sfx], CT["rm" + sfx]
        fw.op("dve", lambda e: e.tensor_tensor(W["da"][:], W["dt"][:], C["a"][:], ALU.mult), reads=[T["dt"], CT["a"]], writes=[T["da"]])
        fw.op("pe", lambda e: e.matmul(PS[7][:, 0:8], tri[:], W["da"][:], start=True, stop=True), reads=[T["da"], ctri], writes=[PT[7]])
        fw.op("pe", lambda e: e.matmul(PS[7][:, 8:16], onb[:], W["da"][:], start=True, stop=True), reads=[T["da"], conb], writes=[PT[7]])
        for ch in range(NCH):
            fw.op("pe", lambda e, ch=ch: e.matmul(PS[7][:, 16 + 8 * ch:24 + 8 * ch], sel[:, ch, :], W["da"][:], start=True, stop=True),
                  reads=[T["da"], csel], writes=[PT[7]])
        nex = 16 + 8 * NCH
        ex = W["ex"]
        fw.op("dve", lambda e: e.tensor_copy(ex[:, 0:nex], PS[7][:, 0:nex]), reads=[PT[7]], writes=[T["ex"]])
        fw.op("dve", lambda e: e.tensor_tensor(ex[:, 8:16], ex[:, 8:16], ex[:, 0:8], ALU.subtract), reads=[T["ex"]], writes=[T["ex"]])
        fw.op("act", lambda e: e.activation(ex[:, 0:nex], ex[:, 0:nex], AF.Exp), reads=[T["ex"]], writes=[T["ex"]])
        fw.op("dve", lambda e: e.tensor_tensor(W["wch"][:, 0, :], W["dt"][:], ex[:, 8:16], ALU.mult), reads=[T["dt"], T["ex"]], writes=[T["wch"]])
        for ch in range(NCH - 1, -1, -1):
            fw.op("dve", lambda e, ch=ch: e.tensor_scalar(W["wch"][:, ch, :], W["wch"][:, 0, :], rm[:, ch:ch + 1], None, ALU.mult),
                  reads=[T["wch"], crm], writes=[T["wch"]])
        xs3 = W["xs"][:].rearrange("p (h d) -> p h d", h=8)
        fw.op("dve", lambda e: e.tensor_tensor(W["xdt"][:], xs3, W["dt"][:].unsqueeze(2).to_broadcast([128, 8, 64]), ALU.mult),
              reads=[T["xs"], T["dt"]], writes=[T["xdt"]])
        for ch in range(NCH):
            eng = "dve"
            fw.op(eng, lambda e, ch=ch: e.tensor_tensor(W["xdte"][:, ch, :].rearrange("p (h d) -> p h d", h=8), xs3,
                                                        W["wch"][:, ch, :].unsqueeze(2).to_broadcast([128, 8, 64]), ALU.mult),
                  reads=[T["xs"], T["wch"]], writes=[self.xdteT[ch]])
        for g in range(2):
            fw.op("pe", lambda e, g=g: e.matmul(PS[1][:, g * 128:(g + 1) * 128], W["bct"][:, g, :], W["bct"][:, 2 + g, :], start=True, stop=True),
                  reads=[T["bct"]], writes=[PT[1]])
        fw.op("dve", lambda e: e.tensor_tensor(W["cbm"][:], PS[1][:, 0:256].rearrange("p (g n) -> p g n", g=2),
                                               tri[:].unsqueeze(1).to_broadcast([128, 2, 128]), ALU.mult), reads=[PT[1], ctri], writes=[T["cbm"]])
        fw.op("dve", lambda e: e.tensor_tensor(W["dau"][:], uu[:].unsqueeze(1).to_broadcast([128, 8, 128]),
                                                W["da"][:].unsqueeze(2).to_broadcast([128, 8, 128]), ALU.mult), reads=[cu, T["da"]], writes=[T["dau"]])
        for h in range(8):
            bank = 2 + h // 4
            fw.op("pe", lambda e, h=h, bank=bank: e.matmul(PS[bank][:, (h % 4) * 128:(h % 4 + 1) * 128], W["dau"][:, h, :], tri[:], start=True, stop=True),
                  reads=[T["dau"], ctri], writes=[PT[bank]])
        for g in range(2):
            fw.op("act", lambda e, g=g: e.activation(W["dec"][:, 4 * g:4 * g + 4, :].rearrange("p h n -> p (h n)"), PS[2 + g][:], AF.Exp),
                  reads=[PT[2 + g]], writes=[self.decT[g]])
        for g in range(2):
            fw.op("dve", lambda e, g=g: e.tensor_tensor(W["mt"][:, 4 * g:4 * g + 4, :], W["dec"][:, 4 * g:4 * g + 4, :],
                                                        W["cbm"][:, g, :].unsqueeze(1).to_broadcast([128, 4, 128]), ALU.mult),
                  reads=[self.decT[g], T["cbm"]], writes=[self.mtT[g]])
        for h in range(8):
            fw.op("pe", lambda e, h=h: e.matmul(PS[4][:, h * 64:(h + 1) * 64], W["mt"][:, h, :], W["xdt"][:, h, :], start=True, stop=True),
                  reads=[self.mtT[h // 4], T["xdt"]], writes=[PT[4]])
        if P:
            s_in, s_inT = W["sst"][self.s_cur], T["sst"][self.s_cur]
        else:
            fw.dma("sp", W["h0"][:], I["ssm_T"].rearrange("s p f -> p s f"), writes=[T["h0"]])
        for ch in range(NCH):
            if P:
                src_ap, src_trk = s_in[:], s_inT
            else:
                src_ap, src_trk = W["h0"][:, ch, :].rearrange("p (h d) -> p h d", h=8), T["h0"]
            fw.op("act", lambda e, ch=ch, src_ap=src_ap: e.copy(W["sbf"][:, ch, :].rearrange("p (h d) -> p h d", h=8), src_ap),
                  reads=[src_trk], writes=[T["sbf"]])
            for h in range(8):
                g = h // 4
                fw.op("pe", lambda e, h=h, g=g, ch=ch: e.matmul(PS[6][:, h * 64:(h + 1) * 64], W["btm"][:, g * 128:(g + 1) * 128],
                                                                W["xdte"][:, ch, h * 64:(h + 1) * 64], start=True, stop=True),
                      reads=[T["btm"], self.xdteT[ch]], writes=[PT[6]])
            cdb = ex[:, 16 + 8 * ch:24 + 8 * ch].unsqueeze(2).to_broadcast([128, 8, 64])
            fw.op("dve", lambda e, src_ap=src_ap, cdb=cdb: e.tensor_tensor(W["stmp"][:], src_ap, cdb, ALU.mult), reads=[src_trk, T["ex"]], writes=[T["stmp"]])
            if P:
                nxt = 1 - self.s_cur
                fw.op("dve", lambda e, nxt=nxt: e.tensor_tensor(W["sst"][nxt][:], W["stmp"][:], PS[6][:].rearrange("p (h d) -> p h d", h=8), ALU.add),
                      reads=[T["stmp"], PT[6]], writes=[T["sst"][nxt]])
                self.s_cur = nxt
                s_in, s_inT = W["sst"][nxt], T["sst"][nxt]
            else:
                fw.op("dve", lambda e: e.tensor_tensor(W["stmp"][:], W["stmp"][:], PS[6][:].rearrange("p (h d) -> p h d", h=8), ALU.add),
                      reads=[T["stmp"], PT[6]], writes=[T["stmp"]])
                self._state_out(W["stmp"][:], T["stmp"], O["ssm_s"][ch])
        for h in range(8):
            g = h // 4
            for ch in range(NCH):
                fw.op("pe", lambda e, h=h, g=g, ch=ch: e.matmul(PS[5][:, h * 64:(h + 1) * 64], W["ctm"][:, ch, g, :], W["sbf"][:, ch, h * 64:(h + 1) * 64],
                                                                start=(ch == 0), stop=(ch == NCH - 1)),
                      reads=[T["ctm"], T["sbf"]], writes=[PT[5]])
        y3 = W["y"][:].rearrange("p (h d) -> p h d", h=8)
        fw.op("dve", lambda e: e.tensor_tensor(W["ytmp"][:], PS[5][:].rearrange("p (h d) -> p h d", h=8),
                                               ex[:, 0:8].unsqueeze(2).to_broadcast([128, 8, 64]), ALU.mult), reads=[PT[5], T["ex"]], writes=[T["ytmp"]])
        fw.op("dve", lambda e: e.tensor_tensor(y3, W["ytmp"][:], PS[4][:].rearrange("p (h d) -> p h d", h=8), ALU.add),
              reads=[T["ytmp"], PT[4]], writes=[T["y"]])
        fw.op("dve", lambda e: e.tensor_tensor(W["ytmp"][:], xs3, C["dskb"][:], ALU.mult), reads=[T["xs"], CT["dskb"]], writes=[T["ytmp"]])
        fw.op("dve", lambda e: e.tensor_tensor(y3, y3, W["ytmp"][:], ALU.add), reads=[T["ytmp"], T["y"]], writes=[T["y"]])
        fw.op("dve", lambda e: e.tensor_tensor(W["y"][:], W["y"][:], W["zs"][:], ALU.mult), reads=[T["y"], T["zs"]], writes=[T["y"]])
        fw.op("dve", lambda e: e.memset(W["ss"][:, 0:1], 0.0), writes=[T["ss"]])
        fw.op("act", lambda e: e.activation(W["junk"][:, 0:512], W["y"][:], AF.Square, accum_out=W["ss"][:, 0:1]),
              reads=[T["y"]], writes=[T["junk"], T["ss"]])
        fw.op("act", lambda e: e.activation(W["ss"][:, 1:2], W["ss"][:, 0:1], AF.Sqrt, bias=C["cst"][:, 2:3], scale=1.0),
              reads=[T["ss"], CT["cst"]], writes=[T["ss"]])
        fw.op("dve", lambda e: e.reciprocal(W["ss"][:, 2:3], W["ss"][:, 1:2]), reads=[T["ss"]], writes=[T["ss"]])
        fw.op("dve", lambda e: e.scalar_tensor_tensor(W["yn"][:], W["y"][:], W["ss"][:, 2:3], C["g_ssd"][:], ALU.mult, ALU.mult),
              reads=[T["y"], T["ss"], CT["g_ssd"]], writes=[T["yn"]])
        if P:
            r, i = t % 4, t // 4
            sj = C["selj"]
            for src, dst in (("q", "ownq"), ("yn", "ownyn")):
                if r == 0:
                    fw.op("dve", lambda e, src=src, dst=dst: e.tensor_scalar(W[dst][:], W[src][:], sj[:, 0:1], None, ALU.mult),
                          reads=[T[src], CT["selj"]], writes=[T[dst]])
                else:
                    fw.op("dve", lambda e, src=src, dst=dst, r=r: e.scalar_tensor_tensor(W[dst][:], W[src][:], sj[:, r:r + 1], W[dst][:], ALU.mult, ALU.add),
                          reads=[T[src], CT["selj"], T[dst]], writes=[T[dst]])
            if r == 3:
                fw.dma("sp", S["q_own"][i], W["ownq"][:], reads=[T["ownq"]], writes=[tS["q_own"]])
                fw.dma("sp", S["yn_own"][i], W["ownyn"][:], reads=[T["ownyn"]], writes=[tS["yn_own"]])
        else:
            fw.dma("sp", S["q_smp"], W["q"][:], reads=[T["q"]], writes=[tS["q_smp"]])
            fw.dma("sp", S["yn_smp"], W["yn"][:], reads=[T["yn"]], writes=[tS["yn_smp"]])

    def _phase2(self):
        fw, I, O, S, C, CT, tS = self.fw, self.I, self.O, self.S, self.C, self.CT, self.tS
        PS, PT = self.PS, self.PT
        SEQ, NT, NOWN, NKP = self.SEQ, self.NT, self.NOWN, self.NKP
        W, T = {}, {}

        def mk(name, shape, dt=F32, n=1):
            if n == 1:
                W[name] = fw.sb("p2_" + name, shape, dt); T[name] = Trk(name)
            else:
                W[name] = [fw.sb(f"p2_{name}{i}", shape, dt) for i in range(n)]
                T[name] = [Trk(f"{name}{i}") for i in range(n)]

        mk("lam", [128, 4, 64]); mk("lt", [128, 2, 64]); mk("lv", [128, 8])
        mk("gsub", [128, 128])
        mk("amp", [128, 4, 128], BF16); mk("ams", [128, 5, 128], BF16)
        mk("qin", [128, 512], F32, 2)
        mk("QT", [128, 4, NOWN * 128], BF16); mk("QTs", [128, 4, 128], BF16)
        mk("kt", [128, SEQ], BF16); mk("va", [128, NT, 130], BF16)
        mk("kts", [128, max(self.PAST, 128)], BF16, 4); mk("vas", [128, NKP, 130], BF16, 4)
        mk("ktn", [128, 128], BF16); mk("van", [128, 130], BF16)
        mk("rec", [128, 4]); mk("o", [128, 128]); mk("junk", [128, 128]); mk("ss", [128, 4])
        mk("on", [128, 128], F32, 2)
        fw.dma("sp", W["lam"][:], I["lam"], writes=[T["lam"]])
        fw.dma("sp", W["gsub"][:], I["g_subln"], writes=[T["gsub"]])
        fw.dma("pool", W["amp"][:], I["amask_p"], writes=[T["amp"]])
        fw.dma("pool", W["ams"][:], I["amask_s"], writes=[T["ams"]])
        fw.op("dve", lambda e: e.tensor_scalar(W["gsub"][:], W["gsub"][:], math.sqrt(128.0) * (1.0 - LAMBDA_INIT), None, ALU.mult),
              reads=[T["gsub"]], writes=[T["gsub"]])
        for i in range(2):
            fw.op("dve", lambda e, i=i: e.tensor_tensor(W["lt"][:, i, :], W["lam"][:, 2 * i, :], W["lam"][:, 2 * i + 1, :], ALU.mult),
                  reads=[T["lam"]], writes=[T["lt"]])
            fw.op("dve", lambda e, i=i: e.reduce_sum(W["lv"][:, i:i + 1], W["lt"][:, i, :], mybir.AxisListType.X), reads=[T["lt"]], writes=[T["lv"]])
        fw.op("act", lambda e: e.activation(W["lv"][:, 2:4], W["lv"][:, 0:2], AF.Exp), reads=[T["lv"]], writes=[T["lv"]])
        fw.op("dve", lambda e: e.tensor_tensor(W["lv"][:, 4:5], W["lv"][:, 2:3], W["lv"][:, 3:4], ALU.subtract), reads=[T["lv"]], writes=[T["lv"]])
        fw.op("dve", lambda e: e.tensor_scalar(W["lv"][:, 4:5], W["lv"][:, 4:5], LAMBDA_INIT, None, ALU.add), reads=[T["lv"]], writes=[T["lv"]])
        fw.op("dve", lambda e: e.tensor_scalar(W["lv"][:, 5:6], W["lv"][:, 4:5], -1.0, None, ALU.mult), reads=[T["lv"]], writes=[T["lv"]])
        for i in range(4):
            fw.op("pool", lambda e, i=i: e.memset(W["vas"][i][:, :, 128:130], 1.0), writes=[T["vas"][i]])
        LV = 9
        if LV < 2:
            return
        def load_qT(src_ap, src_trk, dst_ap, dst_trk, n):
            qin, qinT = W["qin"][n % 2], T["qin"][n % 2]
            fw.dma("sp", qin[:], src_ap, reads=[src_trk], writes=[qinT])
            for h in range(4):
                fw.op("pe", lambda e, h=h: e.transpose(PS[6][:, h * 128:(h + 1) * 128], qin[:, h * 128:(h + 1) * 128], C["ident"][:]),
                      reads=[qinT, CT["ident"]], writes=[PT[6]])
            fw.op("act", lambda e: e.mul(dst_ap, PS[6][:].rearrange("p (h n) -> p h n", h=4), 0.125), reads=[PT[6]], writes=[dst_trk])

        for i in range(NOWN):
            load_qT(S["q_own"][i], tS["q_own"], W["QT"][:, :, i * 128:(i + 1) * 128], T["QT"], i)
        load_qT(S["q_smp"], tS["q_smp"], W["QTs"][:], T["QTs"], NOWN)

        self._acnt = 0
        mk("qblk", [128, 256], BF16, 2)
        mk("E3", [128, 512], BF16, 3)
        for i in range(2):
            fw.op("dve", lambda e, i=i: e.memset(W["qblk"][i][:], 0.0), writes=[T["qblk"][i]])

        def attn(qT_fn, qT_trk, tiles, out_ap, out_trk):
            n = self._acnt
            self._acnt += 1
            ob = 4 + 2 * (n % 2)
            qb, qbT = W["qblk"][n % 2], T["qblk"][n % 2]
            for c in range(2):
                fw.op("dve", lambda e, c=c: e.tensor_copy(qb[64 * c:64 * c + 64, 128 * c:128 * c + 128], qT_fn(c)), reads=[qT_trk], writes=[qbT])
            ntl = len(tiles)
            groups = [tiles[g0:g0 + 2] for g0 in range(0, ntl, 2)]

            def emit_qk(g):
                bk = g % 4
                for kk, (kt_ap, va_ap, trks, m_ap, m_trk) in enumerate(groups[g]):
                    fw.op("pe", lambda e, kk=kk, bk=bk, kt_ap=kt_ap: e.matmul(PS[bk][:, kk * 256:(kk + 1) * 256], kt_ap, qb[:, :], start=True, stop=True),
                          reads=list(trks) + [qbT], writes=[PT[bk]])

            def emit_rest(g):
                bk = g % 4
                grp = groups[g]
                E, ET = W["E3"][g % 3], T["E3"][g % 3]
                ncol = 256 * len(grp)
                fw.op("act", lambda e: e.activation(E[:, 0:ncol], PS[bk][:, 0:ncol], AF.Exp), reads=[PT[bk]], writes=[ET])
                for kk, (kt_ap, va_ap, trks, m_ap, m_trk) in enumerate(grp):
                    if m_ap is not None:
                        ev = E[:, kk * 256:(kk + 1) * 256].rearrange("p (c n) -> p c n", c=2)
                        fw.op("dve", lambda e, ev=ev, m_ap=m_ap: e.tensor_tensor(ev, ev, m_ap.unsqueeze(1).to_broadcast([128, 2, 128]), ALU.mult),
                              reads=[ET, m_trk], writes=[ET])
                for kk, (kt_ap, va_ap, trks, m_ap, m_trk) in enumerate(grp):
                    ti = 2 * g + kk
                    for c in range(2):
                        fw.op("pe", lambda e, kk=kk, c=c, va_ap=va_ap, ti=ti: e.matmul(
                            PS[ob + c][:, 0:130], E[:, (kk * 2 + c) * 128:(kk * 2 + c + 1) * 128], va_ap, start=(ti == 0), stop=(ti == ntl - 1)),
                              reads=[ET] + list(trks), writes=[PT[ob + c]])

            emit_qk(0)
            for g in range(len(groups)):
                if g + 1 < len(groups):
                    emit_qk(g + 1)
                emit_rest(g)
            fw.op("dve", lambda e: e.reciprocal(W["rec"][:, 0:1], PS[ob][:, 128:129]), reads=[PT[ob]], writes=[T["rec"]])
            fw.op("dve", lambda e: e.reciprocal(W["rec"][:, 1:2], PS[ob + 1][:, 128:129]), reads=[PT[ob + 1]], writes=[T["rec"]])
            fw.op("dve", lambda e: e.tensor_tensor(W["rec"][:, 2:3], W["rec"][:, 1:2], W["lv"][:, 5:6], ALU.mult), reads=[T["rec"], T["lv"]], writes=[T["rec"]])
            fw.op("dve", lambda e: e.tensor_scalar(W["o"][:], PS[ob][:, 0:128], W["rec"][:, 0:1], None, ALU.mult), reads=[PT[ob], T["rec"]], writes=[T["o"]])
            fw.op("dve", lambda e: e.scalar_tensor_tensor(W["o"][:], PS[ob + 1][:, 0:128], W["rec"][:, 2:3], W["o"][:], ALU.mult, ALU.add),
                  reads=[PT[ob + 1], T["rec"], T["o"]], writes=[T["o"]])
            fw.op("dve", lambda e: e.memset(W["ss"][:, 0:1], 0.0), writes=[T["ss"]])
            fw.op("act", lambda e: e.activation(W["junk"][:], W["o"][:], AF.Square, accum_out=W["ss"][:, 0:1]), reads=[T["o"]], writes=[T["junk"], T["ss"]])
            fw.op("act", lambda e: e.activation(W["ss"][:, 1:2], W["ss"][:, 0:1], AF.Sqrt, bias=C["cst"][:, 3:4], scale=1.0),
                  reads=[T["ss"], CT["cst"]], writes=[T["ss"]])
            fw.op("dve", lambda e: e.reciprocal(W["ss"][:, 2:3], W["ss"][:, 1:2]), reads=[T["ss"]], writes=[T["ss"]])
            on, onT = W["on"][n % 2], T["on"][n % 2]
            fw.op("dve", lambda e: e.scalar_tensor_tensor(on[:], W["o"][:], W["ss"][:, 2:3], W["gsub"][:], ALU.mult, ALU.mult),
                  reads=[T["o"], T["ss"], T["gsub"]], writes=[onT])
            fw.dma("sp", out_ap, on[:], reads=[onT], writes=[out_trk])

        if LV < 3:
            return
        for h in range(4):
            fw.dma("sp", W["kt"][:], S["KT"][h], reads=[tS["KT"]], writes=[T["kt"]])
            for t0 in range(0, NT, 16):
                t1 = min(NT, t0 + 16)
                fw.dma("sp", W["va"][:, t0:t1, :], S["VA"][h, t0:t1].rearrange("t p n -> p t n"), reads=[tS["VA"]], writes=[T["va"]])
            for i in range(NOWN):
                tiles = []
                for kt in range(4 * i + 4):
                    r = kt - 4 * i
                    tiles.append((W["kt"][:, kt * 128:(kt + 1) * 128], W["va"][:, kt, :], [T["kt"], T["va"]],
                                  W["amp"][:, r, :] if r >= 0 else None, T["amp"]))
                attn(lambda c, h=h, i=i: W["QT"][64 * c:64 * c + 64, h, i * 128:(i + 1) * 128], T["QT"], tiles,
                     S["o_own"][i][:, h * 128:(h + 1) * 128], tS["o_own"])
        if LV < 4:
            return
        for h in range(4):
            fw.dma("sp", W["ktn"][:], S["KT_s"][h], reads=[tS["KT_s"]], writes=[T["ktn"]])
            fw.dma("sp", W["van"][:], S["VA_s"][h], reads=[tS["VA_s"]], writes=[T["van"]])
            tiles = []
            for s_ in range(4):
                kts, ktsT = W["kts"][s_], T["kts"][s_]
                vas, vasT = W["vas"][s_], T["vas"][s_]
                fw.dma("pool", kts[:, 0:self.PAST], I["ck_T"][s_, h], writes=[ktsT])
                fw.dma("pool", vas[:, :, 0:128], I["cv"][s_][:, h * 128:(h + 1) * 128].rearrange("(t p) d -> p t d", p=128), writes=[vasT])
                stl = [(kts[:, kt * 128:(kt + 1) * 128], vas[:, kt, :], [ktsT, vasT], W["ams"][:, s_, :], T["ams"]) for kt in range(NKP)]
                tiles.extend(stl)
            tiles.append((W["ktn"][:], W["van"][:], [T["ktn"], T["van"]], W["ams"][:, 4, :], T["ams"]))
            attn(lambda c, h=h: W["QTs"][64 * c:64 * c + 64, h, :], T["QTs"], tiles, S["o_smp"][:, h * 128:(h + 1) * 128], tS["o_smp"])

    def _phase3(self):
        fw, I, O, S, C, CT, tS = self.fw, self.I, self.O, self.S, self.C, self.CT, self.tS
        PS, PT = self.PS, self.PT
        NOWN = self.NOWN
        W, T = {}, {}
        AXX = mybir.AxisListType.X

        def mk(name, shape, dt=F32, n=1):
            if n == 1:
                W[name] = fw.sb("p3_" + name, shape, dt); T[name] = Trk(name)
            else:
                W[name] = [fw.sb(f"p3_{name}{i}", shape, dt) for i in range(n)]
                T[name] = [Trk(f"{name}{i}") for i in range(n)]

        for gname in ("g_memq", "g_ffn", "g_final", "g_memkv"):
            mk(gname, [128, D])
            fw.dma("sp", W[gname][:], I[gname], writes=[T[gname]])
            fw.op("dve", lambda e, gname=gname: e.tensor_scalar(W[gname][:], W[gname][:], math.sqrt(D), None, ALU.mult),
                  reads=[T[gname]], writes=[T[gname]])
        mk("wbuf", [128, 8, 1024], BF16, 2)
        mk("keysT", [128, 16, 128], BF16)
        fw.dma("pool", W["keysT"][:], I["keysT"], writes=[T["keysT"]])
        mk("iota", [128, 128])
        fw.dma("sp", W["iota"][:], I["iota"], writes=[T["iota"]])
        mk("mkT", [128, 8, 256], BF16)
        mk("vam", [128, 2, 4, 257], BF16)
        mk("big2", [128, 16640], BF16)
        mk("xres", [128, D], F32, 2)
        mk("tmpA", [128, D], F32)
        mk("hb", [128, D], BF16)
        mk("hT", [128, 8, 256], BF16)
        mk("junk", [128, D], F32)
        mk("ss", [128, 4])
        mk("qmT", [128, 8, 128], BF16)
        mk("em", [128, 8, 128], BF16, 4)
        mk("rec", [128, 4])
        mk("qyT", [128, 16, 256], BF16)
        mk("big", [128, 2048], F32)
        mk("scw", [128, 2048], F32)
        mk("tv", [128, 16, 16]); mk("ti", [128, 16, 16], U32); mk("tif", [128, 16, 16])
        mk("sv", [128, 8, 16]); mk("svx", [128, 8, 16]); mk("si", [128, 8, 16], U32); mk("sif", [128, 8, 16])
        mk("aidx", [128, 8, 16]); mk("bidx", [128, 8, 16]); mk("sm", [128, 8, 2])
        mk("ai", [128, 8, 16], U32); mk("bi", [128, 8, 16], U32)
        mk("ijg", [128, 3, 128])
        mk("ijgT", [128, 3, 256])
        mk("oic", [128, 16, 64], BF16, 2); mk("ojc", [128, 16, 128], BF16, 2)
        mk("uT", [128, 2, 8, 128], BF16, 3); mk("vv", [128, 2, D], BF16, 3)
        mk("ga", [128, 256], F32, 2); mk("ptb", [128, 256], BF16, 2)
        mk("yo", [128, D], F32)
        WT = W["big2"][:, 0:16384].rearrange("p (t i) -> p t i", i=64)
        skT = W["big2"][:, 0:8192].rearrange("p (s c m) -> p s c m", s=4, c=8)
        sVA = W["big2"][:, 8192:8192 + 8224].rearrange("p (s t h n) -> p s t h n", s=4, t=2, h=4)
        for i in range(4):
            fw.op("pool", lambda e, i=i: e.memset(W["em"][i][:], 0.0), writes=[T["em"][i]])
        fw.op("pool", lambda e: e.memset(W["vam"][:, :, :, 256:257], 1.0), writes=[T["vam"]])

        psb = PS[0][:].bitcast(BF16)

        def to_hT(src_ap, src_trk, col0):
            for k in range(8):
                fw.op("pe", lambda e, k=k: e.transpose(psb[:, k * 128:(k + 1) * 128], src_ap[:, k * 128:(k + 1) * 128], C["identb"][:]),
                      reads=[src_trk, CT["identb"]], writes=[PT[0]])
            fw.op("act", lambda e: e.copy(W["hT"][:, :, col0:col0 + 128], psb[:, :].rearrange("p (k n) -> p k n", k=8)), reads=[PT[0]], writes=[T["hT"]])

        def rms_to_hT(x_ap, x_trk, gname, col0):
            fw.op("dve", lambda e: e.memset(W["ss"][:, 0:1], 0.0), writes=[T["ss"]])
            fw.op("act", lambda e: e.activation(W["junk"][:], x_ap, AF.Square, accum_out=W["ss"][:, 0:1]), reads=[x_trk], writes=[T["junk"], T["ss"]])
            fw.op("act", lambda e: e.activation(W["ss"][:, 1:2], W["ss"][:, 0:1], AF.Sqrt, bias=C["cst"][:, 1:2], scale=1.0),
                  reads=[T["ss"], CT["cst"]], writes=[T["ss"]])
            fw.op("dve", lambda e: e.reciprocal(W["ss"][:, 2:3], W["ss"][:, 1:2]), reads=[T["ss"]], writes=[T["ss"]])
            fw.op("dve", lambda e: e.scalar_tensor_tensor(W["hb"][:], x_ap, W["ss"][:, 2:3], W[gname][:], ALU.mult, ALU.mult),
                  reads=[x_trk, T["ss"], T[gname]], writes=[T["hb"]])
            to_hT(W["hb"], T["hb"], col0)

        from collections import deque
        wq = deque()
        wloaded = deque()
        wcnt = [0]

        def w_prefetch():
            if not wq:
                return
            name, c0 = wq.popleft()
            buf, trk = W["wbuf"][wcnt[0] % 2], T["wbuf"][wcnt[0] % 2]
            wcnt[0] += 1
            for k in range(8):
                fw.dma("pool", buf[:, k, :], I[name][k * 128:(k + 1) * 128, c0:c0 + 1024], writes=[trk])
            wloaded.append((buf, trk))

        cur_w = [None, None]

        def load_w(name, c0=0, ncols=1024):
            cur_w[0], cur_w[1] = wloaded.popleft()
            w_prefetch()

        def proj_tm(col0, banks=(1, 2)):
            for nh in range(2):
                for k in range(8):
                    fw.op("pe", lambda e, k=k, nh=nh: e.matmul(PS[banks[nh]][:], W["hT"][:, k, col0:col0 + 128], cur_w[0][:, k, nh * 512:(nh + 1) * 512],
                                                               start=(k == 0), stop=(k == 7)), reads=[T["hT"], cur_w[1]], writes=[PT[banks[nh]]])

        wq.extend([("w_mk", 0), ("w_mv", 0)])
        nblk = (NOWN + 1) // 2 + 1
        for _ in range(nblk):
            wq.extend([("w_out", 0), ("w_mq", 0), ("w_mo", 0), ("w_pq", 0), ("w_pq", 1024)])
        w_prefetch()
        for mt in range(2):
            fw.dma("sp", W["tmpA"][:], I["mem_p"][mt * 128:(mt + 1) * 128, :], writes=[T["tmpA"]])
            rms_to_hT(W["tmpA"][:], T["tmpA"], "g_memkv", mt * 128)
        for which, outn in (("w_mk", "memk_p"), ("w_mv", "memv_p")):
            load_w(which)
            for mt in range(2):
                proj_tm(mt * 128)
                for nh in range(2):
                    fw.op("act", lambda e, nh=nh: e.copy(W["yo"][:, nh * 512:(nh + 1) * 512], PS[1 + nh][:]), reads=[PT[1 + nh]], writes=[T["yo"]])
                fw.dma("sp", O[outn][mt * 128:(mt + 1) * 128, :], W["yo"][:], reads=[T["yo"]], is_output=True)
                if which == "w_mv":
                    fw.op("dve", lambda e, mt=mt: e.tensor_copy(W["vam"][:, mt, :, 0:256], W["yo"][:].rearrange("p (h d) -> p h d", h=4)),
                          reads=[T["yo"]], writes=[T["vam"]])
            if which == "w_mk":
                for oc in range(8):
                    bk = 3 + oc % 2
                    for k in range(8):
                        fw.op("pe", lambda e, k=k, oc=oc, bk=bk: e.matmul(PS[bk][:, 0:256], cur_w[0][:, k, oc * 128:(oc + 1) * 128], W["hT"][:, k, 0:256],
                                                                          start=(k == 0), stop=(k == 7)), reads=[T["hT"], cur_w[1]], writes=[PT[bk]])
                    fw.op("act", lambda e, oc=oc, bk=bk: e.copy(W["mkT"][:, oc, :], PS[bk][:, 0:256]), reads=[PT[bk]], writes=[T["mkT"]])

        def block(tiles, smp):
            nt = len(tiles)
            Tn = nt * 128
            load_w("w_out")
            for ti, tl in enumerate(tiles):
                xr, xrT = W["xres"][ti], T["xres"][ti]
                if smp:
                    fw.dma("sp", xr[:], I["x_smp"], writes=[xrT])
                    fw.dma("sp", W["tmpA"][:, 0:512], S["yn_smp"], reads=[tS["yn_smp"]], writes=[T["tmpA"]])
                    fw.dma("sp", W["tmpA"][:, 512:1024], S["o_smp"], reads=[tS["o_smp"]], writes=[T["tmpA"]])
                else:
                    fw.dma("sp", xr[:], I["x_own"][tl * 128:(tl + 1) * 128, :], writes=[xrT])
                    fw.dma("sp", W["tmpA"][:, 0:512], S["yn_own"][tl], reads=[tS["yn_own"]], writes=[T["tmpA"]])
                    fw.dma("sp", W["tmpA"][:, 512:1024], S["o_own"][tl], reads=[tS["o_own"]], writes=[T["tmpA"]])
                fw.op("act", lambda e: e.copy(W["hb"][:], W["tmpA"][:]), reads=[T["tmpA"]], writes=[T["hb"]])
                to_hT(W["hb"], T["hb"], ti * 128)
                proj_tm(ti * 128)
                for nh in range(2):
                    fw.op("dve", lambda e, nh=nh: e.tensor_tensor(xr[:, nh * 512:(nh + 1) * 512], xr[:, nh * 512:(nh + 1) * 512], PS[1 + nh][:], ALU.add),
                          reads=[xrT, PT[1 + nh]], writes=[xrT])
                if self.debug:
                    dst = S["x1_smp"] if smp else S["x1_own"][tl]
                    fw.dma("sp", dst, xr[:], reads=[xrT], writes=[tS["x1_smp" if smp else "x1_own"]])
            load_w("w_mq")
            if smp:
                for s_ in range(4):
                    fw.dma("pool", skT[:, s_], I["cmk_T"][s_].rearrange("h k p m -> p (h k) m"), writes=[T["big2"]])
                    for mt in range(2):
                        fw.dma("pool", sVA[:, s_, mt, :, 0:256], I["cmv"][s_][mt * 128:(mt + 1) * 128, :].rearrange("p (h d) -> p h d", h=4), writes=[T["big2"]])
                fw.op("pool", lambda e: e.memset(sVA[:, :, :, :, 256:257], 1.0), writes=[T["big2"]])
            groups = [(s_, 32 * s_, 32) for s_ in range(4)] if smp else [(0, 0, 128)]
            if smp:
                for i in range(4):
                    fw.op("pool", lambda e, i=i: e.memset(W["em"][i][:], 0.0), writes=[T["em"][i]])
            for ti, tl in enumerate(tiles):
                xr, xrT = W["xres"][ti], T["xres"][ti]
                rms_to_hT(xr[:], xrT, "g_memq", ti * 128)
            for ti, tl in enumerate(tiles):
                for oc in range(8):
                    bk = 3 + oc // 4
                    for k in range(8):
                        fw.op("pe", lambda e, k=k, oc=oc, bk=bk: e.matmul(PS[bk][:, (oc % 4) * 128:(oc % 4 + 1) * 128], cur_w[0][:, k, oc * 128:(oc + 1) * 128],
                                                                          W["hT"][:, k, ti * 128:(ti + 1) * 128], start=(k == 0), stop=(k == 7)),
                              reads=[T["hT"], cur_w[1]], writes=[PT[bk]])
                for hf in range(2):
                    fw.op("act", lambda e, hf=hf: e.mul(W["qmT"][:, 4 * hf:4 * hf + 4, :].rearrange("p c n -> p (c n)"), PS[3 + hf][:], 0.0625),
                          reads=[PT[3 + hf]], writes=[T["qmT"]])
                for (gs, c0, cn) in groups:
                    for h in range(4):
                        for mt in range(2):
                            hm = h * 2 + mt
                            bk = 5 + hm // 4
                            for dk in range(2):
                                kt_ap = skT[:, gs, h * 2 + dk, mt * 128:(mt + 1) * 128] if smp else W["mkT"][:, h * 2 + dk, mt * 128:(mt + 1) * 128]
                                fw.op("pe", lambda e, hm=hm, bk=bk, dk=dk, kt_ap=kt_ap, c0=c0, cn=cn, h=h: e.matmul(
                                    PS[bk][:, (hm % 4) * 128 + c0:(hm % 4) * 128 + c0 + cn], kt_ap, W["qmT"][:, h * 2 + dk, c0:c0 + cn],
                                    start=(dk == 0), stop=(dk == 1)), reads=[T["big2"] if smp else T["mkT"], T["qmT"]], writes=[PT[bk]])
                for (gs, c0, cn) in groups:
                    em, emT = W["em"][gs], T["em"][gs]
                    for hf in range(2):
                        fw.op("act", lambda e, hf=hf, em=em, c0=c0, cn=cn: e.activation(
                            em[:, 4 * hf:4 * hf + 4, c0:c0 + cn], PS[5 + hf][:].rearrange("p (c n) -> p c n", c=4)[:, :, c0:c0 + cn], AF.Exp),
                              reads=[PT[5 + hf]], writes=[emT])
                for h in range(4):
                    n_acc = len(groups) * 2
                    a = 0
                    for (gs, c0, cn) in groups:
                        for mt in range(2):
                            va_ap = sVA[:, gs, mt, h, :] if smp else W["vam"][:, mt, h, :]
                            fw.op("pe", lambda e, h=h, gs=gs, mt=mt, va_ap=va_ap, a=a, n_acc=n_acc: e.matmul(
                                PS[1 + h][:, 0:257], W["em"][gs][:, h * 2 + mt, :], va_ap, start=(a == 0), stop=(a == n_acc - 1)),
                                  reads=[T["em"][gs], T["big2"] if smp else T["vam"]], writes=[PT[1 + h]])
                            a += 1
                for h in range(4):
                    fw.op("dve", lambda e, h=h: e.reciprocal(W["rec"][:, h:h + 1], PS[1 + h][:, 256:257]), reads=[PT[1 + h]], writes=[T["rec"]])
                    fw.op("dve", lambda e, h=h: e.tensor_scalar(W["hb"][:, h * 256:(h + 1) * 256], PS[1 + h][:, 0:256], W["rec"][:, h:h + 1], None, ALU.mult),
                          reads=[PT[1 + h], T["rec"]], writes=[T["hb"]])
                to_hT(W["hb"], T["hb"], ti * 128)
            load_w("w_mo")
            for ti, tl in enumerate(tiles):
                xr, xrT = W["xres"][ti], T["xres"][ti]
                proj_tm(ti * 128)
                for nh in range(2):
                    fw.op("dve", lambda e, nh=nh: e.tensor_tensor(xr[:, nh * 512:(nh + 1) * 512], xr[:, nh * 512:(nh + 1) * 512], PS[1 + nh][:], ALU.add),
                          reads=[xrT, PT[1 + nh]], writes=[xrT])
                if self.debug:
                    dst = S["x2_smp"] if smp else S["x2_own"][tl]
                    fw.dma("sp", dst, xr[:], reads=[xrT], writes=[tS["x2_smp" if smp else "x2_own"]])
            for ti, tl in enumerate(tiles):
                rms_to_hT(W["xres"][ti][:], T["xres"][ti], "g_ffn", ti * 128)
            for half in range(2):
                load_w("w_pq", half * 1024, 1024)
                for hc8 in range(8):
                    hc = half * 8 + hc8
                    bk = 1 + hc % 2
                    for k in range(8):
                        fw.op("pe", lambda e, k=k, hc8=hc8, bk=bk: e.matmul(PS[bk][:, 0:Tn], cur_w[0][:, k, hc8 * 128:(hc8 + 1) * 128], W["hT"][:, k, 0:Tn],
                                                                            start=(k == 0), stop=(k == 7)), reads=[T["hT"], cur_w[1]], writes=[PT[bk]])
                    fw.op("act", lambda e, hc=hc, bk=bk: e.copy(W["qyT"][:, hc, 0:Tn], PS[bk][:, 0:Tn]), reads=[PT[bk]], writes=[T["qyT"]])
            sc = W["big"][:].rearrange("p (c n) -> p c n", c=16)
            comb = W["big"][:].rearrange("p (h a b) -> p h a b", h=8, a=16)
            for ti, tl in enumerate(tiles):
                for hc in range(16):
                    bk = 3 + hc // 4
                    fw.op("pe", lambda e, hc=hc, bk=bk: e.matmul(PS[bk][:, (hc % 4) * 128:(hc % 4 + 1) * 128], W["qyT"][:, hc, ti * 128:(ti + 1) * 128], W["keysT"][:, hc, :],
                                                                 start=True, stop=True), reads=[T["qyT"], T["keysT"]], writes=[PT[bk]])
                for q4 in range(4):
                    fw.op("act", lambda e, q4=q4: e.copy(W["big"][:, q4 * 512:(q4 + 1) * 512], PS[3 + q4][:]), reads=[PT[3 + q4]], writes=[T["big"]])

                def top16_multi(srcs, src_trk, vals, idxs, v_trks, i_trks, w_trks):
                    G = len(srcs)
                    n = srcs[0].shape[-1]
                    wk = [W["scw"][:, g * n:(g + 1) * n] for g in range(G)]
                    for g in range(G):
                        fw.op("dve", lambda e, g=g: e.max(out=vals[g][:, 0:8], in_=srcs[g]), reads=[src_trk], writes=[v_trks[g]])
                    for g in range(G):
                        fw.op("dve", lambda e, g=g: e.max_index(out=idxs[g][:, 0:8], in_max=vals[g][:, 0:8], in_values=srcs[g]),
                              reads=[src_trk, v_trks[g]], writes=[i_trks[g]])
                    for g in range(G):
                        fw.op("dve", lambda e, g=g: e.match_replace(out=wk[g], in_to_replace=vals[g][:, 0:8], in_values=srcs[g], imm_value=-1e30),
                              reads=[src_trk, v_trks[g]], writes=[w_trks[g]])
                    for g in range(G):
                        fw.op("dve", lambda e, g=g: e.max(out=vals[g][:, 8:16], in_=wk[g]), reads=[w_trks[g]], writes=[v_trks[g]])
                    for g in range(G):
                        fw.op("dve", lambda e, g=g: e.max_index(out=idxs[g][:, 8:16], in_max=vals[g][:, 8:16], in_values=wk[g]),
                              reads=[w_trks[g], v_trks[g]], writes=[i_trks[g]])

                tvT = [Trk(f"tv{g}") for g in range(16)]; tiT = [Trk(f"ti{g}") for g in range(16)]; wkT = [Trk(f"wk{g}") for g in range(16)]
                for g in range(16):
                    tvT[g].w, tvT[g].r = T["tv"].w, list(T["tv"].r)
                    tiT[g].w, tiT[g].r = T["ti"].w, list(T["ti"].r)
                    wkT[g].w, wkT[g].r = T["scw"].w, list(T["scw"].r)
                top16_multi([sc[:, hc, :] for hc in range(16)], T["big"], [W["tv"][:, hc, :] for hc in range(16)],
                            [W["ti"][:, hc, :] for hc in range(16)], tvT, tiT, wkT)
                fw.op("dve", lambda e: e.tensor_copy(W["tif"][:], W["ti"][:]), reads=tiT, writes=[T["tif"]])
                fw.op("dve", lambda e: e.memset(W["ss"][:, 3:4], 0.0), reads=tvT + tiT + wkT, writes=[T["tv"], T["ti"], T["scw"]])
                tv4 = W["tv"][:].rearrange("p (h c) a -> p h c a", c=2)
                tif4 = W["tif"][:].rearrange("p (h c) a -> p h c a", c=2)
                fw.op("dve", lambda e: e.tensor_tensor(comb, tv4[:, :, 0, :].unsqueeze(3).to_broadcast([128, 8, 16, 16]),
                                                       tv4[:, :, 1, :].unsqueeze(2).to_broadcast([128, 8, 16, 16]), ALU.add), reads=[T["tv"]], writes=[T["big"]])
                svT = [Trk(f"sv{g}") for g in range(8)]; siT = [Trk(f"si{g}") for g in range(8)]; wk2T = [Trk(f"wkb{g}") for g in range(8)]
                for g in range(8):
                    svT[g].w, svT[g].r = T["sv"].w, list(T["sv"].r)
                    siT[g].w, siT[g].r = T["si"].w, list(T["si"].r)
                    wk2T[g].w, wk2T[g].r = T["scw"].w, list(T["scw"].r)
                top16_multi([comb[:, h].rearrange("p a b -> p (a b)") for h in range(8)], T["big"], [W["sv"][:, h, :] for h in range(8)],
                            [W["si"][:, h, :] for h in range(8)], svT, siT, wk2T)
                fw.op("dve", lambda e: e.memset(W["ss"][:, 3:4], 0.0), reads=svT + siT + wk2T, writes=[T["sv"], T["si"], T["scw"]])
                fw.op("dve", lambda e: e.tensor_scalar(W["bi"][:], W["si"][:], 15, None, ALU.bitwise_and), reads=[T["si"]], writes=[T["bi"]])
                fw.op("dve", lambda e: e.tensor_scalar(W["ai"][:], W["si"][:], 4, None, ALU.logical_shift_right), reads=[T["si"]], writes=[T["ai"]])
                fw.op("dve", lambda e: e.tensor_copy(W["bidx"][:], W["bi"][:]), reads=[T["bi"]], writes=[T["bidx"]])
                fw.op("dve", lambda e: e.tensor_copy(W["aidx"][:], W["ai"][:]), reads=[T["ai"]], writes=[T["aidx"]])
                io16 = W["iota"][:, 0:16].unsqueeze(1).unsqueeze(1).to_broadcast([128, 8, 16, 16])
                for q, (ix, cc) in enumerate((("aidx", 0), ("bidx", 1))):
                    fw.op("dve", lambda e, ix=ix: e.tensor_tensor(comb, io16, W[ix][:].unsqueeze(3).to_broadcast([128, 8, 16, 16]), ALU.is_equal),
                          reads=[T["iota"], T[ix]], writes=[T["big"]])
                    fw.op("dve", lambda e, cc=cc: e.tensor_tensor(comb, comb, tif4[:, :, cc, :].unsqueeze(2).to_broadcast([128, 8, 16, 16]), ALU.mult),
                          reads=[T["big"], T["tif"]], writes=[T["big"]])
                    fw.op("dve", lambda e, q=q: e.reduce_sum(W["ijg"][:, q, :], comb.rearrange("p h k a -> p (h k) a"), AXX), reads=[T["big"]], writes=[T["ijg"]])
                fw.op("dve", lambda e: e.tensor_tensor(W["svx"][:], W["sv"][:], W["sv"][:, :, 0:1].to_broadcast([128, 8, 16]), ALU.subtract),
                      reads=[T["sv"]], writes=[T["svx"]])
                fw.op("act", lambda e: e.activation(W["svx"][:], W["svx"][:], AF.Exp), reads=[T["svx"]], writes=[T["svx"]])
                fw.op("dve", lambda e: e.reduce_sum(W["sm"][:, :, 0], W["svx"][:], AXX), reads=[T["svx"]], writes=[T["sm"]])
                fw.op("dve", lambda e: e.reciprocal(W["sm"][:, :, 1], W["sm"][:, :, 0]), reads=[T["sm"]], writes=[T["sm"]])
                fw.op("dve", lambda e: e.tensor_tensor(W["ijg"][:, 2, :].rearrange("p (h k) -> p h k", h=8), W["svx"][:], W["sm"][:, :, 1:2].to_broadcast([128, 8, 16]), ALU.mult),
                      reads=[T["svx"], T["sm"]], writes=[T["ijg"]])
                for q in range(3):
                    fw.op("pe", lambda e, q=q: e.transpose(PS[7][:, q * 128:(q + 1) * 128], W["ijg"][:, q, :], C["ident"][:]), reads=[T["ijg"], CT["ident"]], writes=[PT[7]])
                fw.op("act", lambda e: e.copy(W["ijgT"][:, :, ti * 128:(ti + 1) * 128], PS[7][:, 0:384].rearrange("p (q n) -> p q n", q=3)),
                      reads=[PT[7]], writes=[T["ijgT"]])
            nchunk = 0
            for half in range(2):
                for t0 in range(0, Tn, 16):
                    cb_ = (t0 // 16) % 2
                    oic, oicT = W["oic"][cb_], T["oic"][cb_]
                    ojc, ojcT = W["ojc"][cb_], T["ojc"][cb_]
                    fw.op("dve", lambda e, oic=oic, t0=t0: e.tensor_tensor(
                        oic[:], W["iota"][:, half * 64:(half + 1) * 64].unsqueeze(1).to_broadcast([128, 16, 64]),
                        W["ijgT"][:, 0, t0:t0 + 16].unsqueeze(2).to_broadcast([128, 16, 64]), ALU.is_equal), reads=[T["iota"], T["ijgT"]], writes=[oicT])
                    fw.op("dve", lambda e, oic=oic, t0=t0: e.tensor_tensor(
                        oic[:], oic[:], W["ijgT"][:, 2, t0:t0 + 16].unsqueeze(2).to_broadcast([128, 16, 64]), ALU.mult), reads=[oicT, T["ijgT"]], writes=[oicT])
                    fw.op("dve", lambda e, ojc=ojc, t0=t0: e.tensor_tensor(
                        ojc[:], W["iota"][:].unsqueeze(1).to_broadcast([128, 16, 128]),
                        W["ijgT"][:, 1, t0:t0 + 16].unsqueeze(2).to_broadcast([128, 16, 128]), ALU.is_equal), reads=[T["iota"], T["ijgT"]], writes=[ojcT])
                    for t8 in range(2):
                        bk = 5 + ((t0 // 8) + t8) % 2
                        for tt in range(8):
                            tq = t8 * 8 + tt
                            fw.op("pe", lambda e, oic=oic, ojc=ojc, tt=tt, tq=tq, bk=bk: e.matmul(PS[bk][:, tt * 64:(tt + 1) * 64], ojc[:, tq, :], oic[:, tq, :], start=True, stop=True),
                                  reads=[oicT, ojcT], writes=[PT[bk]])
                        ts = t0 + t8 * 8
                        ev_eng = "act"
                        fw.op(ev_eng, lambda e, ts=ts, bk=bk, ev_eng=ev_eng: (e.copy if ev_eng == "act" else e.tensor_copy)(
                            WT[:, ts:ts + 8, :], PS[bk][:].rearrange("p (t i) -> p t i", t=8)), reads=[PT[bk]], writes=[T["big2"]])
                def emit_dma(ig):
                    sbn = (ig // 2) % 3
                    fw.dma("pool", W["uT"][sbn][:], I["peer_uT"][ig:ig + 2].rearrange("i p k e -> p i k e"), writes=[T["uT"][sbn]])
                    fw.dma("pool", W["vv"][sbn][:], I["peer_v"][ig * 128:(ig + 2) * 128, :].rearrange("(i p) d -> p i d", p=128), writes=[T["vv"][sbn]])

                def emit_A(i):
                    sbn, ii, bk = (i // 2) % 3, i % 2, 5 + i % 2
                    for k in range(8):
                        fw.op("pe", lambda e, k=k: e.matmul(PS[bk][:, 0:Tn], W["uT"][sbn][:, ii, k, :], W["hT"][:, k, 0:Tn], start=(k == 0), stop=(k == 7)),
                              reads=[T["uT"][sbn], T["hT"]], writes=[PT[bk]])

                def emit_rest(i):
                    sbn, ii, bk = (i // 2) % 3, i % 2, 5 + i % 2
                    ga, gaT = W["ga"][i % 2], T["ga"][i % 2]
                    ptb, ptbT = W["ptb"][i % 2], T["ptb"][i % 2]
                    fw.op("act", lambda e: e.activation(ga[:, 0:Tn], PS[bk][:, 0:Tn], AF.Gelu), reads=[PT[bk]], writes=[gaT])
                    fw.op("dve", lambda e: e.tensor_tensor(ptb[:, 0:Tn], ga[:, 0:Tn], WT[:, 0:Tn, i - half * 64], ALU.mult),
                          reads=[gaT, T["big2"]], writes=[ptbT])
                    for ti in range(nt):
                        for nh in range(2):
                            fw.op("pe", lambda e, ti=ti, nh=nh: e.matmul(PS[1 + 2 * ti + nh][:], ptb[:, ti * 128:(ti + 1) * 128], W["vv"][sbn][:, ii, nh * 512:(nh + 1) * 512],
                                                                         start=(i == 0), stop=(i == 127)),
                                  reads=[ptbT, T["vv"][sbn]], writes=[PT[1 + 2 * ti + nh]])

                i_lo, i_hi = half * 64, half * 64 + 64
                emit_dma(i_lo)
                emit_dma(i_lo + 2)
                emit_A(i_lo)
                for i in range(i_lo, i_hi):
                    if i + 1 < i_hi:
                        if (i + 1) % 2 == 0 and i + 3 < i_hi:
                            emit_dma(i + 3)
                        emit_A(i + 1)
                    emit_rest(i)
            for ti, tl in enumerate(tiles):
                xr, xrT = W["xres"][ti], T["xres"][ti]
                for nh in range(2):
                    fw.op("dve", lambda e, nh=nh, ti=ti: e.tensor_tensor(xr[:, nh * 512:(nh + 1) * 512], xr[:, nh * 512:(nh + 1) * 512], PS[1 + 2 * ti + nh][:], ALU.add),
                          reads=[xrT, PT[1 + 2 * ti + nh]], writes=[xrT])
                if self.debug and smp:
                    fw.dma("sp", S["x3_smp"], xr[:], reads=[xrT], writes=[tS["x3_smp"]])
                fw.op("dve", lambda e: e.memset(W["ss"][:, 0:1], 0.0), writes=[T["ss"]])
                fw.op("act", lambda e: e.activation(W["junk"][:], xr[:], AF.Square, accum_out=W["ss"][:, 0:1]), reads=[xrT], writes=[T["junk"], T["ss"]])
                fw.op("act", lambda e: e.activation(W["ss"][:, 1:2], W["ss"][:, 0:1], AF.Sqrt, bias=C["cst"][:, 1:2], scale=1.0),
                      reads=[T["ss"], CT["cst"]], writes=[T["ss"]])
                fw.op("dve", lambda e: e.reciprocal(W["ss"][:, 2:3], W["ss"][:, 1:2]), reads=[T["ss"]], writes=[T["ss"]])
                fw.op("dve", lambda e: e.scalar_tensor_tensor(W["yo"][:], xr[:], W["ss"][:, 2:3], W["g_final"][:], ALU.mult, ALU.mult),
                      reads=[xrT, T["ss"], T["g_final"]], writes=[T["yo"]])
                dst = O["y_smp"] if smp else O["y_own"][tl * 128:(tl + 1) * 128, :]
                fw.dma("sp", dst, W["yo"][:], reads=[T["yo"]], is_output=True)

        for b0 in range(0, NOWN, 2):
            block(list(range(b0, min(NOWN, b0 + 2))), False)
        block([0], True)


def _chunk_consts(L):
    nch = 128 // L
    idx = np.arange(128)
    same = (idx[:, None] // L) == (idx[None, :] // L)
    tri = (same & (idx[:, None] <= idx[None, :])).astype(np.float32)
    u = (same & (idx[:, None] > idx[None, :])).astype(np.float32)
    onb = same.astype(np.float32)
    sel = np.zeros((128, nch, 128), np.float32)
    rm = np.zeros((128, nch), np.float32)
    cm = np.zeros((128, nch, 128), np.float32)
    for ch in range(nch):
        sel[ch * L:(ch + 1) * L, ch, :] = 1.0
        rm[ch * L:(ch + 1) * L, ch] = 1.0
        cm[:, ch, ch * L:(ch + 1) * L] = 1.0
    return tri, u, onb, sel, rm, cm


def _rope_tables(pos):
    half = 8
    inv = (1.0 / (np.float32(500000.0) ** (np.arange(half, dtype=np.float32) / np.float32(half)))).astype(np.float32)
    ang = (pos.astype(np.float32)[:, None] * inv[None, :]).astype(np.float32)
    return np.cos(ang.astype(np.float64)).astype(np.float32), np.sin(ang.astype(np.float64)).astype(np.float32)


_PROG_CACHE = {}


def _get_prog(SEQ, PAST, stages, debug=False):
    key = (SEQ, PAST, stages, debug)
    if key not in _PROG_CACHE:
        p = Prog(SEQ, PAST, stages, debug)
        p.build()
        _PROG_CACHE[key] = p
    return _PROG_CACHE[key]


def kernel(_stages=3, _debug=False, **inp):
    f32 = np.float32
    g = lambda n: np.asarray(inp[n], dtype=f32)
    x_prompt, x_sample = g("x_prompt"), g("x_sample")
    SEQ = x_prompt.shape[1]
    PAST = inp["cache_attn_k"].shape[2]
    NT, NOWN = SEQ // 128, SEQ // 512
    prog = _get_prog(SEQ, PAST, _stages, _debug)

    bc = lambda v, n=128: np.ascontiguousarray(np.broadcast_to(np.asarray(v, f32).reshape(1, -1), (n, np.asarray(v).size)))
    shared = {}
    shared["w_in"] = g("w_in")[0]
    for n in ("w_out", "w_mq", "w_mk", "w_mv", "w_mo", "w_pq"):
        shared[n] = g(n)[0]
    shared["keysT"] = np.ascontiguousarray(g("peer_keys")[0].reshape(16, 128, 128).transpose(2, 0, 1))
    pu = g("peer_u")[0]
    shared["peer_uT"] = np.ascontiguousarray(pu.reshape(128, 128, 8, 128).transpose(0, 3, 2, 1))
    shared["peer_v"] = g("peer_v")[0]
    shared["g_mix"] = bc(g("g_mix")[0]); shared["g_memq"] = bc(g("g_mem_q")[0]); shared["g_memkv"] = bc(g("g_mem_kv")[0])
    shared["g_ffn"] = bc(g("g_ffn")[0]); shared["g_final"] = bc(g("g_final"))
    shared["g_ssd"] = bc(g("g_ssd")[0]); shared["g_subln"] = bc(g("g_subln")[0])
    shared["dt_bias"] = bc(g("dt_bias")[0]); shared["a_log"] = bc(g("a_log")[0]); shared["d_skip"] = bc(g("d_skip")[0])
    lam = np.stack([g("lam_q1")[0], g("lam_k1")[0], g("lam_q2")[0], g("lam_k2")[0]], 0)
    shared["lam"] = np.ascontiguousarray(np.broadcast_to(lam[None], (128, 4, 64)))
    shared["conv_wT"] = np.ascontiguousarray(g("conv_w")[0].reshape(4, 8, 128).transpose(2, 1, 0))
    shared["conv_bT"] = np.ascontiguousarray(g("conv_b")[0].reshape(8, 128).T)
    shared["ident"] = np.eye(128, dtype=f32)
    for L in (64, 32):
        tri, u, onb, sel, rm, cm = _chunk_consts(L)
        shared[f"tri{L}"], shared[f"u{L}"], shared[f"onb{L}"] = tri, u, onb
        shared[f"sel{L}"], shared[f"rm{L}"], shared[f"cm{L}"] = sel, rm, cm
    cp, sp_ = _rope_tables(np.arange(SEQ))
    shared["cos_p"] = np.ascontiguousarray(cp.reshape(NT, 128, 8).transpose(1, 0, 2))
    shared["sin_p"] = np.ascontiguousarray(sp_.reshape(NT, 128, 8).transpose(1, 0, 2))
    cs, ss = _rope_tables(PAST + (np.arange(128) % 32))
    shared["cos_s"], shared["sin_s"] = cs, ss
    shared["iota"] = np.ascontiguousarray(np.broadcast_to(np.arange(128, dtype=f32)[None], (128, 128)))
    idx = np.arange(128)
    am_s = np.zeros((128, 5, 128), f32)
    for s in range(4):
        am_s[:, s, 32 * s:32 * (s + 1)] = 1.0
    am_s[:, 4, :] = ((idx[:, None] // 32) == (idx[None, :] // 32)).astype(f32)
    shared["amask_s"] = am_s

    in_maps = []
    for c in range(NCORES):
        b, j = c // 4, c % 4
        m = dict(shared)
        m["x_all"] = x_prompt[b]
        m["x_own"] = np.ascontiguousarray(x_prompt[b].reshape(NOWN, 4, 128, D)[:, j].reshape(NOWN * 128, D))
        m["x_smp"] = np.ascontiguousarray(x_sample[4 * c:4 * c + 4].reshape(128, D))
        m["mem_p"] = g("mem_prompt")[b]
        am = np.zeros((128, 4, 128), f32)
        for r in range(4):
            if r < j:
                am[:, r, :] = 1.0
            elif r == j:
                am[:, r, :] = ((idx[:, None] // 64) <= (idx[None, :] // 64)).astype(f32)
        m["amask_p"] = am
        sj = np.zeros((128, 4), f32); sj[:, j] = 1.0
        m["selj"] = sj
        sl = slice(4 * c, 4 * c + 4)
        ck = g("cache_attn_k")[0, sl]
        m["ck_T"] = np.ascontiguousarray(ck.transpose(0, 2, 3, 1))
        m["cv"] = np.ascontiguousarray(g("cache_attn_v")[0, sl].reshape(4, PAST, 512))
        cmk = g("cache_mem_k")[0, sl]
        m["cmk_T"] = np.ascontiguousarray(cmk.reshape(4, 256, 4, 2, 128).transpose(0, 2, 3, 4, 1))
        m["cmv"] = np.ascontiguousarray(g("cache_mem_v")[0, sl].reshape(4, 256, D))
        st = g("state_ssm")[0, sl]
        m["ssm_T"] = np.ascontiguousarray(st.transpose(0, 3, 1, 2).reshape(4, 128, 512))
        cv_ = g("state_conv")[0, sl]
        m["conv_T"] = np.ascontiguousarray(cv_.reshape(4, 3, 8, 128).transpose(3, 2, 0, 1))
        in_maps.append({k: np.ascontiguousarray(v, dtype=f32) for k, v in m.items() if k in prog.in_shapes})

    res = run_bass_kernel_spmd(prog.nc, in_maps, core_ids=list(range(NCORES)))
    R = res.results
    if _debug:
        kernel.last_results = R
    B = x_prompt.shape[0]
    y_prompt = np.zeros((B, SEQ, D), f32)
    for c in range(NCORES):
        b, j = c // 4, c % 4
        y_prompt[b].reshape(NOWN, 4, 128, D)[:, j] = R[c]["y_own"].reshape(NOWN, 128, D)
    y_sample = np.concatenate([R[c]["y_smp"].reshape(4, 32, D) for c in range(NCORES)], 0)
    newk_p = np.stack([R[4 * b]["newk"].reshape(SEQ, 4, 128) for b in range(B)], 0)[None]
    newv_p = np.stack([R[4 * b]["newv"].reshape(SEQ, 4, 128) for b in range(B)], 0)[None]
    ssm_p = np.stack([R[4 * b]["ssm_p"].reshape(8, 64, 128) for b in range(B)], 0)[None]
    conv_p = np.stack([R[4 * b]["conv_p"].reshape(128, 8, 3).transpose(2, 1, 0).reshape(3, D) for b in range(B)], 0)[None]
    memk_p = np.stack([R[4 * b]["memk_p"].reshape(256, 4, 256) for b in range(B)], 0)[None]
    memv_p = np.stack([R[4 * b]["memv_p"].reshape(256, 4, 256) for b in range(B)], 0)[None]
    newk_s = np.concatenate([R[c]["newk_s"].reshape(4, 32, 4, 128) for c in range(NCORES)], 0)[None]
    newv_s = np.concatenate([R[c]["newv_s"].reshape(4, 32, 4, 128) for c in range(NCORES)], 0)[None]
    ssm_s = np.concatenate([R[c]["ssm_s"].reshape(4, 8, 64, 128) for c in range(NCORES)], 0)[None]
    conv_s = np.concatenate([R[c]["conv_s"].reshape(128, 8, 4, 3).transpose(2, 3, 1, 0).reshape(4, 3, D) for c in range(NCORES)], 0)[None]
    return (y_prompt, y_sample, newk_p, newv_p, ssm_p, conv_p, memk_p, memv_p, newk_s, newv_s, ssm_s, conv_s)
```

```python
import math
from contextlib import ExitStack

import numpy as np
import concourse.bass as bass
import concourse.mybir as mybir
from concourse.bass_utils import run_bass_kernel_spmd

F32 = mybir.dt.float32
BF16 = mybir.dt.bfloat16
U32 = mybir.dt.uint32
AF = mybir.ActivationFunctionType
ALU = mybir.AluOpType

D = 1024
IN_DIM = 3080
EPS = 1e-6
NCORES = 8
C_Z, C_XBC, C_DT, C_Q, C_K, C_V = 0, 512, 1536, 1544, 2056, 2568
LAMBDA_INIT = 0.8 - 0.6 * math.exp(0.0)


class Trk:
    __slots__ = ("w", "r", "name")

    def __init__(self, name=""):
        self.w = None
        self.r = []
        self.name = name


class FW:
    SEM_CAP = 6000
    N_DMA_SEMS = 16

    def __init__(self, nc, stack):
        self.nc = nc
        self.stack = stack
        self.eng = {"pe": nc.tensor, "dve": nc.vector, "act": nc.scalar, "pool": nc.gpsimd, "sp": nc.sync}
        self._keep = []
        self.csem = {}
        self.ccnt = {}
        self.nsem = 0
        for e in ("pe", "dve", "act", "pool"):
            self._new_csem(e)
        self.dsem = {}
        for q in ("sp", "pool"):
            self.dsem[q] = [[self._sem(f"d{q}{i}"), 0] for i in range(self.N_DMA_SEMS)]
        self.dnext = {"sp": 0, "pool": 0}
        self.waited = {e: {} for e in self.eng}
        self.out_tokens = []
        self.n_inst = 0
        self.cur_stack = None

    def _sem(self, name):
        self.nsem += 1
        h = self.stack.enter_context(self.nc.semaphore(f"{name}_{self.nsem}"))
        self._keep.append(h)
        return h

    def _new_csem(self, e):
        self.csem[e] = self._sem(f"c{e}")
        self.ccnt[e] = 0

    def _wait(self, e, toks, defer=False):
        need = {}
        for t in toks:
            if t is None:
                continue
            sem, val, src = t
            if src == "pe" and e == "pe":
                continue
            k = id(sem)
            if k not in need or need[k][1] < val:
                need[k] = (sem, val)
        todo = [(k, sem, val) for k, (sem, val) in need.items() if self.waited[e].get(k, 0) < val]
        held = None
        if defer and todo:
            held = todo.pop()
        for k, sem, val in todo:
            self.eng[e].wait_ge(sem, val)
            self.waited[e][k] = val
            self.n_inst += 1
        if held is not None:
            k, sem, val = held
            self.waited[e][k] = val
            return (sem, val)
        return None

    @staticmethod
    def _deps(reads, writes):
        toks = []
        for b in reads:
            toks.append(b.w)
        for b in writes:
            toks.append(b.w)
            toks.extend(b.r)
        return toks

    @staticmethod
    def _commit(tok, reads, writes):
        for b in reads:
            if b not in writes:
                b.r.append(tok)
                if len(b.r) > 64:
                    b.r = b.r[-64:]
        for b in writes:
            b.w = tok
            b.r = []

    def op(self, e, fn, reads=(), writes=()):
        reads = [b for b in reads if b is not None]
        writes = [b for b in writes if b is not None]
        held = self._wait(e, self._deps(reads, writes), defer=True)
        if self.ccnt[e] >= self.SEM_CAP:
            self._new_csem(e)
        n0 = self.nc.n_instructions()
        ins = fn(self.eng[e])
        assert self.nc.n_instructions() - n0 == 1, "multi-instruction op: cannot attach wait"
        if held is not None:
            ins._wait_ge(held[0], held[1])
        self.ccnt[e] += 1
        ins.then_inc(self.csem[e], 1)
        tok = (self.csem[e], self.ccnt[e], e)
        self._commit(tok, reads, writes)
        self.n_inst += 1
        return tok

    def dma(self, q, out, in_, reads=(), writes=(), is_output=False, **kw):
        reads = [b for b in reads if b is not None]
        writes = [b for b in writes if b is not None]
        slot = self.dsem[q][self.dnext[q]]
        self.dnext[q] = (self.dnext[q] + 1) % self.N_DMA_SEMS
        if slot[1] >= self.SEM_CAP:
            slot[0] = self._sem(f"d{q}")
            slot[1] = 0
        toks = self._deps(reads, writes)
        if slot[1] > 0:
            toks.append((slot[0], slot[1], "dma"))
        held = self._wait(q, toks, defer=True)
        n0 = self.nc.n_instructions()
        ins = self.eng[q].dma_start(out=out, in_=in_, **kw)
        if held is not None:
            if self.nc.n_instructions() - n0 == 1:
                ins._wait_ge(held[0], held[1])
            else:
                raise AssertionError("multi-instruction dma")
        slot[1] += 16
        ins.then_inc(slot[0], 16)
        tok = (slot[0], slot[1], "dma")
        self._commit(tok, reads, writes)
        if is_output:
            self.out_tokens.append(tok)
        self.n_inst += 1
        return tok

    def finish(self):
        toks = list(self.out_tokens)
        for q in self.dsem:
            for slot in self.dsem[q]:
                if slot[1] > 0:
                    toks.append((slot[0], slot[1], "dma"))
        self._wait("sp", toks)

    def sb(self, name, shape, dtype=F32):
        st = self.cur_stack if self.cur_stack is not None else self.stack
        return st.enter_context(self.nc.sbuf_tensor(name, list(shape), dtype))

    def barrier(self):
        toks = []
        for e in ("pe", "dve", "act", "pool"):
            if self.ccnt[e] > 0:
                toks.append((self.csem[e], self.ccnt[e], "x"))
        for q in self.dsem:
            for slot in self.dsem[q]:
                if slot[1] > 0:
                    toks.append((slot[0], slot[1], "dma"))
        for e in ("pe", "dve", "act", "pool", "sp"):
            self._wait(e, toks)

    def ps(self, name, shape, dtype=F32):
        return self.stack.enter_context(self.nc.psum_tensor(name, list(shape), dtype))


class Prog:
    def __init__(self, SEQ, PAST, stages=3, debug=False):
        self.debug = debug
        self.SEQ, self.PAST = SEQ, PAST
        self.NT = SEQ // 128
        self.NOWN = self.NT // 4
        self.NKP = PAST // 128
        self.stages = stages
        self.nc = bass.Bass("TRN2", target_bir_lowering=False)
        self.in_shapes = {}
        self.out_shapes = {}

    def din(self, name, shape, dt=F32):
        self.in_shapes[name] = (tuple(shape), dt)
        return self.nc.dram_tensor(name, list(shape), dt, kind="ExternalInput").ap()

    def dout(self, name, shape, dt=F32):
        self.out_shapes[name] = (tuple(shape), dt)
        return self.nc.dram_tensor(name, list(shape), dt, kind="ExternalOutput").ap()

    def dscr(self, name, shape, dt=F32):
        if self.debug and dt == F32:
            return self.dout("dbg_" + name, shape, dt)
        return self.nc.dram_tensor(name, list(shape), dt, kind="Internal").ap()

    def build(self):
        with ExitStack() as st:
            self.fw = FW(self.nc, st)
            self._declare()
            self._setup()
            for ph, fn in ((1, self._phase1), (2, self._phase2), (3, self._phase3)):
                if self.stages >= ph:
                    with ExitStack() as pst:
                        self.fw.cur_stack = pst
                        fn()
                        self.fw.barrier()
                    self.fw.cur_stack = None
            self.fw.finish()
        return self.nc

    def _declare(self):
        SEQ, PAST, NT, NOWN = self.SEQ, self.PAST, self.NT, self.NOWN
        di, do, ds = self.din, self.dout, self.dscr
        I = self.I = {}
        O = self.O = {}
        S = self.S = {}
        I["x_all"] = di("x_all", [SEQ, D])
        I["x_own"] = di("x_own", [NOWN * 128, D])
        I["x_smp"] = di("x_smp", [128, D])
        I["mem_p"] = di("mem_p", [256, D])
        I["w_in"] = di("w_in", [D, IN_DIM])
        for n in ("w_out", "w_mq", "w_mk", "w_mv", "w_mo"):
            I[n] = di(n, [D, D])
        I["w_pq"] = di("w_pq", [D, 2048])
        if self.stages >= 3:
            I["keysT"] = di("keysT", [128, 16, 128])
            I["peer_uT"] = di("peer_uT", [128, 128, 8, 128])
            I["peer_v"] = di("peer_v", [16384, D])
        for n in ("g_mix", "g_memq", "g_memkv", "g_ffn", "g_final"):
            I[n] = di(n, [128, D])
        I["g_ssd"] = di("g_ssd", [128, 512])
        I["g_subln"] = di("g_subln", [128, 128])
        I["dt_bias"] = di("dt_bias", [128, 8])
        I["a_log"] = di("a_log", [128, 8])
        I["d_skip"] = di("d_skip", [128, 8])
        I["lam"] = di("lam", [128, 4, 64])
        I["conv_wT"] = di("conv_wT", [128, 8, 4])
        I["conv_bT"] = di("conv_bT", [128, 8])
        I["ident"] = di("ident", [128, 128])
        for L, nch in ((64, 2), (32, 4)):
            I[f"tri{L}"] = di(f"tri{L}", [128, 128])
            I[f"u{L}"] = di(f"u{L}", [128, 128])
            I[f"onb{L}"] = di(f"onb{L}", [128, 128])
            I[f"sel{L}"] = di(f"sel{L}", [128, nch, 128])
            I[f"rm{L}"] = di(f"rm{L}", [128, nch])
            I[f"cm{L}"] = di(f"cm{L}", [128, nch, 128])
        I["cos_p"] = di("cos_p", [128, NT, 8])
        I["sin_p"] = di("sin_p", [128, NT, 8])
        I["cos_s"] = di("cos_s", [128, 8])
        I["sin_s"] = di("sin_s", [128, 8])
        I["amask_p"] = di("amask_p", [128, 4, 128])
        I["amask_s"] = di("amask_s", [128, 5, 128])
        I["selj"] = di("selj", [128, 4])
        I["iota"] = di("iota", [128, 128])
        I["ck_T"] = di("ck_T", [4, 4, 128, PAST])
        I["cv"] = di("cv", [4, PAST, 512])
        I["cmk_T"] = di("cmk_T", [4, 4, 2, 128, 256])
        I["cmv"] = di("cmv", [4, 256, D])
        I["ssm_T"] = di("ssm_T", [4, 128, 512])
        I["conv_T"] = di("conv_T", [128, 8, 4, 3])

        O["y_own"] = do("y_own", [NOWN * 128, D])
        O["y_smp"] = do("y_smp", [128, D])
        O["newk"] = do("newk", [SEQ, 512])
        O["newv"] = do("newv", [SEQ, 512])
        O["ssm_p"] = do("ssm_p", [512, 128])
        O["conv_p"] = do("conv_p", [128, 8, 1, 3])
        O["memk_p"] = do("memk_p", [256, D])
        O["memv_p"] = do("memv_p", [256, D])
        O["newk_s"] = do("newk_s", [128, 512])
        O["newv_s"] = do("newv_s", [128, 512])
        O["ssm_s"] = do("ssm_s", [4, 512, 128])
        O["conv_s"] = do("conv_s", [128, 8, 4, 3])

        S["KT"] = ds("KT", [4, 128, SEQ], BF16)
        S["VA"] = ds("VA", [4, NT, 128, 130], BF16)
        S["q_own"] = ds("q_own", [NOWN, 128, 512])
        S["yn_own"] = ds("yn_own", [NOWN, 128, 512])
        S["o_own"] = ds("o_own", [NOWN, 128, 512])
        S["q_smp"] = ds("q_smp", [128, 512])
        S["yn_smp"] = ds("yn_smp", [128, 512])
        S["o_smp"] = ds("o_smp", [128, 512])
        if self.debug:
            for nm, shp in (("x1_own", [NOWN, 128, D]), ("x2_own", [NOWN, 128, D]), ("x1_smp", [128, D]), ("x2_smp", [128, D]), ("x3_smp", [128, D])):
                S[nm] = ds(nm, shp)
        S["KT_s"] = ds("KT_s", [4, 128, 128], BF16)
        S["VA_s"] = ds("VA_s", [4, 128, 130], BF16)
        self.tS = {k: Trk("scr_" + k) for k in S}

    def _load_const(self, name, shape, dt=F32, q="sp", src=None):
        t = self.fw.sb("c_" + name, shape, dt)
        trk = Trk("c_" + name)
        src = self.I[name] if src is None else src
        self.fw.dma(q, t[:], src, writes=[trk])
        return t, trk

    def _setup(self):
        fw, I = self.fw, self.I
        C = self.C = {}
        CT = self.CT = {}

        def ld(name, shape, dt=F32, q="sp"):
            C[name], CT[name] = self._load_const(name, shape, dt, q)

        ld("ident", [128, 128])
        C["identb"], CT["identb"] = self._load_const("identb", [128, 128], BF16, "pool", src=I["ident"])
        cst = C["cst"] = fw.sb("cst", [128, 8])
        CT["cst"] = Trk("cst")
        vals = [1.0, D * EPS, 512 * EPS, 128 * EPS, 0.0, EPS, 0.0, 0.0]
        for i, v in enumerate(vals):
            fw.op("dve", lambda e, i=i, v=v: e.memset(cst[:, i:i + 1], v), writes=[CT["cst"]])
        self.PS = [fw.ps(f"ps{i}", [128, 512]) for i in range(8)]
        self.PT = [Trk(f"ps{i}") for i in range(8)]

    def _phase1(self):
        fw, I, O, S, C, CT = self.fw, self.I, self.O, self.S, self.C, self.CT
        PS, PT = self.PS, self.PT
        sb = fw.sb
        def ld(name, shape, dt=F32, q="sp"):
            C[name], CT[name] = self._load_const(name, shape, dt, q)

        for L, nch in ((64, 2), (32, 4)):
            ld(f"tri{L}", [128, 128]); ld(f"u{L}", [128, 128]); ld(f"onb{L}", [128, 128])
            ld(f"sel{L}", [128, nch, 128]); ld(f"rm{L}", [128, nch]); ld(f"cm{L}", [128, nch, 128])
        ld("g_mix", [128, D]); ld("g_ssd", [128, 512])
        ld("dt_bias", [128, 8]); ld("a_log", [128, 8]); ld("d_skip", [128, 8])
        ld("conv_wT", [128, 8, 4]); ld("conv_bT", [128, 8])
        ld("cos_p", [128, self.NT, 8]); ld("sin_p", [128, self.NT, 8]); ld("cos_s", [128, 8]); ld("sin_s", [128, 8])
        ld("selj", [128, 4])
        a_t = C["a"] = fw.sb("a_neg", [128, 8]); CT["a"] = Trk("a")
        fw.op("act", lambda e: e.activation(a_t[:], C["a_log"][:], AF.Exp), reads=[CT["a_log"]], writes=[CT["a"]])
        fw.op("dve", lambda e: e.tensor_scalar(a_t[:], a_t[:], -1.0, None, ALU.mult), reads=[CT["a"]], writes=[CT["a"]])
        fw.op("dve", lambda e: e.tensor_scalar(C["g_mix"][:], C["g_mix"][:], math.sqrt(D), None, ALU.mult),
              reads=[CT["g_mix"]], writes=[CT["g_mix"]])
        fw.op("dve", lambda e: e.tensor_scalar(C["g_ssd"][:], C["g_ssd"][:], math.sqrt(512.0), None, ALU.mult),
              reads=[CT["g_ssd"]], writes=[CT["g_ssd"]])
        dsk = C["dskb"] = fw.sb("dskb", [128, 8, 64]); CT["dskb"] = Trk("dskb")
        fw.op("dve", lambda e: e.tensor_copy(dsk[:], C["d_skip"][:].unsqueeze(2).to_broadcast([128, 8, 64])),
              reads=[CT["d_skip"]], writes=[CT["dskb"]])
        w = C["w_in"] = fw.sb("w_in_sb", [128, 8, IN_DIM], BF16); CT["w_in"] = Trk("w_in")
        for k in range(8):
            fw.dma("pool", w[:, k, :], I["w_in"][k * 128:(k + 1) * 128, :], writes=[CT["w_in"]])
        W = self.W1 = {}
        T = self.T1 = {}

        def mk(name, shape, dt=F32, n=1):
            if n == 1:
                W[name] = sb("p1_" + name, shape, dt); T[name] = Trk(name)
            else:
                W[name] = [sb(f"p1_{name}{i}", shape, dt) for i in range(n)]
                T[name] = [Trk(f"{name}{i}") for i in range(n)]

        mk("xt", [128, D], F32, 2)
        mk("junk", [128, D], F32)
        mk("ss", [128, 4], F32)
        mk("hb", [128, D], BF16)
        mk("hT", [128, 8, 128], BF16)
        mk("cbuf", [128, 8, 140], F32)
        mk("cacc", [128, 8, 128], F32)
        mk("xc", [128, 8, 128], F32)
        mk("bct", [128, 4, 128], BF16, 2)
        mk("ctm", [128, 4, 2, 128], BF16, 2)
        mk("xs", [128, 512], F32, 2)
        mk("btm", [128, 256], BF16, 2)
        mk("dt", [128, 8], F32, 2)
        mk("da", [128, 8], F32)
        mk("ex", [128, 48], F32)
        mk("wch", [128, 4, 8], F32)
        mk("xdt", [128, 8, 64], BF16)
        mk("xdte", [128, 4, 512], BF16)
        mk("cbm", [128, 2, 128], F32)
        mk("dau", [128, 8, 128], F32)
        mk("dec", [128, 8, 128], F32)
        mk("mt", [128, 8, 128], BF16)
        mk("sst", [128, 8, 64], F32, 2)
        mk("sbf", [128, 4, 512], BF16)
        mk("stmp", [128, 8, 64], F32)
        mk("ytmp", [128, 8, 64], F32)
        mk("y", [128, 512], F32)
        mk("zs", [128, 512], F32, 2)
        mk("yn", [128, 512], F32)
        mk("q", [128, 512], F32, 2)
        mk("k", [128, 512], F32)
        mk("v", [128, 512], F32)
        mk("rt", [128, 4, 8, 8], F32)
        mk("ktt", [128, 4, 128], BF16)
        mk("va", [128, 4, 130], BF16, 2)
        mk("ownq", [128, 512], F32)
        mk("ownyn", [128, 512], F32)
        mk("h0", [128, 4, 512], F32)
        mk("sout", [128, 128], F32)
        mk("ptmp", [128, 512], F32)
        self.caccT = [Trk(f"cacc{i}") for i in range(8)]
        mk("rt2", [128, 4, 8, 8], F32)
        self.rtT = {nm: [Trk(f"rt{nm}{i}") for i in range(4)] for nm in ("q", "k")}
        self.xdteT = [Trk(f"xdte{i}") for i in range(4)]
        self.decT = [Trk(f"dec{i}") for i in range(2)]
        self.mtT = [Trk(f"mt{i}") for i in range(2)]
        mk("ss2", [128, 4], F32)
        mk("junk2", [128, 512], F32)
        for i in range(2):
            fw.op("pool", lambda e, i=i: e.memset(W["va"][i][:, :, 128:130], 1.0), writes=[T["va"][i]])
        fw.op("dve", lambda e: e.memset(W["sst"][0][:], 0.0), writes=[T["sst"][0]])
        fw.op("dve", lambda e: e.memset(W["cbuf"][:], 0.0), writes=[T["cbuf"]])

        self.s_cur = 0
        self._mixer_A(0, "p")
        for t in range(self.NT):
            if t + 1 < self.NT:
                self._mixer_A(t + 1, "p")
            self._mixer_B(t, "p")
        self._state_out(W["sst"][self.s_cur], T["sst"][self.s_cur], O["ssm_p"])
        cb_v = W["cbuf"][:, :, 0:131].rearrange("p k (s l) -> p k s l", s=1)
        fw.dma("sp", O["conv_p"], cb_v[:, :, :, 0:3], reads=[T["cbuf"]], is_output=True)
        self._mixer_tile(0, "s")

    def _state_out(self, s_ap, s_trk, out_ap):
        fw, C, CT, PS, PT, W, T = self.fw, self.C, self.CT, self.PS, self.PT, self.W1, self.T1
        sv = s_ap.rearrange("p h d -> p (h d)")
        for c4 in range(4):
            fw.op("pe", lambda e, c4=c4: e.transpose(PS[7][:, c4 * 128:(c4 + 1) * 128], sv[:, c4 * 128:(c4 + 1) * 128], C["ident"][:]),
                  reads=[s_trk, CT["ident"]], writes=[PT[7]])
        for c4 in range(4):
            fw.op("act", lambda e, c4=c4: e.copy(W["sout"][:], PS[7][:, c4 * 128:(c4 + 1) * 128]), reads=[PT[7]], writes=[T["sout"]])
            fw.dma("sp", out_ap[c4 * 128:(c4 + 1) * 128, :], W["sout"][:], reads=[T["sout"]], is_output=True)

    _DB = ("dt", "xs", "btm", "bct", "ctm", "zs", "q")

    def _views(self, t, mode, part):
        par = (t if mode == "p" else self.NT) % 2
        W, T = dict(self.W1), dict(self.T1)
        for nm in self._DB:
            W[nm], T[nm] = self.W1[nm][par], self.T1[nm][par]
        if part == "B":
            W["ss"], T["ss"] = self.W1["ss2"], self.T1["ss2"]
            W["junk"], T["junk"] = self.W1["junk2"], self.T1["junk2"]
        return W, T

    def _mixer_tile(self, t, mode):
        self._mixer_A(t, mode)
        self._mixer_B(t, mode)

    def _mixer_A(self, t, mode):
        fw, I, O, S, C, CT, tS = self.fw, self.I, self.O, self.S, self.C, self.CT, self.tS
        PS, PT = self.PS, self.PT
        W, T = self._views(t, mode, "A")
        P = mode == "p"
        L = 64 if P else 32
        NCH = 2 if P else 4
        NSEG, SL = (1, 128) if P else (4, 32)
        sfx = str(L)
        xt, xtT = W["xt"][t % 2], T["xt"][t % 2]
        src = I["x_all"][t * 128:(t + 1) * 128, :] if P else I["x_smp"]
        fw.dma("sp", xt[:], src, writes=[xtT])
        fw.op("dve", lambda e: e.memset(W["ss"][:, 0:1], 0.0), writes=[T["ss"]])
        fw.op("act", lambda e: e.activation(W["junk"][:], xt[:], AF.Square, accum_out=W["ss"][:, 0:1]),
              reads=[xtT], writes=[T["junk"], T["ss"]])
        fw.op("act", lambda e: e.activation(W["ss"][:, 1:2], W["ss"][:, 0:1], AF.Ln, bias=C["cst"][:, 1:2], scale=1.0),
              reads=[T["ss"], CT["cst"]], writes=[T["ss"]])
        fw.op("act", lambda e: e.activation(W["ss"][:, 2:3], W["ss"][:, 1:2], AF.Exp, scale=-0.5), reads=[T["ss"]], writes=[T["ss"]])
        fw.op("dve", lambda e: e.scalar_tensor_tensor(W["hb"][:], xt[:], W["ss"][:, 2:3], C["g_mix"][:], ALU.mult, ALU.mult),
              reads=[xtT, T["ss"], CT["g_mix"]], writes=[T["hb"]])
        psb = PS[0][:].bitcast(BF16)
        for k in range(8):
            fw.op("pe", lambda e, k=k: e.transpose(psb[:, k * 128:(k + 1) * 128], W["hb"][:, k * 128:(k + 1) * 128], C["identb"][:]),
                  reads=[T["hb"], CT["identb"]], writes=[PT[0]])
        fw.op("act", lambda e: e.copy(W["hT"][:].rearrange("p k n -> p (k n)"), psb[:, :]), reads=[PT[0]], writes=[T["hT"]])
        wi = C["w_in"]

        def proj_tm(bank, c0, n):
            for k in range(8):
                fw.op("pe", lambda e, k=k: e.matmul(PS[bank][:, 0:n], W["hT"][:, k, :], wi[:, k, c0:c0 + n], start=(k == 0), stop=(k == 7)),
                      reads=[T["hT"], CT["w_in"]], writes=[PT[bank]])

        for ck in range(8):
            bank = 1 + ck // 4
            for k in range(8):
                fw.op("pe", lambda e, k=k, ck=ck, bank=bank: e.matmul(
                    PS[bank][:, (ck % 4) * 128:(ck % 4 + 1) * 128], wi[:, k, C_XBC + ck * 128:C_XBC + (ck + 1) * 128], W["hT"][:, k, :],
                    start=(k == 0), stop=(k == 7)), reads=[T["hT"], CT["w_in"]], writes=[PT[bank]])
        proj_tm(3, C_Z, 512)
        proj_tm(4, C_Q, 512)
        proj_tm(5, C_K, 512)
        proj_tm(6, C_V, 512)
        for k in range(8):
            fw.op("pe", lambda e, k=k: e.matmul(PS[7][:, 0:8], W["hT"][:, k, :], wi[:, k, C_DT:C_DT + 8], start=(k == 0), stop=(k == 7)),
                  reads=[T["hT"], CT["w_in"]], writes=[PT[7]])
        cb_v = W["cbuf"][:, :, 0:NSEG * (SL + 3)].rearrange("p k (s l) -> p k s l", s=NSEG)
        if not P:
            fw.dma("sp", cb_v[:, :, :, 0:3], I["conv_T"], writes=[T["cbuf"]])
        for hb2 in range(2):
            fw.op("act", lambda e, hb2=hb2: e.copy(
                cb_v[:, 4 * hb2:4 * hb2 + 4, :, 3:3 + SL],
                PS[1 + hb2][:].rearrange("p (k s l) -> p k s l", k=4, s=NSEG)), reads=[PT[1 + hb2]], writes=[T["cbuf"]])
        acc_v = W["cacc"][:].rearrange("p k (s l) -> p k s l", s=NSEG)
        cT = self.caccT
        for ck in range(8):
            fw.op("dve", lambda e, ck=ck: e.tensor_scalar(acc_v[:, ck], cb_v[:, ck, :, 0:SL], C["conv_wT"][:, ck, 0:1], C["conv_bT"][:, ck:ck + 1],
                                                          ALU.mult, ALU.add),
                  reads=[T["cbuf"], CT["conv_wT"], CT["conv_bT"]], writes=[cT[ck]])
        for j in range(1, 4):
            for ck in range(8):
                fw.op("dve", lambda e, ck=ck, j=j: e.scalar_tensor_tensor(acc_v[:, ck], cb_v[:, ck, :, j:j + SL], C["conv_wT"][:, ck, j:j + 1],
                                                                         acc_v[:, ck], ALU.mult, ALU.add),
                      reads=[T["cbuf"], CT["conv_wT"], cT[ck]], writes=[cT[ck]])
        fw.op("act", lambda e: e.activation(W["xc"][:], W["cacc"][:], AF.Silu), reads=list(cT), writes=[T["xc"]])
        fw.op("act", lambda e: e.activation(W["zs"][:], PS[3][:], AF.Silu), reads=[PT[3]], writes=[T["zs"]])
        if P:
            fw.op("dve", lambda e: e.tensor_copy(cb_v[:, :, :, 0:3], cb_v[:, :, :, SL:SL + 3]), reads=[T["cbuf"]], writes=[T["cbuf"]])
        else:
            fw.dma("sp", O["conv_s"], cb_v[:, :, :, SL:SL + 3], reads=[T["cbuf"]], is_output=True)
        fw.op("act", lambda e: e.copy(W["bct"][:], W["xc"][:, 4:8, :]), reads=[T["xc"]], writes=[T["bct"]])
        fw.op("dve", lambda e: e.tensor_tensor(W["ctm"][:, 0:NCH], W["xc"][:, 6:8, :].unsqueeze(1).to_broadcast([128, NCH, 2, 128]),
                                                C["cm" + sfx][:].unsqueeze(2).to_broadcast([128, NCH, 2, 128]), ALU.mult),
              reads=[T["xc"], CT["cm" + sfx]], writes=[T["ctm"]])
        for ck in range(6):
            bank = 1 + ck // 4
            fw.op("pe", lambda e, ck=ck, bank=bank: e.transpose(PS[bank][:, (ck % 4) * 128:(ck % 4 + 1) * 128], W["xc"][:, ck, :], C["ident"][:]),
                  reads=[T["xc"], CT["ident"]], writes=[PT[bank]])
        fw.op("act", lambda e: e.copy(W["xs"][:], PS[1][:]), reads=[PT[1]], writes=[T["xs"]])
        fw.op("act", lambda e: e.copy(W["btm"][:], PS[2][:, 0:256]), reads=[PT[2]], writes=[T["btm"]])
        fw.op("dve", lambda e: e.tensor_tensor(W["dt"][:], PS[7][:, 0:8], C["dt_bias"][:], ALU.add), reads=[PT[7], CT["dt_bias"]], writes=[T["dt"]])
        fw.op("act", lambda e: e.activation(W["dt"][:], W["dt"][:], AF.Exp), reads=[T["dt"]], writes=[T["dt"]])
        fw.op("act", lambda e: e.activation(W["dt"][:], W["dt"][:], AF.Ln, bias=C["cst"][:, 0:1], scale=1.0), reads=[T["dt"], CT["cst"]], writes=[T["dt"]])
        fw.op("act", lambda e: e.copy(W["q"][:], PS[4][:]), reads=[PT[4]], writes=[T["q"]])
        fw.op("dve", lambda e: e.tensor_copy(W["k"][:], PS[5][:]), reads=[PT[5]], writes=[T["k"]])
        fw.op("act", lambda e: e.copy(W["v"][:], PS[6][:]), reads=[PT[6]], writes=[T["v"]])
        cos = C["cos_p"][:, t, :] if P else C["cos_s"][:]
        sin = C["sin_p"][:, t, :] if P else C["sin_s"][:]
        cosb = cos.unsqueeze(1).to_broadcast([128, 8, 8])
        sinb = sin.unsqueeze(1).to_broadcast([128, 8, 8])
        tcs = CT["cos_p"] if P else CT["cos_s"]
        tsn = CT["sin_p"] if P else CT["sin_s"]
        rp = {}
        for nm, rtn in (("q", "rt"), ("k", "rt2")):
            v3 = W[nm][:].rearrange("p (g d) -> p g d", g=8)
            rp[nm] = (v3[:, :, 0:8], v3[:, :, 8:16], W[rtn], self.rtT[nm])
        for step in range(6):
            for nm in ("q", "k"):
                t1, t2, rt, rT = rp[nm]
                if step == 0:
                    fw.op("dve", lambda e, t1=t1, rt=rt: e.tensor_tensor(rt[:, 0], t1, cosb, ALU.mult), reads=[T[nm], tcs], writes=[rT[0]])
                elif step == 1:
                    fw.op("dve", lambda e, t2=t2, rt=rt: e.tensor_tensor(rt[:, 1], t2, sinb, ALU.mult), reads=[T[nm], tsn], writes=[rT[1]])
                elif step == 2:
                    fw.op("dve", lambda e, t2=t2, rt=rt: e.tensor_tensor(rt[:, 2], t2, cosb, ALU.mult), reads=[T[nm], tcs], writes=[rT[2]])
                elif step == 3:
                    fw.op("dve", lambda e, t1=t1, rt=rt: e.tensor_tensor(rt[:, 3], t1, sinb, ALU.mult), reads=[T[nm], tsn], writes=[rT[3]])
                elif step == 4:
                    fw.op("dve", lambda e, t1=t1, rt=rt: e.tensor_tensor(t1, rt[:, 0], rt[:, 1], ALU.subtract), reads=[rT[0], rT[1], rT[3]], writes=[T[nm]])
                else:
                    fw.op("dve", lambda e, t2=t2, rt=rt: e.tensor_tensor(t2, rt[:, 2], rt[:, 3], ALU.add), reads=[rT[2], rT[3], rT[1]], writes=[T[nm]])
        ko = O["newk"][t * 128:(t + 1) * 128, :] if P else O["newk_s"]
        vo = O["newv"][t * 128:(t + 1) * 128, :] if P else O["newv_s"]
        fw.dma("sp", ko, W["k"][:], reads=[T["k"]], is_output=True)
        fw.dma("sp", vo, W["v"][:], reads=[T["v"]], is_output=True)
        for h in range(4):
            fw.op("pe", lambda e, h=h: e.transpose(PS[5][:, h * 128:(h + 1) * 128], W["k"][:, h * 128:(h + 1) * 128], C["ident"][:]),
                  reads=[T["k"], CT["ident"]], writes=[PT[5]])
        fw.op("act", lambda e: e.copy(W["ktt"][:].rearrange("p h n -> p (h n)"), PS[5][:]), reads=[PT[5]], writes=[T["ktt"]])
        va, vaT = W["va"][t % 2], T["va"][t % 2]
        fw.op("act", lambda e: e.copy(va[:, :, 0:128], W["v"][:].rearrange("p (h d) -> p h d", h=4)), reads=[T["v"]], writes=[vaT])
        if P:
            fw.dma("sp", S["KT"][:, :, t * 128:(t + 1) * 128].rearrange("h p n -> p h n"), W["ktt"][:], reads=[T["ktt"]], writes=[tS["KT"]])
            fw.dma("sp", S["VA"][:, t, :, :].rearrange("h p n -> p h n"), va[:], reads=[vaT], writes=[tS["VA"]])
        else:
            fw.dma("sp", S["KT_s"].rearrange("h p n -> p h n"), W["ktt"][:], reads=[T["ktt"]], writes=[tS["KT_s"]])
            fw.dma("sp", S["VA_s"].rearrange("h p n -> p h n"), va[:], reads=[vaT], writes=[tS["VA_s"]])
    def _mixer_B(self, t, mode):
        fw, I, O, S, C, CT, tS = self.fw, self.I, self.O, self.S, self.C, self.CT, self.tS
        PS, PT = self.PS, self.PT
        W, T = self._views(t, mode, "B")
        P = mode == "p"
        L = 64 if P else 32
        NCH = 2 if P else 4
        sfx = str(L)
        xs3 = W["xs"][:].rearrange("p (h d) -> p h d", h=8)
        tri, uu, onb, sel, rm = C["tri" + sfx], C["u" + sfx], C["onb" + sfx], C["sel" + sfx], C["rm" + sfx]
        ctri, cu, conb, csel, crm = CT["tri" + sfx], CT["u" + sfx], CT["onb" + sfx], CT["sel" + sfx], CT["rm" + sfx]
        fw.op("dve", lambda e: e.tensor_tensor(W["da"][:], W["dt"][:], C["a"][:], ALU.mult), reads=[T["dt"], CT["a"]], writes=[T["da"]])
        fw.op("pe", lambda e: e.matmul(PS[7][:, 0:8], tri[:], W["da"][:], start=True, stop=True), reads=[T["da"], ctri], writes=[PT[7]])
        fw.op("pe", lambda e: e.matmul(PS[7][:, 8:16], onb[:], W["da"][:], start=True, stop=True), reads=[T["da"], conb], writes=[PT[7]])
        for ch in range(NCH):
            fw.op("pe", lambda e, ch=ch: e.matmul(PS[7][:, 16 + 8 * ch:24 + 8 * ch], sel[:, ch, :], W["da"][:], start=True, stop=True),
                  reads=[T["da"], csel], writes=[PT[7]])
        nex = 16 + 8 * NCH
        ex = W["ex"]
        fw.op("dve", lambda e: e.tensor_copy(ex[:, 0:nex], PS[7][:, 0:nex]), reads=[PT[7]], writes=[T["ex"]])
        fw.op("dve", lambda e: e.tensor_tensor(ex[:, 8:16], ex[:, 8:16], ex[:, 0:8], ALU.subtract), reads=[T["ex"]], writes=[T["ex"]])
        fw.op("act", lambda e: e.activation(ex[:, 0:nex], ex[:, 0:nex], AF.Exp), reads=[T["ex"]], writes=[T["ex"]])
        fw.op("dve", lambda e: e.tensor_tensor(W["wch"][:, 0, :], W["dt"][:], ex[:, 8:16], ALU.mult), reads=[T["dt"], T["ex"]], writes=[T["wch"]])
        for ch in range(NCH - 1, -1, -1):
            fw.op("dve", lambda e, ch=ch: e.tensor_scalar(W["wch"][:, ch, :], W["wch"][:, 0, :], rm[:, ch:ch + 1], None, ALU.mult),
                  reads=[T["wch"], crm], writes=[T["wch"]])
        xs3 = W["xs"][:].rearrange("p (h d) -> p h d", h=8)
        fw.op("dve", lambda e: e.tensor_tensor(W["xdt"][:], xs3, W["dt"][:].unsqueeze(2).to_broadcast([128, 8, 64]), ALU.mult),
              reads=[T["xs"], T["dt"]], writes=[T["xdt"]])
        for ch in range(NCH):
            eng = "dve"
            fw.op(eng, lambda e, ch=ch: e.tensor_tensor(W["xdte"][:, ch, :].rearrange("p (h d) -> p h d", h=8), xs3,
                                                        W["wch"][:, ch, :].unsqueeze(2).to_broadcast([128, 8, 64]), ALU.mult),
                  reads=[T["xs"], T["wch"]], writes=[self.xdteT[ch]])
        for g in range(2):
            fw.op("pe", lambda e, g=g: e.matmul(PS[1][:, g * 128:(g + 1) * 128], W["bct"][:, g, :], W["bct"][:, 2 + g, :], start=True, stop=True),
                  reads=[T["bct"]], writes=[PT[1]])
        fw.op("dve", lambda e: e.tensor_tensor(W["cbm"][:], PS[1][:, 0:256].rearrange("p (g n) -> p g n", g=2),
                                               tri[:].unsqueeze(1).to_broadcast([128, 2, 128]), ALU.mult), reads=[PT[1], ctri], writes=[T["cbm"]])
        fw.op("dve", lambda e: e.tensor_tensor(W["dau"][:], uu[:].unsqueeze(1).to_broadcast([128, 8, 128]),
                                                W["da"][:].unsqueeze(2).to_broadcast([128, 8, 128]), ALU.mult), reads=[cu, T["da"]], writes=[T["dau"]])
        for h in range(8):
            bank = 2 + h // 4
            fw.op("pe", lambda e, h=h, bank=bank: e.matmul(PS[bank][:, (h % 4) * 128:(h % 4 + 1) * 128], W["dau"][:, h, :], tri[:], start=True, stop=True),
                  reads=[T["dau"], ctri], writes=[PT[bank]])
        for g in range(2):
            fw.op("act", lambda e, g=g: e.activation(W["dec"][:, 4 * g:4 * g + 4, :].rearrange("p h n -> p (h n)"), PS[2 + g][:], AF.Exp),
                  reads=[PT[2 + g]], writes=[self.decT[g]])
        for g in range(2):
            fw.op("dve", lambda e, g=g: e.tensor_tensor(W["mt"][:, 4 * g:4 * g + 4, :], W["dec"][:, 4 * g:4 * g + 4, :],
                                                        W["cbm"][:, g, :].unsqueeze(1).to_broadcast([128, 4, 128]), ALU.mult),
                  reads=[self.decT[g], T["cbm"]], writes=[self.mtT[g]])
        for h in range(8):
            fw.op("pe", lambda e, h=h: e.matmul(PS[4][:, h * 64:(h + 1) * 64], W["mt"][:, h, :], W["xdt"][:, h, :], start=True, stop=True),
                  reads=[self.mtT[h // 4], T["xdt"]], writes=[PT[4]])
        if P:
            s_in, s_inT = W["sst"][self.s_cur], T["sst"][self.s_cur]
        else:
            fw.dma("sp", W["h0"][:], I["ssm_T"].rearrange("s p f -> p s f"), writes=[T["h0"]])
        for ch in range(NCH):
            if P:
                src_ap, src_trk = s_in[:], s_inT
            else:
                src_ap, src_trk = W["h0"][:, ch, :].rearrange("p (h d) -> p h d", h=8), T["h0"]
            fw.op("act", lambda e, ch=ch, src_ap=src_ap: e.copy(W["sbf"][:, ch, :].rearrange("p (h d) -> p h d", h=8), src_ap),
                  reads=[src_trk], writes=[T["sbf"]])
            for h in range(8):
                g = h // 4
                fw.op("pe", lambda e, h=h, g=g, ch=ch: e.matmul(PS[6][:, h * 64:(h + 1) * 64], W["btm"][:, g * 128:(g + 1) * 128],
                                                                W["xdte"][:, ch, h * 64:(h + 1) * 64], start=True, stop=True),
                      reads=[T["btm"], self.xdteT[ch]], writes=[PT[6]])
            cdb = ex[:, 16 + 8 * ch:24 + 8 * ch].unsqueeze(2).to_broadcast([128, 8, 64])
            fw.op("dve", lambda e, src_ap=src_ap, cdb=cdb: e.tensor_tensor(W["stmp"][:], src_ap, cdb, ALU.mult), reads=[src_trk, T["ex"]], writes=[T["stmp"]])
            if P:
                nxt = 1 - self.s_cur
                fw.op("dve", lambda e, nxt=nxt: e.tensor_tensor(W["sst"][nxt][:], W["stmp"][:], PS[6][:].rearrange("p (h d) -> p h d", h=8), ALU.add),
                      reads=[T["stmp"], PT[6]], writes=[T["sst"][nxt]])
                self.s_cur = nxt
                s_in, s_inT = W["sst"][nxt], T["sst"][nxt]
            else:
                fw.op("dve", lambda e: e.tensor_tensor(W["stmp"][:], W["stmp"][:], PS[6][:].rearrange("p (h d) -> p h d", h=8), ALU.add),
                      reads=[T["stmp"], PT[6]], writes=[T["stmp"]])
                self._state_out(W["stmp"][:], T["stmp"], O["ssm_s"][ch])
        for h in range(8):
            g = h // 4
            for ch in range(NCH):
                fw.op("pe", lambda e, h=h, g=g, ch=ch: e.matmul(PS[5][:, h * 64:(h + 1) * 64], W["ctm"][:, ch, g, :], W["sbf"][:, ch, h * 64:(h + 1) * 64],
                                                                start=(ch == 0), stop=(ch == NCH - 1)),
                      reads=[T["ctm"], T["sbf"]], writes=[PT[5]])
        y3 = W["y"][:].rearrange("p (h d) -> p h d", h=8)
        fw.op("dve", lambda e: e.tensor_tensor(W["ytmp"][:], PS[5][:].rearrange("p (h d) -> p h d", h=8),
                                               ex[:, 0:8].unsqueeze(2).to_broadcast([128, 8, 64]), ALU.mult), reads=[PT[5], T["ex"]], writes=[T["ytmp"]])
        fw.op("dve", lambda e: e.tensor_tensor(y3, W["ytmp"][:], PS[4][:].rearrange("p (h d) -> p h d", h=8), ALU.add),
              reads=[T["ytmp"], PT[4]], writes=[T["y"]])
        fw.op("dve", lambda e: e.tensor_tensor(W["ytmp"][:], xs3, C["dskb"][:], ALU.mult), reads=[T["xs"], CT["dskb"]], writes=[T["ytmp"]])
        fw.op("dve", lambda e: e.tensor_tensor(y3, y3, W["ytmp"][:], ALU.add), reads=[T["ytmp"], T["y"]], writes=[T["y"]])
        fw.op("dve", lambda e: e.tensor_tensor(W["y"][:], W["y"][:], W["zs"][:], ALU.mult), reads=[T["y"], T["zs"]], writes=[T["y"]])
        fw.op("dve", lambda e: e.memset(W["ss"][:, 0:1], 0.0), writes=[T["ss"]])
        fw.op("act", lambda e: e.activation(W["junk"][:, 0:512], W["y"][:], AF.Square, accum_out=W["ss"][:, 0:1]),
              reads=[T["y"]], writes=[T["junk"], T["ss"]])
        fw.op("act", lambda e: e.activation(W["ss"][:, 1:2], W["ss"][:, 0:1], AF.Ln, bias=C["cst"][:, 2:3], scale=1.0),
              reads=[T["ss"], CT["cst"]], writes=[T["ss"]])
        fw.op("act", lambda e: e.activation(W["ss"][:, 2:3], W["ss"][:, 1:2], AF.Exp, scale=-0.5), reads=[T["ss"]], writes=[T["ss"]])
        fw.op("dve", lambda e: e.scalar_tensor_tensor(W["yn"][:], W["y"][:], W["ss"][:, 2:3], C["g_ssd"][:], ALU.mult, ALU.mult),
              reads=[T["y"], T["ss"], CT["g_ssd"]], writes=[T["yn"]])
        if P:
            r, i = t % 4, t // 4
            sj = C["selj"]
            for src, dst in (("q", "ownq"), ("yn", "ownyn")):
                if r == 0:
                    fw.op("dve", lambda e, src=src, dst=dst: e.tensor_scalar(W[dst][:], W[src][:], sj[:, 0:1], None, ALU.mult),
                          reads=[T[src], CT["selj"]], writes=[T[dst]])
                else:
                    fw.op("dve", lambda e, src=src, dst=dst, r=r: e.scalar_tensor_tensor(W[dst][:], W[src][:], sj[:, r:r + 1], W[dst][:], ALU.mult, ALU.add),
                          reads=[T[src], CT["selj"], T[dst]], writes=[T[dst]])
            if r == 3:
                fw.dma("sp", S["q_own"][i], W["ownq"][:], reads=[T["ownq"]], writes=[tS["q_own"]])
                fw.dma("sp", S["yn_own"][i], W["ownyn"][:], reads=[T["ownyn"]], writes=[tS["yn_own"]])
        else:
            fw.dma("sp", S["q_smp"], W["q"][:], reads=[T["q"]], writes=[tS["q_smp"]])
            fw.dma("sp", S["yn_smp"], W["yn"][:], reads=[T["yn"]], writes=[tS["yn_smp"]])

    def _phase2(self):
        fw, I, O, S, C, CT, tS = self.fw, self.I, self.O, self.S, self.C, self.CT, self.tS
        PS, PT = self.PS, self.PT
        SEQ, NT, NOWN, NKP = self.SEQ, self.NT, self.NOWN, self.NKP
        W, T = {}, {}

        def mk(name, shape, dt=F32, n=1):
            if n == 1:
                W[name] = fw.sb("p2_" + name, shape, dt); T[name] = Trk(name)
            else:
                W[name] = [fw.sb(f"p2_{name}{i}", shape, dt) for i in range(n)]
                T[name] = [Trk(f"{name}{i}") for i in range(n)]

        mk("lam", [128, 4, 64]); mk("lt", [128, 2, 64]); mk("lv", [128, 8])
        mk("gsub", [128, 128])
        mk("amp", [128, 4, 128], BF16); mk("ams", [128, 5, 128], BF16)
        mk("qin", [128, 512], F32, 2)
        mk("QT", [128, 4, NOWN * 128], BF16); mk("QTs", [128, 4, 128], BF16)
        mk("kt", [128, SEQ], BF16); mk("va", [128, NT, 130], BF16)
        mk("kts", [128, max(self.PAST, 128)], BF16, 4); mk("vas", [128, NKP, 130], BF16, 4)
        mk("ktn", [128, 128], BF16); mk("van", [128, 130], BF16)
        mk("rec", [128, 4]); mk("o", [128, 128]); mk("junk", [128, 128]); mk("ss", [128, 4])
        mk("on", [128, 128], F32, 2)
        fw.dma("sp", W["lam"][:], I["lam"], writes=[T["lam"]])
        fw.dma("sp", W["gsub"][:], I["g_subln"], writes=[T["gsub"]])
        fw.dma("pool", W["amp"][:], I["amask_p"], writes=[T["amp"]])
        fw.dma("pool", W["ams"][:], I["amask_s"], writes=[T["ams"]])
        fw.op("dve", lambda e: e.tensor_scalar(W["gsub"][:], W["gsub"][:], math.sqrt(128.0) * (1.0 - LAMBDA_INIT), None, ALU.mult),
              reads=[T["gsub"]], writes=[T["gsub"]])
        for i in range(2):
            fw.op("dve", lambda e, i=i: e.tensor_tensor(W["lt"][:, i, :], W["lam"][:, 2 * i, :], W["lam"][:, 2 * i + 1, :], ALU.mult),
                  reads=[T["lam"]], writes=[T["lt"]])
            fw.op("dve", lambda e, i=i: e.reduce_sum(W["lv"][:, i:i + 1], W["lt"][:, i, :], mybir.AxisListType.X), reads=[T["lt"]], writes=[T["lv"]])
        fw.op("act", lambda e: e.activation(W["lv"][:, 2:4], W["lv"][:, 0:2], AF.Exp), reads=[T["lv"]], writes=[T["lv"]])
        fw.op("dve", lambda e: e.tensor_tensor(W["lv"][:, 4:5], W["lv"][:, 2:3], W["lv"][:, 3:4], ALU.subtract), reads=[T["lv"]], writes=[T["lv"]])
        fw.op("dve", lambda e: e.tensor_scalar(W["lv"][:, 4:5], W["lv"][:, 4:5], LAMBDA_INIT, None, ALU.add), reads=[T["lv"]], writes=[T["lv"]])
        fw.op("dve", lambda e: e.tensor_scalar(W["lv"][:, 5:6], W["lv"][:, 4:5], -1.0, None, ALU.mult), reads=[T["lv"]], writes=[T["lv"]])
        for i in range(4):
            fw.op("pool", lambda e, i=i: e.memset(W["vas"][i][:, :, 128:130], 1.0), writes=[T["vas"][i]])
        LV = 9
        if LV < 2:
            return
        def load_qT(src_ap, src_trk, dst_ap, dst_trk, n):
            qin, qinT = W["qin"][n % 2], T["qin"][n % 2]
            fw.dma("sp", qin[:], src_ap, reads=[src_trk], writes=[qinT])
            for h in range(4):
                fw.op("pe", lambda e, h=h: e.transpose(PS[6][:, h * 128:(h + 1) * 128], qin[:, h * 128:(h + 1) * 128], C["ident"][:]),
                      reads=[qinT, CT["ident"]], writes=[PT[6]])
            fw.op("act", lambda e: e.mul(dst_ap, PS[6][:].rearrange("p (h n) -> p h n", h=4), 0.125), reads=[PT[6]], writes=[dst_trk])

        for i in range(NOWN):
            load_qT(S["q_own"][i], tS["q_own"], W["QT"][:, :, i * 128:(i + 1) * 128], T["QT"], i)
        load_qT(S["q_smp"], tS["q_smp"], W["QTs"][:], T["QTs"], NOWN)

        self._acnt = 0
        mk("qblk", [128, 256], BF16, 2)
        mk("E3", [128, 512], BF16, 3)
        for i in range(2):
            fw.op("dve", lambda e, i=i: e.memset(W["qblk"][i][:], 0.0), writes=[T["qblk"][i]])

        def attn(qT_fn, qT_trk, tiles, out_ap, out_trk):
            n = self._acnt
            self._acnt += 1
            ob = 4 + 2 * (n % 2)
            qb, qbT = W["qblk"][n % 2], T["qblk"][n % 2]
            for c in range(2):
                fw.op("dve", lambda e, c=c: e.tensor_copy(qb[64 * c:64 * c + 64, 128 * c:128 * c + 128], qT_fn(c)), reads=[qT_trk], writes=[qbT])
            ntl = len(tiles)
            groups = [tiles[g0:g0 + 2] for g0 in range(0, ntl, 2)]

            def emit_qk(g):
                bk = g % 4
                for kk, (kt_ap, va_ap, trks, m_ap, m_trk) in enumerate(groups[g]):
                    fw.op("pe", lambda e, kk=kk, bk=bk, kt_ap=kt_ap: e.matmul(PS[bk][:, kk * 256:(kk + 1) * 256], kt_ap, qb[:, :], start=True, stop=True),
                          reads=list(trks) + [qbT], writes=[PT[bk]])

            def emit_rest(g):
                bk = g % 4
                grp = groups[g]
                E, ET = W["E3"][g % 3], T["E3"][g % 3]
                ncol = 256 * len(grp)
                fw.op("act", lambda e: e.activation(E[:, 0:ncol], PS[bk][:, 0:ncol], AF.Exp), reads=[PT[bk]], writes=[ET])
                for kk, (kt_ap, va_ap, trks, m_ap, m_trk) in enumerate(grp):
                    if m_ap is not None:
                        ev = E[:, kk * 256:(kk + 1) * 256].rearrange("p (c n) -> p c n", c=2)
                        fw.op("dve", lambda e, ev=ev, m_ap=m_ap: e.tensor_tensor(ev, ev, m_ap.unsqueeze(1).to_broadcast([128, 2, 128]), ALU.mult),
                              reads=[ET, m_trk], writes=[ET])
                for kk, (kt_ap, va_ap, trks, m_ap, m_trk) in enumerate(grp):
                    ti = 2 * g + kk
                    for c in range(2):
                        fw.op("pe", lambda e, kk=kk, c=c, va_ap=va_ap, ti=ti: e.matmul(
                            PS[ob + c][:, 0:130], E[:, (kk * 2 + c) * 128:(kk * 2 + c + 1) * 128], va_ap, start=(ti == 0), stop=(ti == ntl - 1)),
                              reads=[ET] + list(trks), writes=[PT[ob + c]])

            emit_qk(0)
            for g in range(len(groups)):
                if g + 1 < len(groups):
                    emit_qk(g + 1)
                emit_rest(g)
            fw.op("dve", lambda e: e.reciprocal(W["rec"][:, 0:1], PS[ob][:, 128:129]), reads=[PT[ob]], writes=[T["rec"]])
            fw.op("dve", lambda e: e.reciprocal(W["rec"][:, 1:2], PS[ob + 1][:, 128:129]), reads=[PT[ob + 1]], writes=[T["rec"]])
            fw.op("dve", lambda e: e.tensor_tensor(W["rec"][:, 2:3], W["rec"][:, 1:2], W["lv"][:, 5:6], ALU.mult), reads=[T["rec"], T["lv"]], writes=[T["rec"]])
            fw.op("dve", lambda e: e.tensor_scalar(W["o"][:], PS[ob][:, 0:128], W["rec"][:, 0:1], None, ALU.mult), reads=[PT[ob], T["rec"]], writes=[T["o"]])
            fw.op("dve", lambda e: e.scalar_tensor_tensor(W["o"][:], PS[ob + 1][:, 0:128], W["rec"][:, 2:3], W["o"][:], ALU.mult, ALU.add),
                  reads=[PT[ob + 1], T["rec"], T["o"]], writes=[T["o"]])
            fw.op("dve", lambda e: e.memset(W["ss"][:, 0:1], 0.0), writes=[T["ss"]])
            fw.op("act", lambda e: e.activation(W["junk"][:], W["o"][:], AF.Square, accum_out=W["ss"][:, 0:1]), reads=[T["o"]], writes=[T["junk"], T["ss"]])
            fw.op("act", lambda e: e.activation(W["ss"][:, 1:2], W["ss"][:, 0:1], AF.Sqrt, bias=C["cst"][:, 3:4], scale=1.0),
                  reads=[T["ss"], CT["cst"]], writes=[T["ss"]])
            fw.op("dve", lambda e: e.reciprocal(W["ss"][:, 2:3], W["ss"][:, 1:2]), reads=[T["ss"]], writes=[T["ss"]])
            on, onT = W["on"][n % 2], T["on"][n % 2]
            fw.op("dve", lambda e: e.scalar_tensor_tensor(on[:], W["o"][:], W["ss"][:, 2:3], W["gsub"][:], ALU.mult, ALU.mult),
                  reads=[T["o"], T["ss"], T["gsub"]], writes=[onT])
            fw.dma("sp", out_ap, on[:], reads=[onT], writes=[out_trk])

        if LV < 3:
            return
        for h in range(4):
            fw.dma("sp", W["kt"][:], S["KT"][h], reads=[tS["KT"]], writes=[T["kt"]])
            for t0 in range(0, NT, 16):
                t1 = min(NT, t0 + 16)
                fw.dma("sp", W["va"][:, t0:t1, :], S["VA"][h, t0:t1].rearrange("t p n -> p t n"), reads=[tS["VA"]], writes=[T["va"]])
            for i in range(NOWN):
                tiles = []
                for kt in range(4 * i + 4):
                    r = kt - 4 * i
                    tiles.append((W["kt"][:, kt * 128:(kt + 1) * 128], W["va"][:, kt, :], [T["kt"], T["va"]],
                                  W["amp"][:, r, :] if r >= 0 else None, T["amp"]))
                attn(lambda c, h=h, i=i: W["QT"][64 * c:64 * c + 64, h, i * 128:(i + 1) * 128], T["QT"], tiles,
                     S["o_own"][i][:, h * 128:(h + 1) * 128], tS["o_own"])
        if LV < 4:
            return
        for h in range(4):
            fw.dma("sp", W["ktn"][:], S["KT_s"][h], reads=[tS["KT_s"]], writes=[T["ktn"]])
            fw.dma("sp", W["van"][:], S["VA_s"][h], reads=[tS["VA_s"]], writes=[T["van"]])
            tiles = []
            for s_ in range(4):
                kts, ktsT = W["kts"][s_], T["kts"][s_]
                vas, vasT = W["vas"][s_], T["vas"][s_]
                fw.dma("pool", kts[:, 0:self.PAST], I["ck_T"][s_, h], writes=[ktsT])
                fw.dma("pool", vas[:, :, 0:128], I["cv"][s_][:, h * 128:(h + 1) * 128].rearrange("(t p) d -> p t d", p=128), writes=[vasT])
                stl = [(kts[:, kt * 128:(kt + 1) * 128], vas[:, kt, :], [ktsT, vasT], W["ams"][:, s_, :], T["ams"]) for kt in range(NKP)]
                tiles.extend(stl)
            tiles.append((W["ktn"][:], W["van"][:], [T["ktn"], T["van"]], W["ams"][:, 4, :], T["ams"]))
            attn(lambda c, h=h: W["QTs"][64 * c:64 * c + 64, h, :], T["QTs"], tiles, S["o_smp"][:, h * 128:(h + 1) * 128], tS["o_smp"])

    def _phase3(self):
        fw, I, O, S, C, CT, tS = self.fw, self.I, self.O, self.S, self.C, self.CT, self.tS
        PS, PT = self.PS, self.PT
        NOWN = self.NOWN
        W, T = {}, {}
        AXX = mybir.AxisListType.X

        def mk(name, shape, dt=F32, n=1):
            if n == 1:
                W[name] = fw.sb("p3_" + name, shape, dt); T[name] = Trk(name)
            else:
                W[name] = [fw.sb(f"p3_{name}{i}", shape, dt) for i in range(n)]
                T[name] = [Trk(f"{name}{i}") for i in range(n)]

        for gname in ("g_memq", "g_ffn", "g_final", "g_memkv"):
            mk(gname, [128, D])
            fw.dma("sp", W[gname][:], I[gname], writes=[T[gname]])
            fw.op("dve", lambda e, gname=gname: e.tensor_scalar(W[gname][:], W[gname][:], math.sqrt(D), None, ALU.mult),
                  reads=[T[gname]], writes=[T[gname]])
        mk("wbuf", [128, 8, 1024], BF16, 2)
        mk("keysT", [128, 16, 128], BF16)
        fw.dma("pool", W["keysT"][:], I["keysT"], writes=[T["keysT"]])
        mk("iota", [128, 128])
        fw.dma("sp", W["iota"][:], I["iota"], writes=[T["iota"]])
        mk("mkT", [128, 8, 256], BF16)
        mk("vam", [128, 2, 4, 257], BF16)
        mk("big2", [128, 16640], BF16)
        mk("xres", [128, D], F32, 2)
        mk("tmpA", [128, D], F32)
        mk("hb", [128, D], BF16)
        mk("hT", [128, 8, 256], BF16)
        mk("junk", [128, D], F32)
        mk("ss", [128, 4])
        mk("qmT", [128, 8, 128], BF16)
        mk("em", [128, 8, 128], BF16, 4)
        mk("rec", [128, 4])
        mk("qyT", [128, 16, 256], BF16)
        mk("big", [128, 2048], F32)
        mk("scw", [128, 2048], F32)
        mk("tv", [128, 16, 16]); mk("ti", [128, 16, 16], U32); mk("tif", [128, 16, 16])
        mk("sv", [128, 8, 16]); mk("svx", [128, 8, 16]); mk("si", [128, 8, 16], U32); mk("sif", [128, 8, 16])
        mk("aidx", [128, 8, 16]); mk("bidx", [128, 8, 16]); mk("sm", [128, 8, 2])
        mk("ai", [128, 8, 16], U32); mk("bi", [128, 8, 16], U32)
        mk("ijg", [128, 3, 128])
        mk("ijgT", [128, 3, 256])
        mk("oic", [128, 16, 64], BF16, 2); mk("ojc", [128, 16, 128], BF16, 2)
        mk("uT", [128, 2, 8, 128], BF16, 3); mk("vv", [128, 2, D], BF16, 3)
        mk("ga", [128, 256], F32, 2); mk("ptb", [128, 256], BF16, 2)
        mk("yo", [128, D], F32)
        WT = W["big2"][:, 0:16384].rearrange("p (t i) -> p t i", i=64)
        skT = W["big2"][:, 0:8192].rearrange("p (s c m) -> p s c m", s=4, c=8)
        sVA = W["big2"][:, 8192:8192 + 8224].rearrange("p (s t h n) -> p s t h n", s=4, t=2, h=4)
        for i in range(4):
            fw.op("pool", lambda e, i=i: e.memset(W["em"][i][:], 0.0), writes=[T["em"][i]])
        fw.op("pool", lambda e: e.memset(W["vam"][:, :, :, 256:257], 1.0), writes=[T["vam"]])

        psb = PS[0][:].bitcast(BF16)

        def to_hT(src_ap, src_trk, col0):
            for k in range(8):
                fw.op("pe", lambda e, k=k: e.transpose(psb[:, k * 128:(k + 1) * 128], src_ap[:, k * 128:(k + 1) * 128], C["identb"][:]),
                      reads=[src_trk, CT["identb"]], writes=[PT[0]])
            fw.op("act", lambda e: e.copy(W["hT"][:, :, col0:col0 + 128], psb[:, :].rearrange("p (k n) -> p k n", k=8)), reads=[PT[0]], writes=[T["hT"]])

        def rms_to_hT(x_ap, x_trk, gname, col0):
            fw.op("dve", lambda e: e.memset(W["ss"][:, 0:1], 0.0), writes=[T["ss"]])
            fw.op("act", lambda e: e.activation(W["junk"][:], x_ap, AF.Square, accum_out=W["ss"][:, 0:1]), reads=[x_trk], writes=[T["junk"], T["ss"]])
            fw.op("act", lambda e: e.activation(W["ss"][:, 1:2], W["ss"][:, 0:1], AF.Sqrt, bias=C["cst"][:, 1:2], scale=1.0),
                  reads=[T["ss"], CT["cst"]], writes=[T["ss"]])
            fw.op("dve", lambda e: e.reciprocal(W["ss"][:, 2:3], W["ss"][:, 1:2]), reads=[T["ss"]], writes=[T["ss"]])
            fw.op("dve", lambda e: e.scalar_tensor_tensor(W["hb"][:], x_ap, W["ss"][:, 2:3], W[gname][:], ALU.mult, ALU.mult),
                  reads=[x_trk, T["ss"], T[gname]], writes=[T["hb"]])
            to_hT(W["hb"], T["hb"], col0)

        from collections import deque
        wq = deque()
        wloaded = deque()
        wcnt = [0]

        def w_prefetch():
            if not wq:
                return
            name, c0 = wq.popleft()
            buf, trk = W["wbuf"][wcnt[0] % 2], T["wbuf"][wcnt[0] % 2]
            wcnt[0] += 1
            for k in range(8):
                fw.dma("pool", buf[:, k, :], I[name][k * 128:(k + 1) * 128, c0:c0 + 1024], writes=[trk])
            wloaded.append((buf, trk))

        cur_w = [None, None]

        def load_w(name, c0=0, ncols=1024):
            cur_w[0], cur_w[1] = wloaded.popleft()
            w_prefetch()

        def proj_tm(col0, banks=(1, 2)):
            for nh in range(2):
                for k in range(8):
                    fw.op("pe", lambda e, k=k, nh=nh: e.matmul(PS[banks[nh]][:], W["hT"][:, k, col0:col0 + 128], cur_w[0][:, k, nh * 512:(nh + 1) * 512],
                                                               start=(k == 0), stop=(k == 7)), reads=[T["hT"], cur_w[1]], writes=[PT[banks[nh]]])

        wq.extend([("w_mk", 0), ("w_mv", 0)])
        nblk = (NOWN + 1) // 2 + 1
        for _ in range(nblk):
            wq.extend([("w_out", 0), ("w_mq", 0), ("w_mo", 0), ("w_pq", 0), ("w_pq", 1024)])
        w_prefetch()
        for mt in range(2):
            fw.dma("sp", W["tmpA"][:], I["mem_p"][mt * 128:(mt + 1) * 128, :], writes=[T["tmpA"]])
            rms_to_hT(W["tmpA"][:], T["tmpA"], "g_memkv", mt * 128)
        for which, outn in (("w_mk", "memk_p"), ("w_mv", "memv_p")):
            load_w(which)
            for mt in range(2):
                proj_tm(mt * 128)
                for nh in range(2):
                    fw.op("act", lambda e, nh=nh: e.copy(W["yo"][:, nh * 512:(nh + 1) * 512], PS[1 + nh][:]), reads=[PT[1 + nh]], writes=[T["yo"]])
                fw.dma("sp", O[outn][mt * 128:(mt + 1) * 128, :], W["yo"][:], reads=[T["yo"]], is_output=True)
                if which == "w_mv":
                    fw.op("dve", lambda e, mt=mt: e.tensor_copy(W["vam"][:, mt, :, 0:256], W["yo"][:].rearrange("p (h d) -> p h d", h=4)),
                          reads=[T["yo"]], writes=[T["vam"]])
            if which == "w_mk":
                for oc in range(8):
                    bk = 3 + oc % 2
                    for k in range(8):
                        fw.op("pe", lambda e, k=k, oc=oc, bk=bk: e.matmul(PS[bk][:, 0:256], cur_w[0][:, k, oc * 128:(oc + 1) * 128], W["hT"][:, k, 0:256],
                                                                          start=(k == 0), stop=(k == 7)), reads=[T["hT"], cur_w[1]], writes=[PT[bk]])
                    fw.op("act", lambda e, oc=oc, bk=bk: e.copy(W["mkT"][:, oc, :], PS[bk][:, 0:256]), reads=[PT[bk]], writes=[T["mkT"]])

        def block(tiles, smp):
            nt = len(tiles)
            Tn = nt * 128
            load_w("w_out")
            for ti, tl in enumerate(tiles):
                xr, xrT = W["xres"][ti], T["xres"][ti]
                if smp:
                    fw.dma("sp", xr[:], I["x_smp"], writes=[xrT])
                    fw.dma("sp", W["tmpA"][:, 0:512], S["yn_smp"], reads=[tS["yn_smp"]], writes=[T["tmpA"]])
                    fw.dma("sp", W["tmpA"][:, 512:1024], S["o_smp"], reads=[tS["o_smp"]], writes=[T["tmpA"]])
                else:
                    fw.dma("sp", xr[:], I["x_own"][tl * 128:(tl + 1) * 128, :], writes=[xrT])
                    fw.dma("sp", W["tmpA"][:, 0:512], S["yn_own"][tl], reads=[tS["yn_own"]], writes=[T["tmpA"]])
                    fw.dma("sp", W["tmpA"][:, 512:1024], S["o_own"][tl], reads=[tS["o_own"]], writes=[T["tmpA"]])
                fw.op("act", lambda e: e.copy(W["hb"][:], W["tmpA"][:]), reads=[T["tmpA"]], writes=[T["hb"]])
                to_hT(W["hb"], T["hb"], ti * 128)
                proj_tm(ti * 128)
                for nh in range(2):
                    fw.op("dve", lambda e, nh=nh: e.tensor_tensor(xr[:, nh * 512:(nh + 1) * 512], xr[:, nh * 512:(nh + 1) * 512], PS[1 + nh][:], ALU.add),
                          reads=[xrT, PT[1 + nh]], writes=[xrT])
                if self.debug:
                    dst = S["x1_smp"] if smp else S["x1_own"][tl]
                    fw.dma("sp", dst, xr[:], reads=[xrT], writes=[tS["x1_smp" if smp else "x1_own"]])
            load_w("w_mq")
            if smp:
                for s_ in range(4):
                    fw.dma("pool", skT[:, s_], I["cmk_T"][s_].rearrange("h k p m -> p (h k) m"), writes=[T["big2"]])
                    for mt in range(2):
                        fw.dma("pool", sVA[:, s_, mt, :, 0:256], I["cmv"][s_][mt * 128:(mt + 1) * 128, :].rearrange("p (h d) -> p h d", h=4), writes=[T["big2"]])
                fw.op("pool", lambda e: e.memset(sVA[:, :, :, :, 256:257], 1.0), writes=[T["big2"]])
            groups = [(s_, 32 * s_, 32) for s_ in range(4)] if smp else [(0, 0, 128)]
            if smp:
                for i in range(4):
                    fw.op("pool", lambda e, i=i: e.memset(W["em"][i][:], 0.0), writes=[T["em"][i]])
            for ti, tl in enumerate(tiles):
                xr, xrT = W["xres"][ti], T["xres"][ti]
                rms_to_hT(xr[:], xrT, "g_memq", ti * 128)
            for ti, tl in enumerate(tiles):
                for oc in range(8):
                    bk = 3 + oc // 4
                    for k in range(8):
                        fw.op("pe", lambda e, k=k, oc=oc, bk=bk: e.matmul(PS[bk][:, (oc % 4) * 128:(oc % 4 + 1) * 128], cur_w[0][:, k, oc * 128:(oc + 1) * 128],
                                                                          W["hT"][:, k, ti * 128:(ti + 1) * 128], start=(k == 0), stop=(k == 7)),
                              reads=[T["hT"], cur_w[1]], writes=[PT[bk]])
                for hf in range(2):
                    fw.op("act", lambda e, hf=hf: e.mul(W["qmT"][:, 4 * hf:4 * hf + 4, :].rearrange("p c n -> p (c n)"), PS[3 + hf][:], 0.0625),
                          reads=[PT[3 + hf]], writes=[T["qmT"]])
                for (gs, c0, cn) in groups:
                    for h in range(4):
                        for mt in range(2):
                            hm = h * 2 + mt
                            bk = 5 + hm // 4
                            for dk in range(2):
                                kt_ap = skT[:, gs, h * 2 + dk, mt * 128:(mt + 1) * 128] if smp else W["mkT"][:, h * 2 + dk, mt * 128:(mt + 1) * 128]
                                fw.op("pe", lambda e, hm=hm, bk=bk, dk=dk, kt_ap=kt_ap, c0=c0, cn=cn, h=h: e.matmul(
                                    PS[bk][:, (hm % 4) * 128 + c0:(hm % 4) * 128 + c0 + cn], kt_ap, W["qmT"][:, h * 2 + dk, c0:c0 + cn],
                                    start=(dk == 0), stop=(dk == 1)), reads=[T["big2"] if smp else T["mkT"], T["qmT"]], writes=[PT[bk]])
                for (gs, c0, cn) in groups:
                    em, emT = W["em"][gs], T["em"][gs]
                    for hf in range(2):
                        fw.op("act", lambda e, hf=hf, em=em, c0=c0, cn=cn: e.activation(
                            em[:, 4 * hf:4 * hf + 4, c0:c0 + cn], PS[5 + hf][:].rearrange("p (c n) -> p c n", c=4)[:, :, c0:c0 + cn], AF.Exp),
                              reads=[PT[5 + hf]], writes=[emT])
                for h in range(4):
                    n_acc = len(groups) * 2
                    a = 0
                    for (gs, c0, cn) in groups:
                        for mt in range(2):
                            va_ap = sVA[:, gs, mt, h, :] if smp else W["vam"][:, mt, h, :]
                            fw.op("pe", lambda e, h=h, gs=gs, mt=mt, va_ap=va_ap, a=a, n_acc=n_acc: e.matmul(
                                PS[1 + h][:, 0:257], W["em"][gs][:, h * 2 + mt, :], va_ap, start=(a == 0), stop=(a == n_acc - 1)),
                                  reads=[T["em"][gs], T["big2"] if smp else T["vam"]], writes=[PT[1 + h]])
                            a += 1
                for h in range(4):
                    fw.op("dve", lambda e, h=h: e.reciprocal(W["rec"][:, h:h + 1], PS[1 + h][:, 256:257]), reads=[PT[1 + h]], writes=[T["rec"]])
                    fw.op("dve", lambda e, h=h: e.tensor_scalar(W["hb"][:, h * 256:(h + 1) * 256], PS[1 + h][:, 0:256], W["rec"][:, h:h + 1], None, ALU.mult),
                          reads=[PT[1 + h], T["rec"]], writes=[T["hb"]])
                to_hT(W["hb"], T["hb"], ti * 128)
            load_w("w_mo")
            for ti, tl in enumerate(tiles):
                xr, xrT = W["xres"][ti], T["xres"][ti]
                proj_tm(ti * 128)
                for nh in range(2):
                    fw.op("dve", lambda e, nh=nh: e.tensor_tensor(xr[:, nh * 512:(nh + 1) * 512], xr[:, nh * 512:(nh + 1) * 512], PS[1 + nh][:], ALU.add),
                          reads=[xrT, PT[1 + nh]], writes=[xrT])
                if self.debug:
                    dst = S["x2_smp"] if smp else S["x2_own"][tl]
                    fw.dma("sp", dst, xr[:], reads=[xrT], writes=[tS["x2_smp" if smp else "x2_own"]])
            for ti, tl in enumerate(tiles):
                rms_to_hT(W["xres"][ti][:], T["xres"][ti], "g_ffn", ti * 128)
            for half in range(2):
                load_w("w_pq", half * 1024, 1024)
                for hc8 in range(8):
                    hc = half * 8 + hc8
                    bk = 1 + hc % 2
                    for k in range(8):
                        fw.op("pe", lambda e, k=k, hc8=hc8, bk=bk: e.matmul(PS[bk][:, 0:Tn], cur_w[0][:, k, hc8 * 128:(hc8 + 1) * 128], W["hT"][:, k, 0:Tn],
                                                                            start=(k == 0), stop=(k == 7)), reads=[T["hT"], cur_w[1]], writes=[PT[bk]])
                    fw.op("act", lambda e, hc=hc, bk=bk: e.copy(W["qyT"][:, hc, 0:Tn], PS[bk][:, 0:Tn]), reads=[PT[bk]], writes=[T["qyT"]])
            sc = W["big"][:].rearrange("p (c n) -> p c n", c=16)
            comb = W["big"][:].rearrange("p (h a b) -> p h a b", h=8, a=16)
            for ti, tl in enumerate(tiles):
                for hc in range(16):
                    bk = 3 + hc // 4
                    fw.op("pe", lambda e, hc=hc, bk=bk: e.matmul(PS[bk][:, (hc % 4) * 128:(hc % 4 + 1) * 128], W["qyT"][:, hc, ti * 128:(ti + 1) * 128], W["keysT"][:, hc, :],
                                                                 start=True, stop=True), reads=[T["qyT"], T["keysT"]], writes=[PT[bk]])
                for q4 in range(4):
                    fw.op("act", lambda e, q4=q4: e.copy(W["big"][:, q4 * 512:(q4 + 1) * 512], PS[3 + q4][:]), reads=[PT[3 + q4]], writes=[T["big"]])

                def top16_multi(srcs, src_trk, vals, idxs, v_trks, i_trks, w_trks):
                    G = len(srcs)
                    n = srcs[0].shape[-1]
                    wk = [W["scw"][:, g * n:(g + 1) * n] for g in range(G)]
                    for g in range(G):
                        fw.op("dve", lambda e, g=g: e.max(out=vals[g][:, 0:8], in_=srcs[g]), reads=[src_trk], writes=[v_trks[g]])
                    for g in range(G):
                        fw.op("dve", lambda e, g=g: e.max_index(out=idxs[g][:, 0:8], in_max=vals[g][:, 0:8], in_values=srcs[g]),
                              reads=[src_trk, v_trks[g]], writes=[i_trks[g]])
                    for g in range(G):
                        fw.op("dve", lambda e, g=g: e.match_replace(out=wk[g], in_to_replace=vals[g][:, 0:8], in_values=srcs[g], imm_value=-1e30),
                              reads=[src_trk, v_trks[g]], writes=[w_trks[g]])
                    for g in range(G):
                        fw.op("dve", lambda e, g=g: e.max(out=vals[g][:, 8:16], in_=wk[g]), reads=[w_trks[g]], writes=[v_trks[g]])
                    for g in range(G):
                        fw.op("dve", lambda e, g=g: e.max_index(out=idxs[g][:, 8:16], in_max=vals[g][:, 8:16], in_values=wk[g]),
                              reads=[w_trks[g], v_trks[g]], writes=[i_trks[g]])

                tvT = [Trk(f"tv{g}") for g in range(16)]; tiT = [Trk(f"ti{g}") for g in range(16)]; wkT = [Trk(f"wk{g}") for g in range(16)]
                for g in range(16):
                    tvT[g].w, tvT[g].r = T["tv"].w, list(T["tv"].r)
                    tiT[g].w, tiT[g].r = T["ti"].w, list(T["ti"].r)
                    wkT[g].w, wkT[g].r = T["scw"].w, list(T["scw"].r)
                top16_multi([sc[:, hc, :] for hc in range(16)], T["big"], [W["tv"][:, hc, :] for hc in range(16)],
                            [W["ti"][:, hc, :] for hc in range(16)], tvT, tiT, wkT)
                fw.op("dve", lambda e: e.tensor_copy(W["tif"][:], W["ti"][:]), reads=tiT, writes=[T["tif"]])
                fw.op("dve", lambda e: e.memset(W["ss"][:, 3:4], 0.0), reads=tvT + tiT + wkT, writes=[T["tv"], T["ti"], T["scw"]])
                tv4 = W["tv"][:].rearrange("p (h c) a -> p h c a", c=2)
                tif4 = W["tif"][:].rearrange("p (h c) a -> p h c a", c=2)
                fw.op("dve", lambda e: e.tensor_tensor(comb, tv4[:, :, 0, :].unsqueeze(3).to_broadcast([128, 8, 16, 16]),
                                                       tv4[:, :, 1, :].unsqueeze(2).to_broadcast([128, 8, 16, 16]), ALU.add), reads=[T["tv"]], writes=[T["big"]])
                svT = [Trk(f"sv{g}") for g in range(8)]; siT = [Trk(f"si{g}") for g in range(8)]; wk2T = [Trk(f"wkb{g}") for g in range(8)]
                for g in range(8):
                    svT[g].w, svT[g].r = T["sv"].w, list(T["sv"].r)
                    siT[g].w, siT[g].r = T["si"].w, list(T["si"].r)
                    wk2T[g].w, wk2T[g].r = T["scw"].w, list(T["scw"].r)
                top16_multi([comb[:, h].rearrange("p a b -> p (a b)") for h in range(8)], T["big"], [W["sv"][:, h, :] for h in range(8)],
                            [W["si"][:, h, :] for h in range(8)], svT, siT, wk2T)
                fw.op("dve", lambda e: e.memset(W["ss"][:, 3:4], 0.0), reads=svT + siT + wk2T, writes=[T["sv"], T["si"], T["scw"]])
                fw.op("dve", lambda e: e.tensor_scalar(W["bi"][:], W["si"][:], 15, None, ALU.bitwise_and), reads=[T["si"]], writes=[T["bi"]])
                fw.op("dve", lambda e: e.tensor_scalar(W["ai"][:], W["si"][:], 4, None, ALU.logical_shift_right), reads=[T["si"]], writes=[T["ai"]])
                fw.op("dve", lambda e: e.tensor_copy(W["bidx"][:], W["bi"][:]), reads=[T["bi"]], writes=[T["bidx"]])
                fw.op("dve", lambda e: e.tensor_copy(W["aidx"][:], W["ai"][:]), reads=[T["ai"]], writes=[T["aidx"]])
                io16 = W["iota"][:, 0:16].unsqueeze(1).unsqueeze(1).to_broadcast([128, 8, 16, 16])
                for q, (ix, cc) in enumerate((("aidx", 0), ("bidx", 1))):
                    fw.op("dve", lambda e, ix=ix: e.tensor_tensor(comb, io16, W[ix][:].unsqueeze(3).to_broadcast([128, 8, 16, 16]), ALU.is_equal),
                          reads=[T["iota"], T[ix]], writes=[T["big"]])
                    fw.op("dve", lambda e, cc=cc: e.tensor_tensor(comb, comb, tif4[:, :, cc, :].unsqueeze(2).to_broadcast([128, 8, 16, 16]), ALU.mult),
                          reads=[T["big"], T["tif"]], writes=[T["big"]])
                    fw.op("dve", lambda e, q=q: e.reduce_sum(W["ijg"][:, q, :], comb.rearrange("p h k a -> p (h k) a"), AXX), reads=[T["big"]], writes=[T["ijg"]])
                fw.op("dve", lambda e: e.tensor_tensor(W["svx"][:], W["sv"][:], W["sv"][:, :, 0:1].to_broadcast([128, 8, 16]), ALU.subtract),
                      reads=[T["sv"]], writes=[T["svx"]])
                fw.op("act", lambda e: e.activation(W["svx"][:], W["svx"][:], AF.Exp), reads=[T["svx"]], writes=[T["svx"]])
                fw.op("dve", lambda e: e.reduce_sum(W["sm"][:, :, 0], W["svx"][:], AXX), reads=[T["svx"]], writes=[T["sm"]])
                fw.op("dve", lambda e: e.reciprocal(W["sm"][:, :, 1], W["sm"][:, :, 0]), reads=[T["sm"]], writes=[T["sm"]])
                fw.op("dve", lambda e: e.tensor_tensor(W["ijg"][:, 2, :].rearrange("p (h k) -> p h k", h=8), W["svx"][:], W["sm"][:, :, 1:2].to_broadcast([128, 8, 16]), ALU.mult),
                      reads=[T["svx"], T["sm"]], writes=[T["ijg"]])
                for q in range(3):
                    fw.op("pe", lambda e, q=q: e.transpose(PS[7][:, q * 128:(q + 1) * 128], W["ijg"][:, q, :], C["ident"][:]), reads=[T["ijg"], CT["ident"]], writes=[PT[7]])
                fw.op("act", lambda e: e.copy(W["ijgT"][:, :, ti * 128:(ti + 1) * 128], PS[7][:, 0:384].rearrange("p (q n) -> p q n", q=3)),
                      reads=[PT[7]], writes=[T["ijgT"]])
            nchunk = 0
            for half in range(2):
                for t0 in range(0, Tn, 16):
                    cb_ = (t0 // 16) % 2
                    oic, oicT = W["oic"][cb_], T["oic"][cb_]
                    ojc, ojcT = W["ojc"][cb_], T["ojc"][cb_]
                    fw.op("dve", lambda e, oic=oic, t0=t0: e.tensor_tensor(
                        oic[:], W["iota"][:, half * 64:(half + 1) * 64].unsqueeze(1).to_broadcast([128, 16, 64]),
                        W["ijgT"][:, 0, t0:t0 + 16].unsqueeze(2).to_broadcast([128, 16, 64]), ALU.is_equal), reads=[T["iota"], T["ijgT"]], writes=[oicT])
                    fw.op("dve", lambda e, oic=oic, t0=t0: e.tensor_tensor(
                        oic[:], oic[:], W["ijgT"][:, 2, t0:t0 + 16].unsqueeze(2).to_broadcast([128, 16, 64]), ALU.mult), reads=[oicT, T["ijgT"]], writes=[oicT])
                    fw.op("dve", lambda e, ojc=ojc, t0=t0: e.tensor_tensor(
                        ojc[:], W["iota"][:].unsqueeze(1).to_broadcast([128, 16, 128]),
                        W["ijgT"][:, 1, t0:t0 + 16].unsqueeze(2).to_broadcast([128, 16, 128]), ALU.is_equal), reads=[T["iota"], T["ijgT"]], writes=[ojcT])
                    for t8 in range(2):
                        bk = 5 + ((t0 // 8) + t8) % 2
                        for tt in range(8):
                            tq = t8 * 8 + tt
                            fw.op("pe", lambda e, oic=oic, ojc=ojc, tt=tt, tq=tq, bk=bk: e.matmul(PS[bk][:, tt * 64:(tt + 1) * 64], ojc[:, tq, :], oic[:, tq, :], start=True, stop=True),
                                  reads=[oicT, ojcT], writes=[PT[bk]])
                        ts = t0 + t8 * 8
                        ev_eng = "act"
                        fw.op(ev_eng, lambda e, ts=ts, bk=bk, ev_eng=ev_eng: (e.copy if ev_eng == "act" else e.tensor_copy)(
                            WT[:, ts:ts + 8, :], PS[bk][:].rearrange("p (t i) -> p t i", t=8)), reads=[PT[bk]], writes=[T["big2"]])
                def emit_dma(ig):
                    sbn = (ig // 2) % 3
                    fw.dma("pool", W["uT"][sbn][:], I["peer_uT"][ig:ig + 2].rearrange("i p k e -> p i k e"), writes=[T["uT"][sbn]])
                    fw.dma("pool", W["vv"][sbn][:], I["peer_v"][ig * 128:(ig + 2) * 128, :].rearrange("(i p) d -> p i d", p=128), writes=[T["vv"][sbn]])

                def emit_A(i):
                    sbn, ii, bk = (i // 2) % 3, i % 2, 5 + i % 2
                    for k in range(8):
                        fw.op("pe", lambda e, k=k: e.matmul(PS[bk][:, 0:Tn], W["uT"][sbn][:, ii, k, :], W["hT"][:, k, 0:Tn], start=(k == 0), stop=(k == 7)),
                              reads=[T["uT"][sbn], T["hT"]], writes=[PT[bk]])

                def emit_rest(i):
                    sbn, ii, bk = (i // 2) % 3, i % 2, 5 + i % 2
                    ga, gaT = W["ga"][i % 2], T["ga"][i % 2]
                    ptb, ptbT = W["ptb"][i % 2], T["ptb"][i % 2]
                    fw.op("act", lambda e: e.activation(ga[:, 0:Tn], PS[bk][:, 0:Tn], AF.Gelu), reads=[PT[bk]], writes=[gaT])
                    fw.op("dve", lambda e: e.tensor_tensor(ptb[:, 0:Tn], ga[:, 0:Tn], WT[:, 0:Tn, i - half * 64], ALU.mult),
                          reads=[gaT, T["big2"]], writes=[ptbT])
                    for ti in range(nt):
                        for nh in range(2):
                            fw.op("pe", lambda e, ti=ti, nh=nh: e.matmul(PS[1 + 2 * ti + nh][:], ptb[:, ti * 128:(ti + 1) * 128], W["vv"][sbn][:, ii, nh * 512:(nh + 1) * 512],
                                                                         start=(i == 0), stop=(i == 127)),
                                  reads=[ptbT, T["vv"][sbn]], writes=[PT[1 + 2 * ti + nh]])

                i_lo, i_hi = half * 64, half * 64 + 64
                emit_dma(i_lo)
                emit_dma(i_lo + 2)
                emit_A(i_lo)
                for i in range(i_lo, i_hi):
                    if i + 1 < i_hi:
                        if (i + 1) % 2 == 0 and i + 3 < i_hi:
                            emit_dma(i + 3)
                        emit_A(i + 1)
                    emit_rest(i)
            for ti, tl in enumerate(tiles):
                xr, xrT = W["xres"][ti], T["xres"][ti]
                for nh in range(2):
                    fw.op("dve", lambda e, nh=nh, ti=ti: e.tensor_tensor(xr[:, nh * 512:(nh + 1) * 512], xr[:, nh * 512:(nh + 1) * 512], PS[1 + 2 * ti + nh][:], ALU.add),
                          reads=[xrT, PT[1 + 2 * ti + nh]], writes=[xrT])
                if self.debug and smp:
                    fw.dma("sp", S["x3_smp"], xr[:], reads=[xrT], writes=[tS["x3_smp"]])
                fw.op("dve", lambda e: e.memset(W["ss"][:, 0:1], 0.0), writes=[T["ss"]])
                fw.op("act", lambda e: e.activation(W["junk"][:], xr[:], AF.Square, accum_out=W["ss"][:, 0:1]), reads=[xrT], writes=[T["junk"], T["ss"]])
                fw.op("act", lambda e: e.activation(W["ss"][:, 1:2], W["ss"][:, 0:1], AF.Sqrt, bias=C["cst"][:, 1:2], scale=1.0),
                      reads=[T["ss"], CT["cst"]], writes=[T["ss"]])
                fw.op("dve", lambda e: e.reciprocal(W["ss"][:, 2:3], W["ss"][:, 1:2]), reads=[T["ss"]], writes=[T["ss"]])
                fw.op("dve", lambda e: e.scalar_tensor_tensor(W["yo"][:], xr[:], W["ss"][:, 2:3], W["g_final"][:], ALU.mult, ALU.mult),
                      reads=[xrT, T["ss"], T["g_final"]], writes=[T["yo"]])
                dst = O["y_smp"] if smp else O["y_own"][tl * 128:(tl + 1) * 128, :]
                fw.dma("sp", dst, W["yo"][:], reads=[T["yo"]], is_output=True)

        for b0 in range(0, NOWN, 2):
            block(list(range(b0, min(NOWN, b0 + 2))), False)
        block([0], True)


def _chunk_consts(L):
    nch = 128 // L
    idx = np.arange(128)
    same = (idx[:, None] // L) == (idx[None, :] // L)
    tri = (same & (idx[:, None] <= idx[None, :])).astype(np.float32)
    u = (same & (idx[:, None] > idx[None, :])).astype(np.float32)
    onb = same.astype(np.float32)
    sel = np.zeros((128, nch, 128), np.float32)
    rm = np.zeros((128, nch), np.float32)
    cm = np.zeros((128, nch, 128), np.float32)
    for ch in range(nch):
        sel[ch * L:(ch + 1) * L, ch, :] = 1.0
        rm[ch * L:(ch + 1) * L, ch] = 1.0
        cm[:, ch, ch * L:(ch + 1) * L] = 1.0
    return tri, u, onb, sel, rm, cm


def _rope_tables(pos):
    half = 8
    inv = (1.0 / (np.float32(500000.0) ** (np.arange(half, dtype=np.float32) / np.float32(half)))).astype(np.float32)
    ang = (pos.astype(np.float32)[:, None] * inv[None, :]).astype(np.float32)
    return np.cos(ang.astype(np.float64)).astype(np.float32), np.sin(ang.astype(np.float64)).astype(np.float32)


_PROG_CACHE = {}


def _get_prog(SEQ, PAST, stages, debug=False):
    key = (SEQ, PAST, stages, debug)
    if key not in _PROG_CACHE:
        p = Prog(SEQ, PAST, stages, debug)
        p.build()
        _PROG_CACHE[key] = p
    return _PROG_CACHE[key]


def kernel(_stages=3, _debug=False, **inp):
    f32 = np.float32
    g = lambda n: np.asarray(inp[n], dtype=f32)
    x_prompt, x_sample = g("x_prompt"), g("x_sample")
    SEQ = x_prompt.shape[1]
    PAST = inp["cache_attn_k"].shape[2]
    NT, NOWN = SEQ // 128, SEQ // 512
    prog = _get_prog(SEQ, PAST, _stages, _debug)

    bc = lambda v, n=128: np.ascontiguousarray(np.broadcast_to(np.asarray(v, f32).reshape(1, -1), (n, np.asarray(v).size)))
    shared = {}
    shared["w_in"] = g("w_in")[0]
    for n in ("w_out", "w_mq", "w_mk", "w_mv", "w_mo", "w_pq"):
        shared[n] = g(n)[0]
    shared["keysT"] = np.ascontiguousarray(g("peer_keys")[0].reshape(16, 128, 128).transpose(2, 0, 1))
    pu = g("peer_u")[0]
    shared["peer_uT"] = np.ascontiguousarray(pu.reshape(128, 128, 8, 128).transpose(0, 3, 2, 1))
    shared["peer_v"] = g("peer_v")[0]
    shared["g_mix"] = bc(g("g_mix")[0]); shared["g_memq"] = bc(g("g_mem_q")[0]); shared["g_memkv"] = bc(g("g_mem_kv")[0])
    shared["g_ffn"] = bc(g("g_ffn")[0]); shared["g_final"] = bc(g("g_final"))
    shared["g_ssd"] = bc(g("g_ssd")[0]); shared["g_subln"] = bc(g("g_subln")[0])
    shared["dt_bias"] = bc(g("dt_bias")[0]); shared["a_log"] = bc(g("a_log")[0]); shared["d_skip"] = bc(g("d_skip")[0])
    lam = np.stack([g("lam_q1")[0], g("lam_k1")[0], g("lam_q2")[0], g("lam_k2")[0]], 0)
    shared["lam"] = np.ascontiguousarray(np.broadcast_to(lam[None], (128, 4, 64)))
    shared["conv_wT"] = np.ascontiguousarray(g("conv_w")[0].reshape(4, 8, 128).transpose(2, 1, 0))
    shared["conv_bT"] = np.ascontiguousarray(g("conv_b")[0].reshape(8, 128).T)
    shared["ident"] = np.eye(128, dtype=f32)
    for L in (64, 32):
        tri, u, onb, sel, rm, cm = _chunk_consts(L)
        shared[f"tri{L}"], shared[f"u{L}"], shared[f"onb{L}"] = tri, u, onb
        shared[f"sel{L}"], shared[f"rm{L}"], shared[f"cm{L}"] = sel, rm, cm
    cp, sp_ = _rope_tables(np.arange(SEQ))
    shared["cos_p"] = np.ascontiguousarray(cp.reshape(NT, 128, 8).transpose(1, 0, 2))
    shared["sin_p"] = np.ascontiguousarray(sp_.reshape(NT, 128, 8).transpose(1, 0, 2))
    cs, ss = _rope_tables(PAST + (np.arange(128) % 32))
    shared["cos_s"], shared["sin_s"] = cs, ss
    shared["iota"] = np.ascontiguousarray(np.broadcast_to(np.arange(128, dtype=f32)[None], (128, 128)))
    idx = np.arange(128)
    am_s = np.zeros((128, 5, 128), f32)
    for s in range(4):
        am_s[:, s, 32 * s:32 * (s + 1)] = 1.0
    am_s[:, 4, :] = ((idx[:, None] // 32) == (idx[None, :] // 32)).astype(f32)
    shared["amask_s"] = am_s

    in_maps = []
    for c in range(NCORES):
        b, j = c // 4, c % 4
        m = dict(shared)
        m["x_all"] = x_prompt[b]
        m["x_own"] = np.ascontiguousarray(x_prompt[b].reshape(NOWN, 4, 128, D)[:, j].reshape(NOWN * 128, D))
        m["x_smp"] = np.ascontiguousarray(x_sample[4 * c:4 * c + 4].reshape(128, D))
        m["mem_p"] = g("mem_prompt")[b]
        am = np.zeros((128, 4, 128), f32)
        for r in range(4):
            if r < j:
                am[:, r, :] = 1.0
            elif r == j:
                am[:, r, :] = ((idx[:, None] // 64) <= (idx[None, :] // 64)).astype(f32)
        m["amask_p"] = am
        sj = np.zeros((128, 4), f32); sj[:, j] = 1.0
        m["selj"] = sj
        sl = slice(4 * c, 4 * c + 4)
        ck = g("cache_attn_k")[0, sl]
        m["ck_T"] = np.ascontiguousarray(ck.transpose(0, 2, 3, 1))
        m["cv"] = np.ascontiguousarray(g("cache_attn_v")[0, sl].reshape(4, PAST, 512))
        cmk = g("cache_mem_k")[0, sl]
        m["cmk_T"] = np.ascontiguousarray(cmk.reshape(4, 256, 4, 2, 128).transpose(0, 2, 3, 4, 1))
        m["cmv"] = np.ascontiguousarray(g("cache_mem_v")[0, sl].reshape(4, 256, D))
        st = g("state_ssm")[0, sl]
        m["ssm_T"] = np.ascontiguousarray(st.transpose(0, 3, 1, 2).reshape(4, 128, 512))
        cv_ = g("state_conv")[0, sl]
        m["conv_T"] = np.ascontiguousarray(cv_.reshape(4, 3, 8, 128).transpose(3, 2, 0, 1))
        in_maps.append({k: np.ascontiguousarray(v, dtype=f32) for k, v in m.items() if k in prog.in_shapes})

    res = run_bass_kernel_spmd(prog.nc, in_maps, core_ids=list(range(NCORES)))
    R = res.results
    if _debug:
        kernel.last_results = R
    B = x_prompt.shape[0]
    y_prompt = np.zeros((B, SEQ, D), f32)
    for c in range(NCORES):
        b, j = c // 4, c % 4
        y_prompt[b].reshape(NOWN, 4, 128, D)[:, j] = R[c]["y_own"].reshape(NOWN, 128, D)
    y_sample = np.concatenate([R[c]["y_smp"].reshape(4, 32, D) for c in range(NCORES)], 0)
    newk_p = np.stack([R[4 * b]["newk"].reshape(SEQ, 4, 128) for b in range(B)], 0)[None]
    newv_p = np.stack([R[4 * b]["newv"].reshape(SEQ, 4, 128) for b in range(B)], 0)[None]
    ssm_p = np.stack([R[4 * b]["ssm_p"].reshape(8, 64, 128) for b in range(B)], 0)[None]
    conv_p = np.stack([R[4 * b]["conv_p"].reshape(128, 8, 3).transpose(2, 1, 0).reshape(3, D) for b in range(B)], 0)[None]
    memk_p = np.stack([R[4 * b]["memk_p"].reshape(256, 4, 256) for b in range(B)], 0)[None]
    memv_p = np.stack([R[4 * b]["memv_p"].reshape(256, 4, 256) for b in range(B)], 0)[None]
    newk_s = np.concatenate([R[c]["newk_s"].reshape(4, 32, 4, 128) for c in range(NCORES)], 0)[None]
    newv_s = np.concatenate([R[c]["newv_s"].reshape(4, 32, 4, 128) for c in range(NCORES)], 0)[None]
    ssm_s = np.concatenate([R[c]["ssm_s"].reshape(4, 8, 64, 128) for c in range(NCORES)], 0)[None]
    conv_s = np.concatenate([R[c]["conv_s"].reshape(128, 8, 4, 3).transpose(2, 3, 1, 0).reshape(4, 3, D) for c in range(NCORES)], 0)[None]
    return (y_prompt, y_sample, newk_p, newv_p, ssm_p, conv_p, memk_p, memv_p, newk_s, newv_s, ssm_s, conv_s)
```

```python
import math
from contextlib import ExitStack

import numpy as np
import concourse.bass as bass
import concourse.mybir as mybir
from concourse.bass_utils import run_bass_kernel_spmd

F32 = mybir.dt.float32
BF16 = mybir.dt.bfloat16
U32 = mybir.dt.uint32
AF = mybir.ActivationFunctionType
ALU = mybir.AluOpType

D = 1024
IN_DIM = 3080
EPS = 1e-6
NCORES = 8
C_Z, C_XBC, C_DT, C_Q, C_K, C_V = 0, 512, 1536, 1544, 2056, 2568
LAMBDA_INIT = 0.8 - 0.6 * math.exp(0.0)


class Trk:
    __slots__ = ("w", "r", "name")

    def __init__(self, name=""):
        self.w = None
        self.r = []
        self.name = name


class FW:
    SEM_CAP = 6000
    N_DMA_SEMS = 16

    def __init__(self, nc, stack):
        self.nc = nc
        self.stack = stack
        self.eng = {"pe": nc.tensor, "dve": nc.vector, "act": nc.scalar, "pool": nc.gpsimd, "sp": nc.sync}
        self._keep = []
        self.csem = {}
        self.ccnt = {}
        self.nsem = 0
        for e in ("pe", "dve", "act", "pool"):
            self._new_csem(e)
        self.dsem = {}
        for q in ("sp", "pool"):
            self.dsem[q] = [[self._sem(f"d{q}{i}"), 0] for i in range(self.N_DMA_SEMS)]
        self.dnext = {"sp": 0, "pool": 0}
        self.waited = {e: {} for e in self.eng}
        self.out_tokens = []
        self.n_inst = 0
        self.cur_stack = None

    def _sem(self, name):
        self.nsem += 1
        h = self.stack.enter_context(self.nc.semaphore(f"{name}_{self.nsem}"))
        self._keep.append(h)
        return h

    def _new_csem(self, e):
        self.csem[e] = self._sem(f"c{e}")
        self.ccnt[e] = 0

    def _wait(self, e, toks, defer=False):
        need = {}
        for t in toks:
            if t is None:
                continue
            sem, val, src = t
            if src == "pe" and e == "pe":
                continue
            k = id(sem)
            if k not in need or need[k][1] < val:
                need[k] = (sem, val)
        todo = [(k, sem, val) for k, (sem, val) in need.items() if self.waited[e].get(k, 0) < val]
        held = None
        if defer and todo:
            held = todo.pop()
        for k, sem, val in todo:
            self.eng[e].wait_ge(sem, val)
            self.waited[e][k] = val
            self.n_inst += 1
        if held is not None:
            k, sem, val = held
            self.waited[e][k] = val
            return (sem, val)
        return None

    @staticmethod
    def _deps(reads, writes):
        toks = []
        for b in reads:
            toks.append(b.w)
        for b in writes:
            toks.append(b.w)
            toks.extend(b.r)
        return toks

    @staticmethod
    def _commit(tok, reads, writes):
        for b in reads:
            if b not in writes:
                b.r.append(tok)
                if len(b.r) > 64:
                    b.r = b.r[-64:]
        for b in writes:
            b.w = tok
            b.r = []

    def op(self, e, fn, reads=(), writes=()):
        reads = [b for b in reads if b is not None]
        writes = [b for b in writes if b is not None]
        held = self._wait(e, self._deps(reads, writes), defer=True)
        if self.ccnt[e] >= self.SEM_CAP:
            self._new_csem(e)
        n0 = self.nc.n_instructions()
        ins = fn(self.eng[e])
        assert self.nc.n_instructions() - n0 == 1, "multi-instruction op: cannot attach wait"
        if held is not None:
            ins._wait_ge(held[0], held[1])
        self.ccnt[e] += 1
        ins.then_inc(self.csem[e], 1)
        tok = (self.csem[e], self.ccnt[e], e)
        self._commit(tok, reads, writes)
        self.n_inst += 1
        return tok

    def dma(self, q, out, in_, reads=(), writes=(), is_output=False, **kw):
        reads = [b for b in reads if b is not None]
        writes = [b for b in writes if b is not None]
        slot = self.dsem[q][self.dnext[q]]
        self.dnext[q] = (self.dnext[q] + 1) % self.N_DMA_SEMS
        if slot[1] >= self.SEM_CAP:
            slot[0] = self._sem(f"d{q}")
            slot[1] = 0
        toks = self._deps(reads, writes)
        if slot[1] > 0:
            toks.append((slot[0], slot[1], "dma"))
        held = self._wait(q, toks, defer=True)
        n0 = self.nc.n_instructions()
        ins = self.eng[q].dma_start(out=out, in_=in_, **kw)
        if held is not None:
            if self.nc.n_instructions() - n0 == 1:
                ins._wait_ge(held[0], held[1])
            else:
                raise AssertionError("multi-instruction dma")
        slot[1] += 16
        ins.then_inc(slot[0], 16)
        tok = (slot[0], slot[1], "dma")
        self._commit(tok, reads, writes)
        if is_output:
            self.out_tokens.append(tok)
        self.n_inst += 1
        return tok

    def finish(self):
        toks = list(self.out_tokens)
        for q in self.dsem:
            for slot in self.dsem[q]:
                if slot[1] > 0:
                    toks.append((slot[0], slot[1], "dma"))
        self._wait("sp", toks)

    def sb(self, name, shape, dtype=F32):
        st = self.cur_stack if self.cur_stack is not None else self.stack
        return st.enter_context(self.nc.sbuf_tensor(name, list(shape), dtype))

    def barrier(self):
        toks = []
        for e in ("pe", "dve", "act", "pool"):
            if self.ccnt[e] > 0:
                toks.append((self.csem[e], self.ccnt[e], "x"))
        for q in self.dsem:
            for slot in self.dsem[q]:
                if slot[1] > 0:
                    toks.append((slot[0], slot[1], "dma"))
        for e in ("pe", "dve", "act", "pool", "sp"):
            self._wait(e, toks)

    def ps(self, name, shape, dtype=F32):
        return self.stack.enter_context(self.nc.psum_tensor(name, list(shape), dtype))


class Prog:
    def __init__(self, SEQ, PAST, stages=3, debug=False):
        self.debug = debug
        self.SEQ, self.PAST = SEQ, PAST
        self.NT = SEQ // 128
        self.NOWN = self.NT // 4
        self.NKP = PAST // 128
        self.stages = stages
        self.nc = bass.Bass("TRN2", target_bir_lowering=False)
        self.in_shapes = {}
        self.out_shapes = {}

    def din(self, name, shape, dt=F32):
        self.in_shapes[name] = (tuple(shape), dt)
        return self.nc.dram_tensor(name, list(shape), dt, kind="ExternalInput").ap()

    def dout(self, name, shape, dt=F32):
        self.out_shapes[name] = (tuple(shape), dt)
        return self.nc.dram_tensor(name, list(shape), dt, kind="ExternalOutput").ap()

    def dscr(self, name, shape, dt=F32):
        if self.debug and dt == F32:
            return self.dout("dbg_" + name, shape, dt)
        return self.nc.dram_tensor(name, list(shape), dt, kind="Internal").ap()

    def build(self):
        with ExitStack() as st:
            self.fw = FW(self.nc, st)
            self._declare()
            self._setup()
            for ph, fn in ((1, self._phase1), (2, self._phase2), (3, self._phase3)):
                if self.stages >= ph:
                    with ExitStack() as pst:
                        self.fw.cur_stack = pst
                        fn()
                        self.fw.barrier()
                    self.fw.cur_stack = None
            self.fw.finish()
        return self.nc

    def _declare(self):
        SEQ, PAST, NT, NOWN = self.SEQ, self.PAST, self.NT, self.NOWN
        di, do, ds = self.din, self.dout, self.dscr
        I = self.I = {}
        O = self.O = {}
        S = self.S = {}
        I["x_all"] = di("x_all", [SEQ, D])
        I["x_own"] = di("x_own", [NOWN * 128, D])
        I["x_smp"] = di("x_smp", [128, D])
        I["mem_p"] = di("mem_p", [256, D])
        I["w_in"] = di("w_in", [D, IN_DIM])
        for n in ("w_out", "w_mq", "w_mk", "w_mv", "w_mo"):
            I[n] = di(n, [D, D])
        I["w_pq"] = di("w_pq", [D, 2048])
        if self.stages >= 3:
            I["keysT"] = di("keysT", [128, 16, 128])
            I["peer_uT"] = di("peer_uT", [128, 128, 8, 128])
            I["peer_v"] = di("peer_v", [16384, D])
        for n in ("g_mix", "g_memq", "g_memkv", "g_ffn", "g_final"):
            I[n] = di(n, [128, D])
        I["g_ssd"] = di("g_ssd", [128, 512])
        I["g_subln"] = di("g_subln", [128, 128])
        I["dt_bias"] = di("dt_bias", [128, 8])
        I["a_log"] = di("a_log", [128, 8])
        I["d_skip"] = di("d_skip", [128, 8])
        I["lam"] = di("lam", [128, 4, 64])
        I["conv_wT"] = di("conv_wT", [128, 8, 4])
        I["conv_bT"] = di("conv_bT", [128, 8])
        I["ident"] = di("ident", [128, 128])
        for L, nch in ((64, 2), (32, 4)):
            I[f"tri{L}"] = di(f"tri{L}", [128, 128])
            I[f"u{L}"] = di(f"u{L}", [128, 128])
            I[f"onb{L}"] = di(f"onb{L}", [128, 128])
            I[f"sel{L}"] = di(f"sel{L}", [128, nch, 128])
            I[f"rm{L}"] = di(f"rm{L}", [128, nch])
            I[f"cm{L}"] = di(f"cm{L}", [128, nch, 128])
        I["cos_p"] = di("cos_p", [128, NT, 8])
        I["sin_p"] = di("sin_p", [128, NT, 8])
        I["cos_s"] = di("cos_s", [128, 8])
        I["sin_s"] = di("sin_s", [128, 8])
        I["amask_p"] = di("amask_p", [128, 4, 128])
        I["amask_s"] = di("amask_s", [128, 5, 128])
        I["selj"] = di("selj", [128, 4])
        I["iota"] = di("iota", [128, 128])
        I["ck_T"] = di("ck_T", [4, 4, 128, PAST])
        I["cv"] = di("cv", [4, PAST, 512])
        I["cmk_T"] = di("cmk_T", [4, 4, 2, 128, 256])
        I["cmv"] = di("cmv", [4, 256, D])
        I["ssm_T"] = di("ssm_T", [4, 128, 512])
        I["conv_T"] = di("conv_T", [128, 8, 4, 3])

        O["y_own"] = do("y_own", [NOWN * 128, D])
        O["y_smp"] = do("y_smp", [128, D])
        O["newk"] = do("newk", [SEQ, 512])
        O["newv"] = do("newv", [SEQ, 512])
        O["ssm_p"] = do("ssm_p", [512, 128])
        O["conv_p"] = do("conv_p", [128, 8, 1, 3])
        O["memk_p"] = do("memk_p", [256, D])
        O["memv_p"] = do("memv_p", [256, D])
        O["newk_s"] = do("newk_s", [128, 512])
        O["newv_s"] = do("newv_s", [128, 512])
        O["ssm_s"] = do("ssm_s", [4, 512, 128])
        O["conv_s"] = do("conv_s", [128, 8, 4, 3])

        S["KT"] = ds("KT", [4, 128, SEQ], BF16)
        S["VA"] = ds("VA", [4, NT, 128, 130], BF16)
        S["q_own"] = ds("q_own", [NOWN, 128, 512])
        S["yn_own"] = ds("yn_own", [NOWN, 128, 512])
        S["o_own"] = ds("o_own", [NOWN, 128, 512])
        S["q_smp"] = ds("q_smp", [128, 512])
        S["yn_smp"] = ds("yn_smp", [128, 512])
        S["o_smp"] = ds("o_smp", [128, 512])
        if self.debug:
            for nm, shp in (("x1_own", [NOWN, 128, D]), ("x2_own", [NOWN, 128, D]), ("x1_smp", [128, D]), ("x2_smp", [128, D]), ("x3_smp", [128, D])):
                S[nm] = ds(nm, shp)
        S["KT_s"] = ds("KT_s", [4, 128, 128], BF16)
        S["VA_s"] = ds("VA_s", [4, 128, 130], BF16)
        self.tS = {k: Trk("scr_" + k) for k in S}

    def _load_const(self, name, shape, dt=F32, q="sp", src=None):
        t = self.fw.sb("c_" + name, shape, dt)
        trk = Trk("c_" + name)
        src = self.I[name] if src is None else src
        self.fw.dma(q, t[:], src, writes=[trk])
        return t, trk

    def _setup(self):
        fw, I = self.fw, self.I
        C = self.C = {}
        CT = self.CT = {}

        def ld(name, shape, dt=F32, q="sp"):
            C[name], CT[name] = self._load_const(name, shape, dt, q)

        ld("ident", [128, 128])
        C["identb"], CT["identb"] = self._load_const("identb", [128, 128], BF16, "pool", src=I["ident"])
        cst = C["cst"] = fw.sb("cst", [128, 8])
        CT["cst"] = Trk("cst")
        vals = [1.0, D * EPS, 512 * EPS, 128 * EPS, 0.0, EPS, 0.0, 0.0]
        for i, v in enumerate(vals):
            fw.op("dve", lambda e, i=i, v=v: e.memset(cst[:, i:i + 1], v), writes=[CT["cst"]])
        self.PS = [fw.ps(f"ps{i}", [128, 512]) for i in range(8)]
        self.PT = [Trk(f"ps{i}") for i in range(8)]

    def _phase1(self):
        fw, I, O, S, C, CT = self.fw, self.I, self.O, self.S, self.C, self.CT
        PS, PT = self.PS, self.PT
        sb = fw.sb
        def ld(name, shape, dt=F32, q="sp"):
            C[name], CT[name] = self._load_const(name, shape, dt, q)

        for L, nch in ((64, 2), (32, 4)):
            ld(f"tri{L}", [128, 128]); ld(f"u{L}", [128, 128]); ld(f"onb{L}", [128, 128])
            ld(f"sel{L}", [128, nch, 128]); ld(f"rm{L}", [128, nch]); ld(f"cm{L}", [128, nch, 128])
        ld("g_mix", [128, D]); ld("g_ssd", [128, 512])
        ld("dt_bias", [128, 8]); ld("a_log", [128, 8]); ld("d_skip", [128, 8])
        ld("conv_wT", [128, 8, 4]); ld("conv_bT", [128, 8])
        ld("cos_p", [128, self.NT, 8]); ld("sin_p", [128, self.NT, 8]); ld("cos_s", [128, 8]); ld("sin_s", [128, 8])
        ld("selj", [128, 4])
        a_t = C["a"] = fw.sb("a_neg", [128, 8]); CT["a"] = Trk("a")
        fw.op("act", lambda e: e.activation(a_t[:], C["a_log"][:], AF.Exp), reads=[CT["a_log"]], writes=[CT["a"]])
        fw.op("dve", lambda e: e.tensor_scalar(a_t[:], a_t[:], -1.0, None, ALU.mult), reads=[CT["a"]], writes=[CT["a"]])
        fw.op("dve", lambda e: e.tensor_scalar(C["g_mix"][:], C["g_mix"][:], math.sqrt(D), None, ALU.mult),
              reads=[CT["g_mix"]], writes=[CT["g_mix"]])
        fw.op("dve", lambda e: e.tensor_scalar(C["g_ssd"][:], C["g_ssd"][:], math.sqrt(512.0), None, ALU.mult),
              reads=[CT["g_ssd"]], writes=[CT["g_ssd"]])
        dsk = C["dskb"] = fw.sb("dskb", [128, 8, 64]); CT["dskb"] = Trk("dskb")
        fw.op("dve", lambda e: e.tensor_copy(dsk[:], C["d_skip"][:].unsqueeze(2).to_broadcast([128, 8, 64])),
              reads=[CT["d_skip"]], writes=[CT["dskb"]])
        w = C["w_in"] = fw.sb("w_in_sb", [128, 8, IN_DIM], BF16); CT["w_in"] = Trk("w_in")
        for k in range(8):
            fw.dma("pool", w[:, k, :], I["w_in"][k * 128:(k + 1) * 128, :], writes=[CT["w_in"]])
        W = self.W1 = {}
        T = self.T1 = {}

        def mk(name, shape, dt=F32, n=1):
            if n == 1:
                W[name] = sb("p1_" + name, shape, dt); T[name] = Trk(name)
            else:
                W[name] = [sb(f"p1_{name}{i}", shape, dt) for i in range(n)]
                T[name] = [Trk(f"{name}{i}") for i in range(n)]

        mk("xt", [128, D], F32, 2)
        mk("junk", [128, D], F32)
        mk("ss", [128, 4], F32)
        mk("hb", [128, D], BF16)
        mk("hT", [128, 8, 128], BF16)
        mk("cbuf", [128, 8, 140], F32)
        mk("cacc", [128, 8, 128], F32)
        mk("xc", [128, 8, 128], F32)
        mk("bct", [128, 4, 128], BF16, 2)
        mk("ctm", [128, 4, 2, 128], BF16, 2)
        mk("xs", [128, 512], F32, 2)
        mk("btm", [128, 256], BF16, 2)
        mk("dt", [128, 8], F32, 2)
        mk("da", [128, 8], F32)
        mk("ex", [128, 48], F32)
        mk("wch", [128, 4, 8], F32)
        mk("xdt", [128, 8, 64], BF16)
        mk("xdte", [128, 4, 512], BF16)
        mk("cbm", [128, 2, 128], F32)
        mk("dau", [128, 8, 128], F32)
        mk("dec", [128, 8, 128], F32)
        mk("mt", [128, 8, 128], BF16)
        mk("sst", [128, 8, 64], F32, 2)
        mk("sbf", [128, 4, 512], BF16)
        mk("stmp", [128, 8, 64], F32)
        mk("ytmp", [128, 8, 64], F32)
        mk("y", [128, 512], F32)
        mk("zs", [128, 512], F32, 2)
        mk("yn", [128, 512], F32)
        mk("q", [128, 512], F32, 2)
        mk("k", [128, 512], F32)
        mk("v", [128, 512], F32)
        mk("rt", [128, 4, 8, 8], F32)
        mk("ktt", [128, 4, 128], BF16)
        mk("va", [128, 4, 130], BF16, 2)
        mk("ownq", [128, 512], F32)
        mk("ownyn", [128, 512], F32)
        mk("h0", [128, 4, 512], F32)
        mk("sout", [128, 128], F32)
        mk("ptmp", [128, 512], F32)
        self.caccT = [Trk(f"cacc{i}") for i in range(8)]
        mk("rt2", [128, 4, 8, 8], F32)
        self.rtT = {nm: [Trk(f"rt{nm}{i}") for i in range(4)] for nm in ("q", "k")}
        self.xdteT = [Trk(f"xdte{i}") for i in range(4)]
        self.decT = [Trk(f"dec{i}") for i in range(2)]
        self.mtT = [Trk(f"mt{i}") for i in range(2)]
        mk("ss2", [128, 4], F32)
        mk("junk2", [128, 512], F32)
        for i in range(2):
            fw.op("pool", lambda e, i=i: e.memset(W["va"][i][:, :, 128:130], 1.0), writes=[T["va"][i]])
        fw.op("dve", lambda e: e.memset(W["sst"][0][:], 0.0), writes=[T["sst"][0]])
        fw.op("dve", lambda e: e.memset(W["cbuf"][:], 0.0), writes=[T["cbuf"]])

        self.s_cur = 0
        self._mixer_A(0, "p")
        for t in range(self.NT):
            if t + 1 < self.NT:
                self._mixer_A(t + 1, "p")
            self._mixer_B(t, "p")
        self._state_out(W["sst"][self.s_cur], T["sst"][self.s_cur], O["ssm_p"])
        cb_v = W["cbuf"][:, :, 0:131].rearrange("p k (s l) -> p k s l", s=1)
        fw.dma("sp", O["conv_p"], cb_v[:, :, :, 0:3], reads=[T["cbuf"]], is_output=True)
        self._mixer_tile(0, "s")

    def _state_out(self, s_ap, s_trk, out_ap):
        fw, C, CT, PS, PT, W, T = self.fw, self.C, self.CT, self.PS, self.PT, self.W1, self.T1
        sv = s_ap.rearrange("p h d -> p (h d)")
        for c4 in range(4):
            fw.op("pe", lambda e, c4=c4: e.transpose(PS[7][:, c4 * 128:(c4 + 1) * 128], sv[:, c4 * 128:(c4 + 1) * 128], C["ident"][:]),
                  reads=[s_trk, CT["ident"]], writes=[PT[7]])
        for c4 in range(4):
            fw.op("act", lambda e, c4=c4: e.copy(W["sout"][:], PS[7][:, c4 * 128:(c4 + 1) * 128]), reads=[PT[7]], writes=[T["sout"]])
            fw.dma("sp", out_ap[c4 * 128:(c4 + 1) * 128, :], W["sout"][:], reads=[T["sout"]], is_output=True)

    _DB = ("dt", "xs", "btm", "bct", "ctm", "zs", "q")

    def _views(self, t, mode, part):
        par = (t if mode == "p" else self.NT) % 2
        W, T = dict(self.W1), dict(self.T1)
        for nm in self._DB:
            W[nm], T[nm] = self.W1[nm][par], self.T1[nm][par]
        if part == "B":
            W["ss"], T["ss"] = self.W1["ss2"], self.T1["ss2"]
            W["junk"], T["junk"] = self.W1["junk2"], self.T1["junk2"]
        return W, T

    def _mixer_tile(self, t, mode):
        self._mixer_A(t, mode)
        self._mixer_B(t, mode)

    def _mixer_A(self, t, mode):
        fw, I, O, S, C, CT, tS = self.fw, self.I, self.O, self.S, self.C, self.CT, self.tS
        PS, PT = self.PS, self.PT
        W, T = self._views(t, mode, "A")
        P = mode == "p"
        L = 64 if P else 32
        NCH = 2 if P else 4
        NSEG, SL = (1, 128) if P else (4, 32)
        sfx = str(L)
        xt, xtT = W["xt"][t % 2], T["xt"][t % 2]
        src = I["x_all"][t * 128:(t + 1) * 128, :] if P else I["x_smp"]
        fw.dma("sp", xt[:], src, writes=[xtT])
        fw.op("dve", lambda e: e.memset(W["ss"][:, 0:1], 0.0), writes=[T["ss"]])
        fw.op("act", lambda e: e.activation(W["junk"][:], xt[:], AF.Square, accum_out=W["ss"][:, 0:1]),
              reads=[xtT], writes=[T["junk"], T["ss"]])
        fw.op("act", lambda e: e.activation(W["ss"][:, 1:2], W["ss"][:, 0:1], AF.Ln, bias=C["cst"][:, 1:2], scale=1.0),
              reads=[T["ss"], CT["cst"]], writes=[T["ss"]])
        fw.op("act", lambda e: e.activation(W["ss"][:, 2:3], W["ss"][:, 1:2], AF.Exp, scale=-0.5), reads=[T["ss"]], writes=[T["ss"]])
        fw.op("dve", lambda e: e.scalar_tensor_tensor(W["hb"][:], xt[:], W["ss"][:, 2:3], C["g_mix"][:], ALU.mult, ALU.mult),
              reads=[xtT, T["ss"], CT["g_mix"]], writes=[T["hb"]])
        psb = PS[0][:].bitcast(BF16)
        for k in range(8):
            fw.op("pe", lambda e, k=k: e.transpose(psb[:, k * 128:(k + 1) * 128], W["hb"][:, k * 128:(k + 1) * 128], C["identb"][:]),
                  reads=[T["hb"], CT["identb"]], writes=[PT[0]])
        fw.op("act", lambda e: e.copy(W["hT"][:].rearrange("p k n -> p (k n)"), psb[:, :]), reads=[PT[0]], writes=[T["hT"]])
        wi = C["w_in"]

        def proj_tm(bank, c0, n):
            for k in range(8):
                fw.op("pe", lambda e, k=k: e.matmul(PS[bank][:, 0:n], W["hT"][:, k, :], wi[:, k, c0:c0 + n], start=(k == 0), stop=(k == 7)),
                      reads=[T["hT"], CT["w_in"]], writes=[PT[bank]])

        for ck in range(8):
            bank = 1 + ck // 4
            for k in range(8):
                fw.op("pe", lambda e, k=k, ck=ck, bank=bank: e.matmul(
                    PS[bank][:, (ck % 4) * 128:(ck % 4 + 1) * 128], wi[:, k, C_XBC + ck * 128:C_XBC + (ck + 1) * 128], W["hT"][:, k, :],
                    start=(k == 0), stop=(k == 7)), reads=[T["hT"], CT["w_in"]], writes=[PT[bank]])
        proj_tm(3, C_Z, 512)
        proj_tm(4, C_Q, 512)
        proj_tm(5, C_K, 512)
        proj_tm(6, C_V, 512)
        for k in range(8):
            fw.op("pe", lambda e, k=k: e.matmul(PS[7][:, 0:8], W["hT"][:, k, :], wi[:, k, C_DT:C_DT + 8], start=(k == 0), stop=(k == 7)),
                  reads=[T["hT"], CT["w_in"]], writes=[PT[7]])
        cb_v = W["cbuf"][:, :, 0:NSEG * (SL + 3)].rearrange("p k (s l) -> p k s l", s=NSEG)
        if not P:
            fw.dma("sp", cb_v[:, :, :, 0:3], I["conv_T"], writes=[T["cbuf"]])
        for hb2 in range(2):
            fw.op("act", lambda e, hb2=hb2: e.copy(
                cb_v[:, 4 * hb2:4 * hb2 + 4, :, 3:3 + SL],
                PS[1 + hb2][:].rearrange("p (k s l) -> p k s l", k=4, s=NSEG)), reads=[PT[1 + hb2]], writes=[T["cbuf"]])
        acc_v = W["cacc"][:].rearrange("p k (s l) -> p k s l", s=NSEG)
        cT = self.caccT
        for ck in range(8):
            fw.op("dve", lambda e, ck=ck: e.tensor_scalar(acc_v[:, ck], cb_v[:, ck, :, 0:SL], C["conv_wT"][:, ck, 0:1], C["conv_bT"][:, ck:ck + 1],
                                                          ALU.mult, ALU.add),
                  reads=[T["cbuf"], CT["conv_wT"], CT["conv_bT"]], writes=[cT[ck]])
        for j in range(1, 4):
            for ck in range(8):
                fw.op("dve", lambda e, ck=ck, j=j: e.scalar_tensor_tensor(acc_v[:, ck], cb_v[:, ck, :, j:j + SL], C["conv_wT"][:, ck, j:j + 1],
                                                                         acc_v[:, ck], ALU.mult, ALU.add),
                      reads=[T["cbuf"], CT["conv_wT"], cT[ck]], writes=[cT[ck]])
        fw.op("act", lambda e: e.activation(W["xc"][:], W["cacc"][:], AF.Silu), reads=list(cT), writes=[T["xc"]])
        fw.op("act", lambda e: e.activation(W["zs"][:], PS[3][:], AF.Silu), reads=[PT[3]], writes=[T["zs"]])
        if P:
            fw.op("dve", lambda e: e.tensor_copy(cb_v[:, :, :, 0:3], cb_v[:, :, :, SL:SL + 3]), reads=[T["cbuf"]], writes=[T["cbuf"]])
        else:
            fw.dma("sp", O["conv_s"], cb_v[:, :, :, SL:SL + 3], reads=[T["cbuf"]], is_output=True)
        fw.op("act", lambda e: e.copy(W["bct"][:], W["xc"][:, 4:8, :]), reads=[T["xc"]], writes=[T["bct"]])
        fw.op("dve", lambda e: e.tensor_tensor(W["ctm"][:, 0:NCH], W["xc"][:, 6:8, :].unsqueeze(1).to_broadcast([128, NCH, 2, 128]),
                                                C["cm" + sfx][:].unsqueeze(2).to_broadcast([128, NCH, 2, 128]), ALU.mult),
              reads=[T["xc"], CT["cm" + sfx]], writes=[T["ctm"]])
        for ck in range(6):
            bank = 1 + ck // 4
            fw.op("pe", lambda e, ck=ck, bank=bank: e.transpose(PS[bank][:, (ck % 4) * 128:(ck % 4 + 1) * 128], W["xc"][:, ck, :], C["ident"][:]),
                  reads=[T["xc"], CT["ident"]], writes=[PT[bank]])
        fw.op("act", lambda e: e.copy(W["xs"][:], PS[1][:]), reads=[PT[1]], writes=[T["xs"]])
        fw.op("act", lambda e: e.copy(W["btm"][:], PS[2][:, 0:256]), reads=[PT[2]], writes=[T["btm"]])
        fw.op("dve", lambda e: e.tensor_tensor(W["dt"][:], PS[7][:, 0:8], C["dt_bias"][:], ALU.add), reads=[PT[7], CT["dt_bias"]], writes=[T["dt"]])
        fw.op("act", lambda e: e.activation(W["dt"][:], W["dt"][:], AF.Exp), reads=[T["dt"]], writes=[T["dt"]])
        fw.op("act", lambda e: e.activation(W["dt"][:], W["dt"][:], AF.Ln, bias=C["cst"][:, 0:1], scale=1.0), reads=[T["dt"], CT["cst"]], writes=[T["dt"]])
        fw.op("act", lambda e: e.copy(W["q"][:], PS[4][:]), reads=[PT[4]], writes=[T["q"]])
        fw.op("dve", lambda e: e.tensor_copy(W["k"][:], PS[5][:]), reads=[PT[5]], writes=[T["k"]])
        fw.op("act", lambda e: e.copy(W["v"][:], PS[6][:]), reads=[PT[6]], writes=[T["v"]])
        cos = C["cos_p"][:, t, :] if P else C["cos_s"][:]
        sin = C["sin_p"][:, t, :] if P else C["sin_s"][:]
        cosb = cos.unsqueeze(1).to_broadcast([128, 8, 8])
        sinb = sin.unsqueeze(1).to_broadcast([128, 8, 8])
        tcs = CT["cos_p"] if P else CT["cos_s"]
        tsn = CT["sin_p"] if P else CT["sin_s"]
        rp = {}
        for nm, rtn in (("q", "rt"), ("k", "rt2")):
            v3 = W[nm][:].rearrange("p (g d) -> p g d", g=8)
            rp[nm] = (v3[:, :, 0:8], v3[:, :, 8:16], W[rtn], self.rtT[nm])
        for step in range(6):
            for nm in ("q", "k"):
                t1, t2, rt, rT = rp[nm]
                if step == 0:
                    fw.op("dve", lambda e, t1=t1, rt=rt: e.tensor_tensor(rt[:, 0], t1, cosb, ALU.mult), reads=[T[nm], tcs], writes=[rT[0]])
                elif step == 1:
                    fw.op("dve", lambda e, t2=t2, rt=rt: e.tensor_tensor(rt[:, 1], t2, sinb, ALU.mult), reads=[T[nm], tsn], writes=[rT[1]])
                elif step == 2:
                    fw.op("dve", lambda e, t2=t2, rt=rt: e.tensor_tensor(rt[:, 2], t2, cosb, ALU.mult), reads=[T[nm], tcs], writes=[rT[2]])
                elif step == 3:
                    fw.op("dve", lambda e, t1=t1, rt=rt: e.tensor_tensor(rt[:, 3], t1, sinb, ALU.mult), reads=[T[nm], tsn], writes=[rT[3]])
                elif step == 4:
                    fw.op("dve", lambda e, t1=t1, rt=rt: e.tensor_tensor(t1, rt[:, 0], rt[:, 1], ALU.subtract), reads=[rT[0], rT[1], rT[3]], writes=[T[nm]])
                else:
                    fw.op("dve", lambda e, t2=t2, rt=rt: e.tensor_tensor(t2, rt[:, 2], rt[:, 3], ALU.add), reads=[rT[2], rT[3], rT[1]], writes=[T[nm]])
        ko = O["newk"][t * 128:(t + 1) * 128, :] if P else O["newk_s"]
        vo = O["newv"][t * 128:(t + 1) * 128, :] if P else O["newv_s"]
        fw.dma("sp", ko, W["k"][:], reads=[T["k"]], is_output=True)
        fw.dma("sp", vo, W["v"][:], reads=[T["v"]], is_output=True)
        for h in range(4):
            fw.op("pe", lambda e, h=h: e.transpose(PS[5][:, h * 128:(h + 1) * 128], W["k"][:, h * 128:(h + 1) * 128], C["ident"][:]),
                  reads=[T["k"], CT["ident"]], writes=[PT[5]])
        fw.op("act", lambda e: e.copy(W["ktt"][:].rearrange("p h n -> p (h n)"), PS[5][:]), reads=[PT[5]], writes=[T["ktt"]])
        va, vaT = W["va"][t % 2], T["va"][t % 2]
        fw.op("act", lambda e: e.copy(va[:, :, 0:128], W["v"][:].rearrange("p (h d) -> p h d", h=4)), reads=[T["v"]], writes=[vaT])
        if P:
            fw.dma("sp", S["KT"][:, :, t * 128:(t + 1) * 128].rearrange("h p n -> p h n"), W["ktt"][:], reads=[T["ktt"]], writes=[tS["KT"]])
            fw.dma("sp", S["VA"][:, t, :, :].rearrange("h p n -> p h n"), va[:], reads=[vaT], writes=[tS["VA"]])
        else:
            fw.dma("sp", S["KT_s"].rearrange("h p n -> p h n"), W["ktt"][:], reads=[T["ktt"]], writes=[tS["KT_s"]])
            fw.dma("sp", S["VA_s"].rearrange("h p n -> p h n"), va[:], reads=[vaT], writes=[tS["VA_s"]])
    def _mixer_B(self, t, mode):
        fw, I, O, S, C, CT, tS = self.fw, self.I, self.O, self.S, self.C, self.CT, self.tS
        PS, PT = self.PS, self.PT
        W, T = self._views(t, mode, "B")
        P = mode == "p"
        L = 64 if P else 32
        NCH = 2 if P else 4
        sfx = str(L)
        xs3 = W["xs"][:].rearrange("p (h d) -> p h d", h=8)
        tri, uu, onb, sel, rm = C["tri" + sfx], C["u" + sfx], C["onb" + sfx], C["sel" + sfx], C["rm" + sfx]
        ctri, cu, conb, csel, crm = CT["tri" + sfx], CT["u" + sfx], CT["onb" + sfx], CT["sel" + sfx], CT["rm" + sfx]
        fw.op("dve", lambda e: e.tensor_tensor(W["da"][:], W["dt"][:], C["a"][:], ALU.mult), reads=[T["dt"], CT["a"]], writes=[T["da"]])
        fw.op("pe", lambda e: e.matmul(PS[7][:, 0:8], tri[:], W["da"][:], start=True, stop=True), reads=[T["da"], ctri], writes=[PT[7]])
        fw.op("pe", lambda e: e.matmul(PS[7][:, 8:16], onb[:], W["da"][:], start=True, stop=True), reads=[T["da"], conb], writes=[PT[7]])
        for ch in range(NCH):
            fw.op("pe", lambda e, ch=ch: e.matmul(PS[7][:, 16 + 8 * ch:24 + 8 * ch], sel[:, ch, :], W["da"][:], start=True, stop=True),
                  reads=[T["da"], csel], writes=[PT[7]])
        nex = 16 + 8 * NCH
        ex = W["ex"]
        fw.op("dve", lambda e: e.tensor_copy(ex[:, 0:nex], PS[7][:, 0:nex]), reads=[PT[7]], writes=[T["ex"]])
        fw.op("dve", lambda e: e.tensor_tensor(ex[:, 8:16], ex[:, 8:16], ex[:, 0:8], ALU.subtract), reads=[T["ex"]], writes=[T["ex"]])
        fw.op("act", lambda e: e.activation(ex[:, 0:nex], ex[:, 0:nex], AF.Exp), reads=[T["ex"]], writes=[T["ex"]])
        fw.op("dve", lambda e: e.tensor_tensor(W["wch"][:, 0, :], W["dt"][:], ex[:, 8:16], ALU.mult), reads=[T["dt"], T["ex"]], writes=[T["wch"]])
        for ch in range(NCH - 1, -1, -1):
            fw.op("dve", lambda e, ch=ch: e.tensor_scalar(W["wch"][:, ch, :], W["wch"][:, 0, :], rm[:, ch:ch + 1], None, ALU.mult),
                  reads=[T["wch"], crm], writes=[T["wch"]])
        xs3 = W["xs"][:].rearrange("p (h d) -> p h d", h=8)
        fw.op("dve", lambda e: e.tensor_tensor(W["xdt"][:], xs3, W["dt"][:].unsqueeze(2).to_broadcast([128, 8, 64]), ALU.mult),
              reads=[T["xs"], T["dt"]], writes=[T["xdt"]])
        for ch in range(NCH):
            eng = "dve"
            fw.op(eng, lambda e, ch=ch: e.tensor_tensor(W["xdte"][:, ch, :].rearrange("p (h d) -> p h d", h=8), xs3,
                                                        W["wch"][:, ch, :].unsqueeze(2).to_broadcast([128, 8, 64]), ALU.mult),
                  reads=[T["xs"], T["wch"]], writes=[self.xdteT[ch]])
        for g in range(2):
            fw.op("pe", lambda e, g=g: e.matmul(PS[1][:, g * 128:(g + 1) * 128], W["bct"][:, g, :], W["bct"][:, 2 + g, :], start=True, stop=True),
                  reads=[T["bct"]], writes=[PT[1]])
        fw.op("dve", lambda e: e.tensor_tensor(W["cbm"][:], PS[1][:, 0:256].rearrange("p (g n) -> p g n", g=2),
                                               tri[:].unsqueeze(1).to_broadcast([128, 2, 128]), ALU.mult), reads=[PT[1], ctri], writes=[T["cbm"]])
        fw.op("dve", lambda e: e.tensor_tensor(W["dau"][:], uu[:].unsqueeze(1).to_broadcast([128, 8, 128]),
                                                W["da"][:].unsqueeze(2).to_broadcast([128, 8, 128]), ALU.mult), reads=[cu, T["da"]], writes=[T["dau"]])
        for h in range(8):
            bank = 2 + h // 4
            fw.op("pe", lambda e, h=h, bank=bank: e.matmul(PS[bank][:, (h % 4) * 128:(h % 4 + 1) * 128], W["dau"][:, h, :], tri[:], start=True, stop=True),
                  reads=[T["dau"], ctri], writes=[PT[bank]])
        for g in range(2):
            fw.op("act", lambda e, g=g: e.activation(W["dec"][:, 4 * g:4 * g + 4, :].rearrange("p h n -> p (h n)"), PS[2 + g][:], AF.Exp),
                  reads=[PT[2 + g]], writes=[self.decT[g]])
        for g in range(2):
            fw.op("dve", lambda e, g=g: e.tensor_tensor(W["mt"][:, 4 * g:4 * g + 4, :], W["dec"][:, 4 * g:4 * g + 4, :],
                                                        W["cbm"][:, g, :].unsqueeze(1).to_broadcast([128, 4, 128]), ALU.mult),
                  reads=[self.decT[g], T["cbm"]], writes=[self.mtT[g]])
        for h in range(8):
            fw.op("pe", lambda e, h=h: e.matmul(PS[4][:, h * 64:(h + 1) * 64], W["mt"][:, h, :], W["xdt"][:, h, :], start=True, stop=True),
                  reads=[self.mtT[h // 4], T["xdt"]], writes=[PT[4]])
        if P:
            s_in, s_inT = W["sst"][self.s_cur], T["sst"][self.s_cur]
        else:
            fw.dma("sp", W["h0"][:], I["ssm_T"].rearrange("s p f -> p s f"), writes=[T["h0"]])
        for ch in range(NCH):
            if P:
                src_ap, src_trk = s_in[:], s_inT
            else:
                src_ap, src_trk = W["h0"][:, ch, :].rearrange("p (h d) -> p h d", h=8), T["h0"]
            fw.op("act", lambda e, ch=ch, src_ap=src_ap: e.copy(W["sbf"][:, ch, :].rearrange("p (h d) -> p h d", h=8), src_ap),
                  reads=[src_trk], writes=[T["sbf"]])
            for h in range(8):
                g = h // 4
                fw.op("pe", lambda e, h=h, g=g, ch=ch: e.matmul(PS[6][:, h * 64:(h + 1) * 64], W["btm"][:, g * 128:(g + 1) * 128],
                                                                W["xdte"][:, ch, h * 64:(h + 1) * 64], start=True, stop=True),
                      reads=[T["btm"], self.xdteT[ch]], writes=[PT[6]])
            cdb = ex[:, 16 + 8 * ch:24 + 8 * ch].unsqueeze(2).to_broadcast([128, 8, 64])
            fw.op("dve", lambda e, src_ap=src_ap, cdb=cdb: e.tensor_tensor(W["stmp"][:], src_ap, cdb, ALU.mult), reads=[src_trk, T["ex"]], writes=[T["stmp"]])
            if P:
                nxt = 1 - self.s_cur
                fw.op("dve", lambda e, nxt=nxt: e.tensor_tensor(W["sst"][nxt][:], W["stmp"][:], PS[6][:].rearrange("p (h d) -> p h d", h=8), ALU.add),
                      reads=[T["stmp"], PT[6]], writes=[T["sst"][nxt]])
                self.s_cur = nxt
                s_in, s_inT = W["sst"][nxt], T["sst"][nxt]
            else:
                fw.op("dve", lambda e: e.tensor_tensor(W["stmp"][:], W["stmp"][:], PS[6][:].rearrange("p (h d) -> p h d", h=8), ALU.add),
                      reads=[T["stmp"], PT[6]], writes=[T["stmp"]])
                self._state_out(W["stmp"][:], T["stmp"], O["ssm_s"][ch])
        for h in range(8):
            g = h // 4
            for ch in range(NCH):
                fw.op("pe", lambda e, h=h, g=g, ch=ch: e.matmul(PS[5][:, h * 64:(h + 1) * 64], W["ctm"][:, ch, g, :], W["sbf"][:, ch, h * 64:(h + 1) * 64],
                                                                start=(ch == 0), stop=(ch == NCH - 1)),
                      reads=[T["ctm"], T["sbf"]], writes=[PT[5]])
        y3 = W["y"][:].rearrange("p (h d) -> p h d", h=8)
        fw.op("dve", lambda e: e.tensor_tensor(W["ytmp"][:], PS[5][:].rearrange("p (h d) -> p h d", h=8),
                                               ex[:, 0:8].unsqueeze(2).to_broadcast([128, 8, 64]), ALU.mult), reads=[PT[5], T["ex"]], writes=[T["ytmp"]])
        fw.op("dve", lambda e: e.tensor_tensor(y3, W["ytmp"][:], PS[4][:].rearrange("p (h d) -> p h d", h=8), ALU.add),
              reads=[T["ytmp"], PT[4]], writes=[T["y"]])
        fw.op("dve", lambda e: e.tensor_tensor(W["ytmp"][:], xs3, C["dskb"][:], ALU.mult), reads=[T["xs"], CT["dskb"]], writes=[T["ytmp"]])
        fw.op("dve", lambda e: e.tensor_tensor(y3, y3, W["ytmp"][:], ALU.add), reads=[T["ytmp"], T["y"]], writes=[T["y"]])
        fw.op("dve", lambda e: e.tensor_tensor(W["y"][:], W["y"][:], W["zs"][:], ALU.mult), reads=[T["y"], T["zs"]], writes=[T["y"]])
        fw.op("dve", lambda e: e.memset(W["ss"][:, 0:1], 0.0), writes=[T["ss"]])
        fw.op("act", lambda e: e.activation(W["junk"][:, 0:512], W["y"][:], AF.Square, accum_out=W["ss"][:, 0:1]),
              reads=[T["y"]], writes=[T["junk"], T["ss"]])
        fw.op("act", lambda e: e.activation(W["ss"][:, 1:2], W["ss"][:, 0:1], AF.Ln, bias=C["cst"][:, 2:3], scale=1.0),
              reads=[T["ss"], CT["cst"]], writes=[T["ss"]])
        fw.op("act", lambda e: e.activation(W["ss"][:, 2:3], W["ss"][:, 1:2], AF.Exp, scale=-0.5), reads=[T["ss"]], writes=[T["ss"]])
        fw.op("dve", lambda e: e.scalar_tensor_tensor(W["yn"][:], W["y"][:], W["ss"][:, 2:3], C["g_ssd"][:], ALU.mult, ALU.mult),
              reads=[T["y"], T["ss"], CT["g_ssd"]], writes=[T["yn"]])
        if P:
            r, i = t % 4, t // 4
            sj = C["selj"]
            for src, dst in (("q", "ownq"), ("yn", "ownyn")):
                if r == 0:
                    fw.op("dve", lambda e, src=src, dst=dst: e.tensor_scalar(W[dst][:], W[src][:], sj[:, 0:1], None, ALU.mult),
                          reads=[T[src], CT["selj"]], writes=[T[dst]])
                else:
                    fw.op("dve", lambda e, src=src, dst=dst, r=r: e.scalar_tensor_tensor(W[dst][:], W[src][:], sj[:, r:r + 1], W[dst][:], ALU.mult, ALU.add),
                          reads=[T[src], CT["selj"], T[dst]], writes=[T[dst]])
            if r == 3:
                fw.dma("sp", S["q_own"][i], W["ownq"][:], reads=[T["ownq"]], writes=[tS["q_own"]])
                fw.dma("sp", S["yn_own"][i], W["ownyn"][:], reads=[T["ownyn"]], writes=[tS["yn_own"]])
        else:
            fw.dma("sp", S["q_smp"], W["q"][:], reads=[T["q"]], writes=[tS["q_smp"]])
            fw.dma("sp", S["yn_smp"], W["yn"][:], reads=[T["yn"]], writes=[tS["yn_smp"]])

    def _phase2(self):
        fw, I, O, S, C, CT, tS = self.fw, self.I, self.O, self.S, self.C, self.CT, self.tS
        PS, PT = self.PS, self.PT
        SEQ, NT, NOWN, NKP = self.SEQ, self.NT, self.NOWN, self.NKP
        W, T = {}, {}

        def mk(name, shape, dt=F32, n=1):
            if n == 1:
                W[name] = fw.sb("p2_" + name, shape, dt); T[name] = Trk(name)
            else:
                W[name] = [fw.sb(f"p2_{name}{i}", shape, dt) for i in range(n)]
                T[name] = [Trk(f"{name}{i}") for i in range(n)]

        mk("lam", [128, 4, 64]); mk("lt", [128, 2, 64]); mk("lv", [128, 8])
        mk("gsub", [128, 128])
        mk("amp", [128, 4, 128], BF16); mk("ams", [128, 5, 128], BF16)
        mk("qin", [128, 512], F32, 2)
        mk("QT", [128, 4, NOWN * 128], BF16); mk("QTs", [128, 4, 128], BF16)
        mk("kt", [128, SEQ], BF16); mk("va", [128, NT, 130], BF16)
        mk("kts", [128, max(self.PAST, 128)], BF16, 4); mk("vas", [128, NKP, 130], BF16, 4)
        mk("ktn", [128, 128], BF16); mk("van", [128, 130], BF16)
        mk("rec", [128, 4]); mk("o", [128, 128]); mk("junk", [128, 128]); mk("ss", [128, 4])
        mk("on", [128, 128], F32, 2)
        fw.dma("sp", W["lam"][:], I["lam"], writes=[T["lam"]])
        fw.dma("sp", W["gsub"][:], I["g_subln"], writes=[T["gsub"]])
        fw.dma("pool", W["amp"][:], I["amask_p"], writes=[T["amp"]])
        fw.dma("pool", W["ams"][:], I["amask_s"], writes=[T["ams"]])
        fw.op("dve", lambda e: e.tensor_scalar(W["gsub"][:], W["gsub"][:], math.sqrt(128.0) * (1.0 - LAMBDA_INIT), None, ALU.mult),
              reads=[T["gsub"]], writes=[T["gsub"]])
        for i in range(2):
            fw.op("dve", lambda e, i=i: e.tensor_tensor(W["lt"][:, i, :], W["lam"][:, 2 * i, :], W["lam"][:, 2 * i + 1, :], ALU.mult),
                  reads=[T["lam"]], writes=[T["lt"]])
            fw.op("dve", lambda e, i=i: e.reduce_sum(W["lv"][:, i:i + 1], W["lt"][:, i, :], mybir.AxisListType.X), reads=[T["lt"]], writes=[T["lv"]])
        fw.op("act", lambda e: e.activation(W["lv"][:, 2:4], W["lv"][:, 0:2], AF.Exp), reads=[T["lv"]], writes=[T["lv"]])
        fw.op("dve", lambda e: e.tensor_tensor(W["lv"][:, 4:5], W["lv"][:, 2:3], W["lv"][:, 3:4], ALU.subtract), reads=[T["lv"]], writes=[T["lv"]])
        fw.op("dve", lambda e: e.tensor_scalar(W["lv"][:, 4:5], W["lv"][:, 4:5], LAMBDA_INIT, None, ALU.add), reads=[T["lv"]], writes=[T["lv"]])
        fw.op("dve", lambda e: e.tensor_scalar(W["lv"][:, 5:6], W["lv"][:, 4:5], -1.0, None, ALU.mult), reads=[T["lv"]], writes=[T["lv"]])
        for i in range(4):
            fw.op("pool", lambda e, i=i: e.memset(W["vas"][i][:, :, 128:130], 1.0), writes=[T["vas"][i]])
        LV = 9
        if LV < 2:
            return
        def load_qT(src_ap, src_trk, dst_ap, dst_trk, n):
            qin, qinT = W["qin"][n % 2], T["qin"][n % 2]
            fw.dma("sp", qin[:], src_ap, reads=[src_trk], writes=[qinT])
            for h in range(4):
                fw.op("pe", lambda e, h=h: e.transpose(PS[6][:, h * 128:(h + 1) * 128], qin[:, h * 128:(h + 1) * 128], C["ident"][:]),
                      reads=[qinT, CT["ident"]], writes=[PT[6]])
            fw.op("act", lambda e: e.mul(dst_ap, PS[6][:].rearrange("p (h n) -> p h n", h=4), 0.125), reads=[PT[6]], writes=[dst_trk])

        for i in range(NOWN):
            load_qT(S["q_own"][i], tS["q_own"], W["QT"][:, :, i * 128:(i + 1) * 128], T["QT"], i)
        load_qT(S["q_smp"], tS["q_smp"], W["QTs"][:], T["QTs"], NOWN)

        self._acnt = 0
        mk("qblk", [128, 256], BF16, 2)
        mk("E3", [128, 512], BF16, 3)
        for i in range(2):
            fw.op("dve", lambda e, i=i: e.memset(W["qblk"][i][:], 0.0), writes=[T["qblk"][i]])

        def attn(qT_fn, qT_trk, tiles, out_ap, out_trk):
            n = self._acnt
            self._acnt += 1
            ob = 4 + 2 * (n % 2)
            qb, qbT = W["qblk"][n % 2], T["qblk"][n % 2]
            for c in range(2):
                fw.op("dve", lambda e, c=c: e.tensor_copy(qb[64 * c:64 * c + 64, 128 * c:128 * c + 128], qT_fn(c)), reads=[qT_trk], writes=[qbT])
            ntl = len(tiles)
            groups = [tiles[g0:g0 + 2] for g0 in range(0, ntl, 2)]

            def emit_qk(g):
                bk = g % 4
                for kk, (kt_ap, va_ap, trks, m_ap, m_trk) in enumerate(groups[g]):
                    fw.op("pe", lambda e, kk=kk, bk=bk, kt_ap=kt_ap: e.matmul(PS[bk][:, kk * 256:(kk + 1) * 256], kt_ap, qb[:, :], start=True, stop=True),
                          reads=list(trks) + [qbT], writes=[PT[bk]])

            def emit_rest(g):
                bk = g % 4
                grp = groups[g]
                E, ET = W["E3"][g % 3], T["E3"][g % 3]
                ncol = 256 * len(grp)
                fw.op("act", lambda e: e.activation(E[:, 0:ncol], PS[bk][:, 0:ncol], AF.Exp), reads=[PT[bk]], writes=[ET])
                for kk, (kt_ap, va_ap, trks, m_ap, m_trk) in enumerate(grp):
                    if m_ap is not None:
                        ev = E[:, kk * 256:(kk + 1) * 256].rearrange("p (c n) -> p c n", c=2)
                        fw.op("dve", lambda e, ev=ev, m_ap=m_ap: e.tensor_tensor(ev, ev, m_ap.unsqueeze(1).to_broadcast([128, 2, 128]), ALU.mult),
                              reads=[ET, m_trk], writes=[ET])
                for kk, (kt_ap, va_ap, trks, m_ap, m_trk) in enumerate(grp):
                    ti = 2 * g + kk
                    for c in range(2):
                        fw.op("pe", lambda e, kk=kk, c=c, va_ap=va_ap, ti=ti: e.matmul(
                            PS[ob + c][:, 0:130], E[:, (kk * 2 + c) * 128:(kk * 2 + c + 1) * 128], va_ap, start=(ti == 0), stop=(ti == ntl - 1)),
                              reads=[ET] + list(trks), writes=[PT[ob + c]])

            emit_qk(0)
            for g in range(len(groups)):
                if g + 1 < len(groups):
                    emit_qk(g + 1)
                emit_rest(g)
            fw.op("dve", lambda e: e.reciprocal(W["rec"][:, 0:1], PS[ob][:, 128:129]), reads=[PT[ob]], writes=[T["rec"]])
            fw.op("dve", lambda e: e.reciprocal(W["rec"][:, 1:2], PS[ob + 1][:, 128:129]), reads=[PT[ob + 1]], writes=[T["rec"]])
            fw.op("dve", lambda e: e.tensor_tensor(W["rec"][:, 2:3], W["rec"][:, 1:2], W["lv"][:, 5:6], ALU.mult), reads=[T["rec"], T["lv"]], writes=[T["rec"]])
            fw.op("dve", lambda e: e.tensor_scalar(W["o"][:], PS[ob][:, 0:128], W["rec"][:, 0:1], None, ALU.mult), reads=[PT[ob], T["rec"]], writes=[T["o"]])
            fw.op("dve", lambda e: e.scalar_tensor_tensor(W["o"][:], PS[ob + 1][:, 0:128], W["rec"][:, 2:3], W["o"][:], ALU.mult, ALU.add),
                  reads=[PT[ob + 1], T["rec"], T["o"]], writes=[T["o"]])
            fw.op("dve", lambda e: e.memset(W["ss"][:, 0:1], 0.0), writes=[T["ss"]])
            fw.op("act", lambda e: e.activation(W["junk"][:], W["o"][:], AF.Square, accum_out=W["ss"][:, 0:1]), reads=[T["o"]], writes=[T["junk"], T["ss"]])
            fw.op("act", lambda e: e.activation(W["ss"][:, 1:2], W["ss"][:, 0:1], AF.Ln, bias=C["cst"][:, 3:4], scale=1.0),
                  reads=[T["ss"], CT["cst"]], writes=[T["ss"]])
            fw.op("act", lambda e: e.activation(W["ss"][:, 2:3], W["ss"][:, 1:2], AF.Exp, scale=-0.5), reads=[T["ss"]], writes=[T["ss"]])
            on, onT = W["on"][n % 2], T["on"][n % 2]
            fw.op("dve", lambda e: e.scalar_tensor_tensor(on[:], W["o"][:], W["ss"][:, 2:3], W["gsub"][:], ALU.mult, ALU.mult),
                  reads=[T["o"], T["ss"], T["gsub"]], writes=[onT])
            fw.dma("sp", out_ap, on[:], reads=[onT], writes=[out_trk])

        if LV < 3:
            return
        for h in range(4):
            fw.dma("sp", W["kt"][:], S["KT"][h], reads=[tS["KT"]], writes=[T["kt"]])
            for t0 in range(0, NT, 16):
                t1 = min(NT, t0 + 16)
                fw.dma("sp", W["va"][:, t0:t1, :], S["VA"][h, t0:t1].rearrange("t p n -> p t n"), reads=[tS["VA"]], writes=[T["va"]])
            for i in range(NOWN):
                tiles = []
                for kt in range(4 * i + 4):
                    r = kt - 4 * i
                    tiles.append((W["kt"][:, kt * 128:(kt + 1) * 128], W["va"][:, kt, :], [T["kt"], T["va"]],
                                  W["amp"][:, r, :] if r >= 0 else None, T["amp"]))
                attn(lambda c, h=h, i=i: W["QT"][64 * c:64 * c + 64, h, i * 128:(i + 1) * 128], T["QT"], tiles,
                     S["o_own"][i][:, h * 128:(h + 1) * 128], tS["o_own"])
        if LV < 4:
            return
        for h in range(4):
            fw.dma("sp", W["ktn"][:], S["KT_s"][h], reads=[tS["KT_s"]], writes=[T["ktn"]])
            fw.dma("sp", W["van"][:], S["VA_s"][h], reads=[tS["VA_s"]], writes=[T["van"]])
            tiles = []
            for s_ in range(4):
                kts, ktsT = W["kts"][s_], T["kts"][s_]
                vas, vasT = W["vas"][s_], T["vas"][s_]
                fw.dma("pool", kts[:, 0:self.PAST], I["ck_T"][s_, h], writes=[ktsT])
                fw.dma("pool", vas[:, :, 0:128], I["cv"][s_][:, h * 128:(h + 1) * 128].rearrange("(t p) d -> p t d", p=128), writes=[vasT])
                stl = [(kts[:, kt * 128:(kt + 1) * 128], vas[:, kt, :], [ktsT, vasT], W["ams"][:, s_, :], T["ams"]) for kt in range(NKP)]
                tiles.extend(stl)
            tiles.append((W["ktn"][:], W["van"][:], [T["ktn"], T["van"]], W["ams"][:, 4, :], T["ams"]))
            attn(lambda c, h=h: W["QTs"][64 * c:64 * c + 64, h, :], T["QTs"], tiles, S["o_smp"][:, h * 128:(h + 1) * 128], tS["o_smp"])

    def _phase3(self):
        fw, I, O, S, C, CT, tS = self.fw, self.I, self.O, self.S, self.C, self.CT, self.tS
        PS, PT = self.PS, self.PT
        NOWN = self.NOWN
        W, T = {}, {}
        AXX = mybir.AxisListType.X

        def mk(name, shape, dt=F32, n=1):
            if n == 1:
                W[name] = fw.sb("p3_" + name, shape, dt); T[name] = Trk(name)
            else:
                W[name] = [fw.sb(f"p3_{name}{i}", shape, dt) for i in range(n)]
                T[name] = [Trk(f"{name}{i}") for i in range(n)]

        for gname in ("g_memq", "g_ffn", "g_final", "g_memkv"):
            mk(gname, [128, D])
            fw.dma("sp", W[gname][:], I[gname], writes=[T[gname]])
            fw.op("dve", lambda e, gname=gname: e.tensor_scalar(W[gname][:], W[gname][:], math.sqrt(D), None, ALU.mult),
                  reads=[T[gname]], writes=[T[gname]])
        mk("wbuf", [128, 8, 1024], BF16, 2)
        mk("keysT", [128, 16, 128], BF16)
        fw.dma("pool", W["keysT"][:], I["keysT"], writes=[T["keysT"]])
        mk("iota", [128, 128])
        fw.dma("sp", W["iota"][:], I["iota"], writes=[T["iota"]])
        mk("mkT", [128, 8, 256], BF16)
        mk("vam", [128, 2, 4, 257], BF16)
        mk("big2", [128, 16640], BF16)
        mk("xres", [128, D], F32, 2)
        mk("tmpA", [128, D], F32)
        mk("hb", [128, D], BF16)
        mk("hT", [128, 8, 256], BF16)
        mk("junk", [128, D], F32)
        mk("ss", [128, 4])
        mk("qmT", [128, 8, 128], BF16)
        mk("em", [128, 8, 128], BF16, 4)
        mk("rec", [128, 4])
        mk("qyT", [128, 16, 256], BF16)
        mk("big", [128, 2048], F32)
        mk("scw", [128, 2048], F32)
        mk("tv", [128, 16, 16]); mk("ti", [128, 16, 16], U32); mk("tif", [128, 16, 16])
        mk("sv", [128, 8, 16]); mk("svx", [128, 8, 16]); mk("si", [128, 8, 16], U32); mk("sif", [128, 8, 16])
        mk("aidx", [128, 8, 16]); mk("bidx", [128, 8, 16]); mk("sm", [128, 8, 2])
        mk("ai", [128, 8, 16], U32); mk("bi", [128, 8, 16], U32)
        mk("ijg", [128, 3, 128])
        mk("ijgT", [128, 3, 256])
        mk("oic", [128, 16, 64], BF16, 2); mk("ojc", [128, 16, 128], BF16, 2)
        mk("uT", [128, 2, 8, 128], BF16, 3); mk("vv", [128, 2, D], BF16, 3)
        mk("ga", [128, 256], F32, 2); mk("ptb", [128, 256], BF16, 2)
        mk("yo", [128, D], F32)
        WT = W["big2"][:, 0:16384].rearrange("p (t i) -> p t i", i=64)
        skT = W["big2"][:, 0:8192].rearrange("p (s c m) -> p s c m", s=4, c=8)
        sVA = W["big2"][:, 8192:8192 + 8224].rearrange("p (s t h n) -> p s t h n", s=4, t=2, h=4)
        for i in range(4):
            fw.op("pool", lambda e, i=i: e.memset(W["em"][i][:], 0.0), writes=[T["em"][i]])
        fw.op("pool", lambda e: e.memset(W["vam"][:, :, :, 256:257], 1.0), writes=[T["vam"]])

        psb = PS[0][:].bitcast(BF16)

        def to_hT(src_ap, src_trk, col0):
            for k in range(8):
                fw.op("pe", lambda e, k=k: e.transpose(psb[:, k * 128:(k + 1) * 128], src_ap[:, k * 128:(k + 1) * 128], C["identb"][:]),
                      reads=[src_trk, CT["identb"]], writes=[PT[0]])
            fw.op("act", lambda e: e.copy(W["hT"][:, :, col0:col0 + 128], psb[:, :].rearrange("p (k n) -> p k n", k=8)), reads=[PT[0]], writes=[T["hT"]])

        def rms_to_hT(x_ap, x_trk, gname, col0):
            fw.op("dve", lambda e: e.memset(W["ss"][:, 0:1], 0.0), writes=[T["ss"]])
            fw.op("act", lambda e: e.activation(W["junk"][:], x_ap, AF.Square, accum_out=W["ss"][:, 0:1]), reads=[x_trk], writes=[T["junk"], T["ss"]])
            fw.op("act", lambda e: e.activation(W["ss"][:, 1:2], W["ss"][:, 0:1], AF.Ln, bias=C["cst"][:, 1:2], scale=1.0),
                  reads=[T["ss"], CT["cst"]], writes=[T["ss"]])
            fw.op("act", lambda e: e.activation(W["ss"][:, 2:3], W["ss"][:, 1:2], AF.Exp, scale=-0.5), reads=[T["ss"]], writes=[T["ss"]])
            fw.op("dve", lambda e: e.scalar_tensor_tensor(W["hb"][:], x_ap, W["ss"][:, 2:3], W[gname][:], ALU.mult, ALU.mult),
                  reads=[x_trk, T["ss"], T[gname]], writes=[T["hb"]])
            to_hT(W["hb"], T["hb"], col0)

        from collections import deque
        wq = deque()
        wloaded = deque()
        wcnt = [0]

        def w_prefetch():
            if not wq:
                return
            name, c0 = wq.popleft()
            buf, trk = W["wbuf"][wcnt[0] % 2], T["wbuf"][wcnt[0] % 2]
            wcnt[0] += 1
            for k in range(8):
                fw.dma("pool", buf[:, k, :], I[name][k * 128:(k + 1) * 128, c0:c0 + 1024], writes=[trk])
            wloaded.append((buf, trk))

        cur_w = [None, None]

        def load_w(name, c0=0, ncols=1024):
            cur_w[0], cur_w[1] = wloaded.popleft()
            w_prefetch()

        def proj_tm(col0, banks=(1, 2)):
            for nh in range(2):
                for k in range(8):
                    fw.op("pe", lambda e, k=k, nh=nh: e.matmul(PS[banks[nh]][:], W["hT"][:, k, col0:col0 + 128], cur_w[0][:, k, nh * 512:(nh + 1) * 512],
                                                               start=(k == 0), stop=(k == 7)), reads=[T["hT"], cur_w[1]], writes=[PT[banks[nh]]])

        wq.extend([("w_mk", 0), ("w_mv", 0)])
        nblk = (NOWN + 1) // 2 + 1
        for _ in range(nblk):
            wq.extend([("w_out", 0), ("w_mq", 0), ("w_mo", 0), ("w_pq", 0), ("w_pq", 1024)])
        w_prefetch()
        for mt in range(2):
            fw.dma("sp", W["tmpA"][:], I["mem_p"][mt * 128:(mt + 1) * 128, :], writes=[T["tmpA"]])
            rms_to_hT(W["tmpA"][:], T["tmpA"], "g_memkv", mt * 128)
        for which, outn in (("w_mk", "memk_p"), ("w_mv", "memv_p")):
            load_w(which)
            for mt in range(2):
                proj_tm(mt * 128)
                for nh in range(2):
                    fw.op("act", lambda e, nh=nh: e.copy(W["yo"][:, nh * 512:(nh + 1) * 512], PS[1 + nh][:]), reads=[PT[1 + nh]], writes=[T["yo"]])
                fw.dma("sp", O[outn][mt * 128:(mt + 1) * 128, :], W["yo"][:], reads=[T["yo"]], is_output=True)
                if which == "w_mv":
                    fw.op("dve", lambda e, mt=mt: e.tensor_copy(W["vam"][:, mt, :, 0:256], W["yo"][:].rearrange("p (h d) -> p h d", h=4)),
                          reads=[T["yo"]], writes=[T["vam"]])
            if which == "w_mk":
                for oc in range(8):
                    bk = 3 + oc % 2
                    for k in range(8):
                        fw.op("pe", lambda e, k=k, oc=oc, bk=bk: e.matmul(PS[bk][:, 0:256], cur_w[0][:, k, oc * 128:(oc + 1) * 128], W["hT"][:, k, 0:256],
                                                                          start=(k == 0), stop=(k == 7)), reads=[T["hT"], cur_w[1]], writes=[PT[bk]])
                    fw.op("act", lambda e, oc=oc, bk=bk: e.copy(W["mkT"][:, oc, :], PS[bk][:, 0:256]), reads=[PT[bk]], writes=[T["mkT"]])

        def block(tiles, smp):
            nt = len(tiles)
            Tn = nt * 128
            load_w("w_out")
            for ti, tl in enumerate(tiles):
                xr, xrT = W["xres"][ti], T["xres"][ti]
                if smp:
                    fw.dma("sp", xr[:], I["x_smp"], writes=[xrT])
                    fw.dma("sp", W["tmpA"][:, 0:512], S["yn_smp"], reads=[tS["yn_smp"]], writes=[T["tmpA"]])
                    fw.dma("sp", W["tmpA"][:, 512:1024], S["o_smp"], reads=[tS["o_smp"]], writes=[T["tmpA"]])
                else:
                    fw.dma("sp", xr[:], I["x_own"][tl * 128:(tl + 1) * 128, :], writes=[xrT])
                    fw.dma("sp", W["tmpA"][:, 0:512], S["yn_own"][tl], reads=[tS["yn_own"]], writes=[T["tmpA"]])
                    fw.dma("sp", W["tmpA"][:, 512:1024], S["o_own"][tl], reads=[tS["o_own"]], writes=[T["tmpA"]])
                fw.op("act", lambda e: e.copy(W["hb"][:], W["tmpA"][:]), reads=[T["tmpA"]], writes=[T["hb"]])
                to_hT(W["hb"], T["hb"], ti * 128)
                proj_tm(ti * 128)
                for nh in range(2):
                    fw.op("dve", lambda e, nh=nh: e.tensor_tensor(xr[:, nh * 512:(nh + 1) * 512], xr[:, nh * 512:(nh + 1) * 512], PS[1 + nh][:], ALU.add),
                          reads=[xrT, PT[1 + nh]], writes=[xrT])
                if self.debug:
                    dst = S["x1_smp"] if smp else S["x1_own"][tl]
                    fw.dma("sp", dst, xr[:], reads=[xrT], writes=[tS["x1_smp" if smp else "x1_own"]])
            load_w("w_mq")
            if smp:
                for s_ in range(4):
                    fw.dma("pool", skT[:, s_], I["cmk_T"][s_].rearrange("h k p m -> p (h k) m"), writes=[T["big2"]])
                    for mt in range(2):
                        fw.dma("pool", sVA[:, s_, mt, :, 0:256], I["cmv"][s_][mt * 128:(mt + 1) * 128, :].rearrange("p (h d) -> p h d", h=4), writes=[T["big2"]])
                fw.op("pool", lambda e: e.memset(sVA[:, :, :, :, 256:257], 1.0), writes=[T["big2"]])
            groups = [(s_, 32 * s_, 32) for s_ in range(4)] if smp else [(0, 0, 128)]
            if smp:
                for i in range(4):
                    fw.op("pool", lambda e, i=i: e.memset(W["em"][i][:], 0.0), writes=[T["em"][i]])
            for ti, tl in enumerate(tiles):
                xr, xrT = W["xres"][ti], T["xres"][ti]
                rms_to_hT(xr[:], xrT, "g_memq", ti * 128)
            for ti, tl in enumerate(tiles):
                for oc in range(8):
                    bk = 3 + oc // 4
                    for k in range(8):
                        fw.op("pe", lambda e, k=k, oc=oc, bk=bk: e.matmul(PS[bk][:, (oc % 4) * 128:(oc % 4 + 1) * 128], cur_w[0][:, k, oc * 128:(oc + 1) * 128],
                                                                          W["hT"][:, k, ti * 128:(ti + 1) * 128], start=(k == 0), stop=(k == 7)),
                              reads=[T["hT"], cur_w[1]], writes=[PT[bk]])
                for hf in range(2):
                    fw.op("act", lambda e, hf=hf: e.mul(W["qmT"][:, 4 * hf:4 * hf + 4, :].rearrange("p c n -> p (c n)"), PS[3 + hf][:], 0.0625),
                          reads=[PT[3 + hf]], writes=[T["qmT"]])
                for (gs, c0, cn) in groups:
                    for h in range(4):
                        for mt in range(2):
                            hm = h * 2 + mt
                            bk = 5 + hm // 4
                            for dk in range(2):
                                kt_ap = skT[:, gs, h * 2 + dk, mt * 128:(mt + 1) * 128] if smp else W["mkT"][:, h * 2 + dk, mt * 128:(mt + 1) * 128]
                                fw.op("pe", lambda e, hm=hm, bk=bk, dk=dk, kt_ap=kt_ap, c0=c0, cn=cn, h=h: e.matmul(
                                    PS[bk][:, (hm % 4) * 128 + c0:(hm % 4) * 128 + c0 + cn], kt_ap, W["qmT"][:, h * 2 + dk, c0:c0 + cn],
                                    start=(dk == 0), stop=(dk == 1)), reads=[T["big2"] if smp else T["mkT"], T["qmT"]], writes=[PT[bk]])
                for (gs, c0, cn) in groups:
                    em, emT = W["em"][gs], T["em"][gs]
                    for hf in range(2):
                        fw.op("act", lambda e, hf=hf, em=em, c0=c0, cn=cn: e.activation(
                            em[:, 4 * hf:4 * hf + 4, c0:c0 + cn], PS[5 + hf][:].rearrange("p (c n) -> p c n", c=4)[:, :, c0:c0 + cn], AF.Exp),
                              reads=[PT[5 + hf]], writes=[emT])
                for h in range(4):
                    n_acc = len(groups) * 2
                    a = 0
                    for (gs, c0, cn) in groups:
                        for mt in range(2):
                            va_ap = sVA[:, gs, mt, h, :] if smp else W["vam"][:, mt, h, :]
                            fw.op("pe", lambda e, h=h, gs=gs, mt=mt, va_ap=va_ap, a=a, n_acc=n_acc: e.matmul(
                                PS[1 + h][:, 0:257], W["em"][gs][:, h * 2 + mt, :], va_ap, start=(a == 0), stop=(a == n_acc - 1)),
                                  reads=[T["em"][gs], T["big2"] if smp else T["vam"]], writes=[PT[1 + h]])
                            a += 1
                for h in range(4):
                    fw.op("dve", lambda e, h=h: e.reciprocal(W["rec"][:, h:h + 1], PS[1 + h][:, 256:257]), reads=[PT[1 + h]], writes=[T["rec"]])
                    fw.op("dve", lambda e, h=h: e.tensor_scalar(W["hb"][:, h * 256:(h + 1) * 256], PS[1 + h][:, 0:256], W["rec"][:, h:h + 1], None, ALU.mult),
                          reads=[PT[1 + h], T["rec"]], writes=[T["hb"]])
                to_hT(W["hb"], T["hb"], ti * 128)
            load_w("w_mo")
            for ti, tl in enumerate(tiles):
                xr, xrT = W["xres"][ti], T["xres"][ti]
                proj_tm(ti * 128)
                for nh in range(2):
                    fw.op("dve", lambda e, nh=nh: e.tensor_tensor(xr[:, nh * 512:(nh + 1) * 512], xr[:, nh * 512:(nh + 1) * 512], PS[1 + nh][:], ALU.add),
                          reads=[xrT, PT[1 + nh]], writes=[xrT])
                if self.debug:
                    dst = S["x2_smp"] if smp else S["x2_own"][tl]
                    fw.dma("sp", dst, xr[:], reads=[xrT], writes=[tS["x2_smp" if smp else "x2_own"]])
            for ti, tl in enumerate(tiles):
                rms_to_hT(W["xres"][ti][:], T["xres"][ti], "g_ffn", ti * 128)
            for half in range(2):
                load_w("w_pq", half * 1024, 1024)
                for hc8 in range(8):
                    hc = half * 8 + hc8
                    bk = 1 + hc % 2
                    for k in range(8):
                        fw.op("pe", lambda e, k=k, hc8=hc8, bk=bk: e.matmul(PS[bk][:, 0:Tn], cur_w[0][:, k, hc8 * 128:(hc8 + 1) * 128], W["hT"][:, k, 0:Tn],
                                                                            start=(k == 0), stop=(k == 7)), reads=[T["hT"], cur_w[1]], writes=[PT[bk]])
                    fw.op("act", lambda e, hc=hc, bk=bk: e.copy(W["qyT"][:, hc, 0:Tn], PS[bk][:, 0:Tn]), reads=[PT[bk]], writes=[T["qyT"]])
            sc = W["big"][:].rearrange("p (c n) -> p c n", c=16)
            comb = W["big"][:].rearrange("p (h a b) -> p h a b", h=8, a=16)
            for ti, tl in enumerate(tiles):
                for hc in range(16):
                    bk = 3 + hc // 4
                    fw.op("pe", lambda e, hc=hc, bk=bk: e.matmul(PS[bk][:, (hc % 4) * 128:(hc % 4 + 1) * 128], W["qyT"][:, hc, ti * 128:(ti + 1) * 128], W["keysT"][:, hc, :],
                                                                 start=True, stop=True), reads=[T["qyT"], T["keysT"]], writes=[PT[bk]])
                for q4 in range(4):
                    fw.op("act", lambda e, q4=q4: e.copy(W["big"][:, q4 * 512:(q4 + 1) * 512], PS[3 + q4][:]), reads=[PT[3 + q4]], writes=[T["big"]])

                def top16_multi(srcs, src_trk, vals, idxs, v_trks, i_trks, w_trks):
                    G = len(srcs)
                    n = srcs[0].shape[-1]
                    wk = [W["scw"][:, g * n:(g + 1) * n] for g in range(G)]
                    for g in range(G):
                        fw.op("dve", lambda e, g=g: e.max(out=vals[g][:, 0:8], in_=srcs[g]), reads=[src_trk], writes=[v_trks[g]])
                    for g in range(G):
                        fw.op("dve", lambda e, g=g: e.max_index(out=idxs[g][:, 0:8], in_max=vals[g][:, 0:8], in_values=srcs[g]),
                              reads=[src_trk, v_trks[g]], writes=[i_trks[g]])
                    for g in range(G):
                        fw.op("dve", lambda e, g=g: e.match_replace(out=wk[g], in_to_replace=vals[g][:, 0:8], in_values=srcs[g], imm_value=-1e30),
                              reads=[src_trk, v_trks[g]], writes=[w_trks[g]])
                    for g in range(G):
                        fw.op("dve", lambda e, g=g: e.max(out=vals[g][:, 8:16], in_=wk[g]), reads=[w_trks[g]], writes=[v_trks[g]])
                    for g in range(G):
                        fw.op("dve", lambda e, g=g: e.max_index(out=idxs[g][:, 8:16], in_max=vals[g][:, 8:16], in_values=wk[g]),
                              reads=[w_trks[g], v_trks[g]], writes=[i_trks[g]])

                tvT = [Trk(f"tv{g}") for g in range(16)]; tiT = [Trk(f"ti{g}") for g in range(16)]; wkT = [Trk(f"wk{g}") for g in range(16)]
                for g in range(16):
                    tvT[g].w, tvT[g].r = T["tv"].w, list(T["tv"].r)
                    tiT[g].w, tiT[g].r = T["ti"].w, list(T["ti"].r)
                    wkT[g].w, wkT[g].r = T["scw"].w, list(T["scw"].r)
                top16_multi([sc[:, hc, :] for hc in range(16)], T["big"], [W["tv"][:, hc, :] for hc in range(16)],
                            [W["ti"][:, hc, :] for hc in range(16)], tvT, tiT, wkT)
                fw.op("dve", lambda e: e.tensor_copy(W["tif"][:], W["ti"][:]), reads=tiT, writes=[T["tif"]])
                fw.op("dve", lambda e: e.memset(W["ss"][:, 3:4], 0.0), reads=tvT + tiT + wkT, writes=[T["tv"], T["ti"], T["scw"]])
                tv4 = W["tv"][:].rearrange("p (h c) a -> p h c a", c=2)
                tif4 = W["tif"][:].rearrange("p (h c) a -> p h c a", c=2)
                fw.op("dve", lambda e: e.tensor_tensor(comb, tv4[:, :, 0, :].unsqueeze(3).to_broadcast([128, 8, 16, 16]),
                                                       tv4[:, :, 1, :].unsqueeze(2).to_broadcast([128, 8, 16, 16]), ALU.add), reads=[T["tv"]], writes=[T["big"]])
                svT = [Trk(f"sv{g}") for g in range(8)]; siT = [Trk(f"si{g}") for g in range(8)]; wk2T = [Trk(f"wkb{g}") for g in range(8)]
                for g in range(8):
                    svT[g].w, svT[g].r = T["sv"].w, list(T["sv"].r)
                    siT[g].w, siT[g].r = T["si"].w, list(T["si"].r)
                    wk2T[g].w, wk2T[g].r = T["scw"].w, list(T["scw"].r)
                top16_multi([comb[:, h].rearrange("p a b -> p (a b)") for h in range(8)], T["big"], [W["sv"][:, h, :] for h in range(8)],
                            [W["si"][:, h, :] for h in range(8)], svT, siT, wk2T)
                fw.op("dve", lambda e: e.memset(W["ss"][:, 3:4], 0.0), reads=svT + siT + wk2T, writes=[T["sv"], T["si"], T["scw"]])
                fw.op("dve", lambda e: e.tensor_scalar(W["bi"][:], W["si"][:], 15, None, ALU.bitwise_and), reads=[T["si"]], writes=[T["bi"]])
                fw.op("dve", lambda e: e.tensor_scalar(W["ai"][:], W["si"][:], 4, None, ALU.logical_shift_right), reads=[T["si"]], writes=[T["ai"]])
                fw.op("dve", lambda e: e.tensor_copy(W["bidx"][:], W["bi"][:]), reads=[T["bi"]], writes=[T["bidx"]])
                fw.op("dve", lambda e: e.tensor_copy(W["aidx"][:], W["ai"][:]), reads=[T["ai"]], writes=[T["aidx"]])
                io16 = W["iota"][:, 0:16].unsqueeze(1).unsqueeze(1).to_broadcast([128, 8, 16, 16])
                for q, (ix, cc) in enumerate((("aidx", 0), ("bidx", 1))):
                    fw.op("dve", lambda e, ix=ix: e.tensor_tensor(comb, io16, W[ix][:].unsqueeze(3).to_broadcast([128, 8, 16, 16]), ALU.is_equal),
                          reads=[T["iota"], T[ix]], writes=[T["big"]])
                    fw.op("dve", lambda e, cc=cc: e.tensor_tensor(comb, comb, tif4[:, :, cc, :].unsqueeze(2).to_broadcast([128, 8, 16, 16]), ALU.mult),
                          reads=[T["big"], T["tif"]], writes=[T["big"]])
                    fw.op("dve", lambda e, q=q: e.reduce_sum(W["ijg"][:, q, :], comb.rearrange("p h k a -> p (h k) a"), AXX), reads=[T["big"]], writes=[T["ijg"]])
                fw.op("dve", lambda e: e.tensor_tensor(W["svx"][:], W["sv"][:], W["sv"][:, :, 0:1].to_broadcast([128, 8, 16]), ALU.subtract),
                      reads=[T["sv"]], writes=[T["svx"]])
                fw.op("act", lambda e: e.activation(W["svx"][:], W["svx"][:], AF.Exp), reads=[T["svx"]], writes=[T["svx"]])
                fw.op("dve", lambda e: e.reduce_sum(W["sm"][:, :, 0], W["svx"][:], AXX), reads=[T["svx"]], writes=[T["sm"]])
                fw.op("dve", lambda e: e.reciprocal(W["sm"][:, :, 1], W["sm"][:, :, 0]), reads=[T["sm"]], writes=[T["sm"]])
                fw.op("dve", lambda e: e.tensor_tensor(W["ijg"][:, 2, :].rearrange("p (h k) -> p h k", h=8), W["svx"][:], W["sm"][:, :, 1:2].to_broadcast([128, 8, 16]), ALU.mult),
                      reads=[T["svx"], T["sm"]], writes=[T["ijg"]])
                for q in range(3):
                    fw.op("pe", lambda e, q=q: e.transpose(PS[7][:, q * 128:(q + 1) * 128], W["ijg"][:, q, :], C["ident"][:]), reads=[T["ijg"], CT["ident"]], writes=[PT[7]])
                fw.op("act", lambda e: e.copy(W["ijgT"][:, :, ti * 128:(ti + 1) * 128], PS[7][:, 0:384].rearrange("p (q n) -> p q n", q=3)),
                      reads=[PT[7]], writes=[T["ijgT"]])
            nchunk = 0
            for half in range(2):
                for t0 in range(0, Tn, 16):
                    cb_ = (t0 // 16) % 2
                    oic, oicT = W["oic"][cb_], T["oic"][cb_]
                    ojc, ojcT = W["ojc"][cb_], T["ojc"][cb_]
                    fw.op("dve", lambda e, oic=oic, t0=t0: e.tensor_tensor(
                        oic[:], W["iota"][:, half * 64:(half + 1) * 64].unsqueeze(1).to_broadcast([128, 16, 64]),
                        W["ijgT"][:, 0, t0:t0 + 16].unsqueeze(2).to_broadcast([128, 16, 64]), ALU.is_equal), reads=[T["iota"], T["ijgT"]], writes=[oicT])
                    fw.op("dve", lambda e, oic=oic, t0=t0: e.tensor_tensor(
                        oic[:], oic[:], W["ijgT"][:, 2, t0:t0 + 16].unsqueeze(2).to_broadcast([128, 16, 64]), ALU.mult), reads=[oicT, T["ijgT"]], writes=[oicT])
                    fw.op("dve", lambda e, ojc=ojc, t0=t0: e.tensor_tensor(
                        ojc[:], W["iota"][:].unsqueeze(1).to_broadcast([128, 16, 128]),
                        W["ijgT"][:, 1, t0:t0 + 16].unsqueeze(2).to_broadcast([128, 16, 128]), ALU.is_equal), reads=[T["iota"], T["ijgT"]], writes=[ojcT])
                    for t8 in range(2):
                        bk = 5 + ((t0 // 8) + t8) % 2
                        for tt in range(8):
                            tq = t8 * 8 + tt
                            fw.op("pe", lambda e, oic=oic, ojc=ojc, tt=tt, tq=tq, bk=bk: e.matmul(PS[bk][:, tt * 64:(tt + 1) * 64], ojc[:, tq, :], oic[:, tq, :], start=True, stop=True),
                                  reads=[oicT, ojcT], writes=[PT[bk]])
                        ts = t0 + t8 * 8
                        ev_eng = "act"
                        fw.op(ev_eng, lambda e, ts=ts, bk=bk, ev_eng=ev_eng: (e.copy if ev_eng == "act" else e.tensor_copy)(
                            WT[:, ts:ts + 8, :], PS[bk][:].rearrange("p (t i) -> p t i", t=8)), reads=[PT[bk]], writes=[T["big2"]])
                def emit_dma(ig):
                    sbn = (ig // 2) % 3
                    fw.dma("pool", W["uT"][sbn][:], I["peer_uT"][ig:ig + 2].rearrange("i p k e -> p i k e"), writes=[T["uT"][sbn]])
                    fw.dma("pool", W["vv"][sbn][:], I["peer_v"][ig * 128:(ig + 2) * 128, :].rearrange("(i p) d -> p i d", p=128), writes=[T["vv"][sbn]])

                def emit_A(i):
                    sbn, ii, bk = (i // 2) % 3, i % 2, 5 + i % 2
                    for k in range(8):
                        fw.op("pe", lambda e, k=k: e.matmul(PS[bk][:, 0:Tn], W["uT"][sbn][:, ii, k, :], W["hT"][:, k, 0:Tn], start=(k == 0), stop=(k == 7)),
                              reads=[T["uT"][sbn], T["hT"]], writes=[PT[bk]])

                def emit_rest(i):
                    sbn, ii, bk = (i // 2) % 3, i % 2, 5 + i % 2
                    ga, gaT = W["ga"][i % 2], T["ga"][i % 2]
                    ptb, ptbT = W["ptb"][i % 2], T["ptb"][i % 2]
                    fw.op("act", lambda e: e.activation(ga[:, 0:Tn], PS[bk][:, 0:Tn], AF.Gelu), reads=[PT[bk]], writes=[gaT])
                    fw.op("dve", lambda e: e.tensor_tensor(ptb[:, 0:Tn], ga[:, 0:Tn], WT[:, 0:Tn, i - half * 64], ALU.mult),
                          reads=[gaT, T["big2"]], writes=[ptbT])
                    for ti in range(nt):
                        for nh in range(2):
                            fw.op("pe", lambda e, ti=ti, nh=nh: e.matmul(PS[1 + 2 * ti + nh][:], ptb[:, ti * 128:(ti + 1) * 128], W["vv"][sbn][:, ii, nh * 512:(nh + 1) * 512],
                                                                         start=(i == 0), stop=(i == 127)),
                                  reads=[ptbT, T["vv"][sbn]], writes=[PT[1 + 2 * ti + nh]])

                i_lo, i_hi = half * 64, half * 64 + 64
                emit_dma(i_lo)
                emit_dma(i_lo + 2)
                emit_A(i_lo)
                for i in range(i_lo, i_hi):
                    if i + 1 < i_hi:
                        if (i + 1) % 2 == 0 and i + 3 < i_hi:
                            emit_dma(i + 3)
                        emit_A(i + 1)
                    emit_rest(i)
            for ti, tl in enumerate(tiles):
                xr, xrT = W["xres"][ti], T["xres"][ti]
                for nh in range(2):
                    fw.op("dve", lambda e, nh=nh, ti=ti: e.tensor_tensor(xr[:, nh * 512:(nh + 1) * 512], xr[:, nh * 512:(nh + 1) * 512], PS[1 + 2 * ti + nh][:], ALU.add),
                          reads=[xrT, PT[1 + 2 * ti + nh]], writes=[xrT])
                if self.debug and smp:
                    fw.dma("sp", S["x3_smp"], xr[:], reads=[xrT], writes=[tS["x3_smp"]])
                fw.op("dve", lambda e: e.memset(W["ss"][:, 0:1], 0.0), writes=[T["ss"]])
                fw.op("act", lambda e: e.activation(W["junk"][:], xr[:], AF.Square, accum_out=W["ss"][:, 0:1]), reads=[xrT], writes=[T["junk"], T["ss"]])
                fw.op("act", lambda e: e.activation(W["ss"][:, 1:2], W["ss"][:, 0:1], AF.Ln, bias=C["cst"][:, 1:2], scale=1.0),
                      reads=[T["ss"], CT["cst"]], writes=[T["ss"]])
                fw.op("act", lambda e: e.activation(W["ss"][:, 2:3], W["ss"][:, 1:2], AF.Exp, scale=-0.5), reads=[T["ss"]], writes=[T["ss"]])
                fw.op("dve", lambda e: e.scalar_tensor_tensor(W["yo"][:], xr[:], W["ss"][:, 2:3], W["g_final"][:], ALU.mult, ALU.mult),
                      reads=[xrT, T["ss"], T["g_final"]], writes=[T["yo"]])
                dst = O["y_smp"] if smp else O["y_own"][tl * 128:(tl + 1) * 128, :]
                fw.dma("sp", dst, W["yo"][:], reads=[T["yo"]], is_output=True)

        for b0 in range(0, NOWN, 2):
            block(list(range(b0, min(NOWN, b0 + 2))), False)
        block([0], True)


def _chunk_consts(L):
    nch = 128 // L
    idx = np.arange(128)
    same = (idx[:, None] // L) == (idx[None, :] // L)
    tri = (same & (idx[:, None] <= idx[None, :])).astype(np.float32)
    u = (same & (idx[:, None] > idx[None, :])).astype(np.float32)
    onb = same.astype(np.float32)
    sel = np.zeros((128, nch, 128), np.float32)
    rm = np.zeros((128, nch), np.float32)
    cm = np.zeros((128, nch, 128), np.float32)
    for ch in range(nch):
        sel[ch * L:(ch + 1) * L, ch, :] = 1.0
        rm[ch * L:(ch + 1) * L, ch] = 1.0
        cm[:, ch, ch * L:(ch + 1) * L] = 1.0
    return tri, u, onb, sel, rm, cm


def _rope_tables(pos):
    half = 8
    inv = (1.0 / (np.float32(500000.0) ** (np.arange(half, dtype=np.float32) / np.float32(half)))).astype(np.float32)
    ang = (pos.astype(np.float32)[:, None] * inv[None, :]).astype(np.float32)
    return np.cos(ang.astype(np.float64)).astype(np.float32), np.sin(ang.astype(np.float64)).astype(np.float32)


_PROG_CACHE = {}


def _get_prog(SEQ, PAST, stages, debug=False):
    key = (SEQ, PAST, stages, debug)
    if key not in _PROG_CACHE:
        p = Prog(SEQ, PAST, stages, debug)
        p.build()
        _PROG_CACHE[key] = p
    return _PROG_CACHE[key]


def kernel(_stages=3, _debug=False, **inp):
    f32 = np.float32
    g = lambda n: np.asarray(inp[n], dtype=f32)
    x_prompt, x_sample = g("x_prompt"), g("x_sample")
    SEQ = x_prompt.shape[1]
    PAST = inp["cache_attn_k"].shape[2]
    NT, NOWN = SEQ // 128, SEQ // 512
    prog = _get_prog(SEQ, PAST, _stages, _debug)

    bc = lambda v, n=128: np.ascontiguousarray(np.broadcast_to(np.asarray(v, f32).reshape(1, -1), (n, np.asarray(v).size)))
    shared = {}
    shared["w_in"] = g("w_in")[0]
    for n in ("w_out", "w_mq", "w_mk", "w_mv", "w_mo", "w_pq"):
        shared[n] = g(n)[0]
    shared["keysT"] = np.ascontiguousarray(g("peer_keys")[0].reshape(16, 128, 128).transpose(2, 0, 1))
    pu = g("peer_u")[0]
    shared["peer_uT"] = np.ascontiguousarray(pu.reshape(128, 128, 8, 128).transpose(0, 3, 2, 1))
    shared["peer_v"] = g("peer_v")[0]
    shared["g_mix"] = bc(g("g_mix")[0]); shared["g_memq"] = bc(g("g_mem_q")[0]); shared["g_memkv"] = bc(g("g_mem_kv")[0])
    shared["g_ffn"] = bc(g("g_ffn")[0]); shared["g_final"] = bc(g("g_final"))
    shared["g_ssd"] = bc(g("g_ssd")[0]); shared["g_subln"] = bc(g("g_subln")[0])
    shared["dt_bias"] = bc(g("dt_bias")[0]); shared["a_log"] = bc(g("a_log")[0]); shared["d_skip"] = bc(g("d_skip")[0])
    lam = np.stack([g("lam_q1")[0], g("lam_k1")[0], g("lam_q2")[0], g("lam_k2")[0]], 0)
    shared["lam"] = np.ascontiguousarray(np.broadcast_to(lam[None], (128, 4, 64)))
    shared["conv_wT"] = np.ascontiguousarray(g("conv_w")[0].reshape(4, 8, 128).transpose(2, 1, 0))
    shared["conv_bT"] = np.ascontiguousarray(g("conv_b")[0].reshape(8, 128).T)
    shared["ident"] = np.eye(128, dtype=f32)
    for L in (64, 32):
        tri, u, onb, sel, rm, cm = _chunk_consts(L)
        shared[f"tri{L}"], shared[f"u{L}"], shared[f"onb{L}"] = tri, u, onb
        shared[f"sel{L}"], shared[f"rm{L}"], shared[f"cm{L}"] = sel, rm, cm
    cp, sp_ = _rope_tables(np.arange(SEQ))
    shared["cos_p"] = np.ascontiguousarray(cp.reshape(NT, 128, 8).transpose(1, 0, 2))
    shared["sin_p"] = np.ascontiguousarray(sp_.reshape(NT, 128, 8).transpose(1, 0, 2))
    cs, ss = _rope_tables(PAST + (np.arange(128) % 32))
    shared["cos_s"], shared["sin_s"] = cs, ss
    shared["iota"] = np.ascontiguousarray(np.broadcast_to(np.arange(128, dtype=f32)[None], (128, 128)))
    idx = np.arange(128)
    am_s = np.zeros((128, 5, 128), f32)
    for s in range(4):
        am_s[:, s, 32 * s:32 * (s + 1)] = 1.0
    am_s[:, 4, :] = ((idx[:, None] // 32) == (idx[None, :] // 32)).astype(f32)
    shared["amask_s"] = am_s

    in_maps = []
    for c in range(NCORES):
        b, j = c // 4, c % 4
        m = dict(shared)
        m["x_all"] = x_prompt[b]
        m["x_own"] = np.ascontiguousarray(x_prompt[b].reshape(NOWN, 4, 128, D)[:, j].reshape(NOWN * 128, D))
        m["x_smp"] = np.ascontiguousarray(x_sample[4 * c:4 * c + 4].reshape(128, D))
        m["mem_p"] = g("mem_prompt")[b]
        am = np.zeros((128, 4, 128), f32)
        for r in range(4):
            if r < j:
                am[:, r, :] = 1.0
            elif r == j:
                am[:, r, :] = ((idx[:, None] // 64) <= (idx[None, :] // 64)).astype(f32)
        m["amask_p"] = am
        sj = np.zeros((128, 4), f32); sj[:, j] = 1.0
        m["selj"] = sj
        sl = slice(4 * c, 4 * c + 4)
        ck = g("cache_attn_k")[0, sl]
        m["ck_T"] = np.ascontiguousarray(ck.transpose(0, 2, 3, 1))
        m["cv"] = np.ascontiguousarray(g("cache_attn_v")[0, sl].reshape(4, PAST, 512))
        cmk = g("cache_mem_k")[0, sl]
        m["cmk_T"] = np.ascontiguousarray(cmk.reshape(4, 256, 4, 2, 128).transpose(0, 2, 3, 4, 1))
        m["cmv"] = np.ascontiguousarray(g("cache_mem_v")[0, sl].reshape(4, 256, D))
        st = g("state_ssm")[0, sl]
        m["ssm_T"] = np.ascontiguousarray(st.transpose(0, 3, 1, 2).reshape(4, 128, 512))
        cv_ = g("state_conv")[0, sl]
        m["conv_T"] = np.ascontiguousarray(cv_.reshape(4, 3, 8, 128).transpose(3, 2, 0, 1))
        in_maps.append({k: np.ascontiguousarray(v, dtype=f32) for k, v in m.items() if k in prog.in_shapes})

    res = run_bass_kernel_spmd(prog.nc, in_maps, core_ids=list(range(NCORES)))
    R = res.results
    if _debug:
        kernel.last_results = R
    B = x_prompt.shape[0]
    y_prompt = np.zeros((B, SEQ, D), f32)
    for c in range(NCORES):
        b, j = c // 4, c % 4
        y_prompt[b].reshape(NOWN, 4, 128, D)[:, j] = R[c]["y_own"].reshape(NOWN, 128, D)
    y_sample = np.concatenate([R[c]["y_smp"].reshape(4, 32, D) for c in range(NCORES)], 0)
    newk_p = np.stack([R[4 * b]["newk"].reshape(SEQ, 4, 128) for b in range(B)], 0)[None]
    newv_p = np.stack([R[4 * b]["newv"].reshape(SEQ, 4, 128) for b in range(B)], 0)[None]
    ssm_p = np.stack([R[4 * b]["ssm_p"].reshape(8, 64, 128) for b in range(B)], 0)[None]
    conv_p = np.stack([R[4 * b]["conv_p"].reshape(128, 8, 3).transpose(2, 1, 0).reshape(3, D) for b in range(B)], 0)[None]
    memk_p = np.stack([R[4 * b]["memk_p"].reshape(256, 4, 256) for b in range(B)], 0)[None]
    memv_p = np.stack([R[4 * b]["memv_p"].reshape(256, 4, 256) for b in range(B)], 0)[None]
    newk_s = np.concatenate([R[c]["newk_s"].reshape(4, 32, 4, 128) for c in range(NCORES)], 0)[None]
    newv_s = np.concatenate([R[c]["newv_s"].reshape(4, 32, 4, 128) for c in range(NCORES)], 0)[None]
    ssm_s = np.concatenate([R[c]["ssm_s"].reshape(4, 8, 64, 128) for c in range(NCORES)], 0)[None]
    conv_s = np.concatenate([R[c]["conv_s"].reshape(128, 8, 4, 3).transpose(2, 3, 1, 0).reshape(4, 3, D) for c in range(NCORES)], 0)[None]
    return (y_prompt, y_sample, newk_p, newv_p, ssm_p, conv_p, memk_p, memv_p, newk_s, newv_s, ssm_s, conv_s)
```
